# Optimizing a Trainium2 kernel written in Bass

```python
import jax, jax.numpy as jnp
from jax import lax
import numpy as np

D_MODEL = 1024
BATCH = 2
SEQ = 8192
DEPTH = 2
DEC_BATCH = 8
DEC_SEQ = 32
PAST_LEN = 2048

CHUNK = 64
N_EVEN = (DEPTH + 1) // 2
N_ODD = DEPTH // 2
BRANCH = D_MODEL // 2
HEAD_DIM = 64
N_HEADS = BRANCH // HEAD_DIM
CONV_WIDTH = 31
POOL_WINDOWS = (2, 4, 8, 16)
N_POOL_GROUPS = len(POOL_WINDOWS)
POOL_GROUP = BRANCH // N_POOL_GROUPS
POOL_PAST = max(POOL_WINDOWS) - 1
IDX_HEADS = 4
IDX_DIM = 64
TOPK_MAX = 256
ROPE_THETA = 10000.0
Q_BLOCK = 128
LN_EPS = 1e-5
ALPHA = (2 * DEPTH) ** 0.25
BETA = (8 * DEPTH) ** -0.25
FORGET_BIAS = 3.0
NEG = -1e30

EVEN_SIZES = (BRANCH, BRANCH, BRANCH, BRANCH, BRANCH, BRANCH, BRANCH, N_HEADS)
ODD_SIZES = (BRANCH, BRANCH, BRANCH, BRANCH, BRANCH, BRANCH, IDX_HEADS * IDX_DIM, IDX_DIM, IDX_HEADS)
EVEN_IN = int(sum(EVEN_SIZES))
ODD_IN = int(sum(ODD_SIZES))
EVEN_SPLITS = [int(s) for s in np.cumsum(EVEN_SIZES)[:-1]]
ODD_SPLITS = [int(s) for s in np.cumsum(ODD_SIZES)[:-1]]

kernel_name = "hybrid_streaming_encoder_step"


def layer_norm(x, g, b):
    xf = x.astype(jnp.float32)
    mu = jnp.mean(xf, -1, keepdims=True)
    var = jnp.mean(jnp.square(xf - mu), -1, keepdims=True)
    return ((xf - mu) * lax.rsqrt(var + LN_EPS) * g.astype(jnp.float32) + b.astype(jnp.float32)).astype(x.dtype)


def rope(x, pos):
    d = x.shape[-1]
    inv = ROPE_THETA ** (-jnp.arange(0, d, 2, dtype=jnp.float32) / d)
    ang = pos.astype(jnp.float32)[:, None] * inv[None, :]
    cos, sin = jnp.cos(ang)[:, None, :], jnp.sin(ang)[:, None, :]
    xf = x.astype(jnp.float32)
    x1, x2 = xf[..., : d // 2], xf[..., d // 2:]
    return jnp.concatenate([x1 * cos - x2 * sin, x2 * cos + x1 * sin], -1).astype(x.dtype)


def cat(past, new):
    return new if past is None else jnp.concatenate([past.astype(new.dtype), new], axis=1)


def to_blocks(a, qb):
    b, t = a.shape[:2]
    return jnp.moveaxis(a.reshape(b, t // qb, qb, *a.shape[2:]), 1, 0)


def from_blocks(a):
    nb, b, qb = a.shape[:3]
    return jnp.moveaxis(a, 0, 1).reshape(b, nb * qb, *a.shape[3:])


def causal_dwconv(x, past, w, bias):
    xp = jnp.concatenate([past.astype(x.dtype), x], axis=1)
    y = lax.conv_general_dilated(xp, w[:, None, :].astype(x.dtype), window_strides=(1,), padding='VALID',
                                 dimension_numbers=('NWC', 'WIO', 'NWC'), feature_group_count=x.shape[-1])
    return y + bias, xp[:, -(CONV_WIDTH - 1):]


def multi_scale_pool(x, past, pos, w_grp, scale):
    B, T, _ = x.shape
    xp = jnp.concatenate([past.astype(x.dtype), x], axis=1)
    xf = xp.astype(jnp.float32)
    cs = jnp.concatenate([jnp.zeros_like(xf[:, :1]), jnp.cumsum(xf, axis=1)], axis=1)
    end = cs[:, POOL_PAST + 1:]
    outs = []
    for g, w in enumerate(POOL_WINDOWS):
        sl = slice(g * POOL_GROUP, (g + 1) * POOL_GROUP)
        start = cs[:, POOL_PAST + 1 - w: POOL_PAST + 1 - w + T, sl]
        cnt = jnp.minimum(pos + 1, w).astype(jnp.float32)[None, :, None]
        outs.append((end[..., sl] - start) / cnt)
    pooled = (jnp.concatenate(outs, -1) - x.astype(jnp.float32)).reshape(B, T, N_POOL_GROUPS, POOL_GROUP)
    mixed = jnp.einsum('btgc,gcd->btgd', pooled, w_grp.astype(jnp.float32)).reshape(B, T, BRANCH)
    return (mixed * scale.astype(jnp.float32)).astype(x.dtype), xp[:, -POOL_PAST:]


def fox_block(q, cq, qpos, k, v, ck, kpos):
    s = jnp.einsum('bqhd,bshd->bhqs', q.astype(jnp.float32), k.astype(jnp.float32)) * HEAD_DIM ** -0.5
    s = s + jnp.moveaxis(cq, 1, 2)[..., :, None] - jnp.moveaxis(ck, 1, 2)[..., None, :]
    mask = kpos[None, :] <= qpos[:, None]
    p = jax.nn.softmax(jnp.where(mask[None, None], s, NEG), axis=-1)
    return jnp.einsum('bhqs,bshd->bqhd', p, v.astype(jnp.float32)).astype(q.dtype)


def dsa_block(q, iq, iw, qpos, k, v, ik, kpos, topk):
    dots = jnp.einsum('bqhi,bsi->bqhs', iq.astype(jnp.float32), ik.astype(jnp.float32)) * IDX_DIM ** -0.5
    score = jnp.einsum('bqh,bqhs->bqs', iw.astype(jnp.float32) * IDX_HEADS ** -0.5, jax.nn.relu(dots))
    qchunk = qpos // CHUNK
    adm = (kpos[None, :] // CHUNK) <= qchunk[:, None]
    score = jnp.where(adm[None], score, NEG)
    _, top_idx = lax.top_k(score, topk)
    gather = jax.vmap(lambda a, i: a[i])
    k_sel = gather(k, top_idx)
    v_sel = gather(v, top_idx)
    valid = (kpos[top_idx] // CHUNK) <= qchunk[None, :, None]
    s = jnp.einsum('bqhd,bqkhd->bhqk', q.astype(jnp.float32), k_sel.astype(jnp.float32)) * HEAD_DIM ** -0.5
    p = jax.nn.softmax(jnp.where(valid[:, None], s, NEG), axis=-1)
    return jnp.einsum('bhqk,bqkhd->bqhd', p, v_sel.astype(jnp.float32)).astype(q.dtype)


def even_layer(x, pos, kpos, past_conv, past_k, past_v, past_logf,
               w_in, b_f, conv_w, conv_b, cln_g, cln_b, w_out, ln_g, ln_b):
    B, T, _ = x.shape
    h = jnp.einsum('btd,de->bte', x, w_in)
    a_val, a_glu, a_gate, q, k, v, b_gate, f_logit = jnp.split(h, EVEN_SPLITS, axis=-1)
    a = a_val * jax.nn.sigmoid(a_glu)
    if past_conv is None:
        past_conv = jnp.zeros((B, CONV_WIDTH - 1, BRANCH), x.dtype)
    a, conv_state = causal_dwconv(a, past_conv, conv_w, conv_b)
    a = jax.nn.silu(layer_norm(a, cln_g, cln_b)) * jax.nn.silu(a_gate)
    q = q.reshape(B, T, N_HEADS, HEAD_DIM)
    k = k.reshape(B, T, N_HEADS, HEAD_DIM)
    v = v.reshape(B, T, N_HEADS, HEAD_DIM)
    logf = jax.nn.log_sigmoid((f_logit + b_f).astype(jnp.float32))
    k_all, v_all = cat(past_k, k), cat(past_v, v)
    logf_all = cat(past_logf, logf)
    ck = jnp.cumsum(logf_all.astype(jnp.float32), axis=1)
    cq = ck[:, -T:]
    qb = min(Q_BLOCK, T)
    o = lax.map(lambda blk: fox_block(blk[0], blk[1], blk[2], k_all, v_all, ck, kpos),
                (to_blocks(q, qb), to_blocks(cq, qb), pos.reshape(T // qb, qb)))
    o = from_blocks(o).reshape(B, T, BRANCH) * jax.nn.silu(b_gate)
    y = jnp.einsum('bte,ed->btd', jnp.concatenate([a, o], axis=-1), w_out)
    x = layer_norm(ALPHA * x + y, ln_g, ln_b)
    return x, conv_state, k, v, logf.astype(x.dtype)


def odd_layer(x, pos, kpos, past_pool, past_k, past_v, past_ik,
              w_in, pool_w, pool_scale, w_out, ln_g, ln_b):
    B, T, _ = x.shape
    h = jnp.einsum('btd,de->bte', x, w_in)
    c_val, c_gate, q, k, v, d_gate, iq, ik, iw = jnp.split(h, ODD_SPLITS, axis=-1)
    if past_pool is None:
        past_pool = jnp.zeros((B, POOL_PAST, BRANCH), x.dtype)
    c, pool_state = multi_scale_pool(c_val, past_pool, pos, pool_w, pool_scale)
    c = c * jax.nn.silu(c_gate)
    q = rope(q.reshape(B, T, N_HEADS, HEAD_DIM), pos)
    k = rope(k.reshape(B, T, N_HEADS, HEAD_DIM), pos)
    v = v.reshape(B, T, N_HEADS, HEAD_DIM)
    iq = rope(iq.reshape(B, T, IDX_HEADS, IDX_DIM), pos)
    ik = rope(ik[:, :, None, :], pos)[:, :, 0, :]
    k_all, v_all, ik_all = cat(past_k, k), cat(past_v, v), cat(past_ik, ik)
    topk = min(TOPK_MAX, kpos.shape[0] // 4)
    qb = min(Q_BLOCK, T)
    o = lax.map(lambda blk: dsa_block(blk[0], blk[1], blk[2], blk[3], k_all, v_all, ik_all, kpos, topk),
                (to_blocks(q, qb), to_blocks(iq, qb), to_blocks(iw, qb), pos.reshape(T // qb, qb)))
    o = from_blocks(o).reshape(B, T, BRANCH) * jax.nn.silu(d_gate)
    y = jnp.einsum('bte,ed->btd', jnp.concatenate([c, o], axis=-1), w_out)
    x = layer_norm(ALPHA * x + y, ln_g, ln_b)
    return x, pool_state, k, v, ik


def setup_inputs(seed: int = 0) -> dict:
    key = jax.random.key(seed)
    ks = jax.random.split(key, 32)
    n = lambda i, shape: jax.random.normal(ks[i], shape, jnp.float32)
    return {
        "x_prompt": n(0, (BATCH, SEQ, D_MODEL)),
        "x_sample": n(1, (DEC_BATCH, DEC_SEQ, D_MODEL)),
        "cache_conv": 0.5 * n(2, (N_EVEN, DEC_BATCH, CONV_WIDTH - 1, BRANCH)),
        "cache_fox_k": n(3, (N_EVEN, DEC_BATCH, PAST_LEN, N_HEADS, HEAD_DIM)),
        "cache_fox_v": n(4, (N_EVEN, DEC_BATCH, PAST_LEN, N_HEADS, HEAD_DIM)),
        "cache_fox_logf": jax.nn.log_sigmoid(FORGET_BIAS + n(5, (N_EVEN, DEC_BATCH, PAST_LEN, N_HEADS))),
        "cache_pool": n(6, (N_ODD, DEC_BATCH, POOL_PAST, BRANCH)),
        "cache_dsa_k": n(7, (N_ODD, DEC_BATCH, PAST_LEN, N_HEADS, HEAD_DIM)),
        "cache_dsa_v": n(8, (N_ODD, DEC_BATCH, PAST_LEN, N_HEADS, HEAD_DIM)),
        "cache_dsa_idx_k": n(9, (N_ODD, DEC_BATCH, PAST_LEN, IDX_DIM)),
        "e_w_in": n(10, (N_EVEN, D_MODEL, EVEN_IN)) * D_MODEL ** -0.5,
        "e_b_f": FORGET_BIAS + 0.1 * n(11, (N_EVEN, N_HEADS)),
        "e_conv_w": n(12, (N_EVEN, CONV_WIDTH, BRANCH)) * CONV_WIDTH ** -0.5,
        "e_conv_b": 0.02 * n(13, (N_EVEN, BRANCH)),
        "e_conv_ln_g": 1.0 + 0.05 * n(14, (N_EVEN, BRANCH)),
        "e_conv_ln_b": 0.02 * n(15, (N_EVEN, BRANCH)),
        "e_w_out": n(16, (N_EVEN, D_MODEL, D_MODEL)) * (D_MODEL ** -0.5 * BETA),
        "e_ln_g": 1.0 + 0.05 * n(17, (N_EVEN, D_MODEL)),
        "e_ln_b": 0.02 * n(18, (N_EVEN, D_MODEL)),
        "o_w_in": n(19, (N_ODD, D_MODEL, ODD_IN)) * D_MODEL ** -0.5,
        "o_pool_w": n(20, (N_ODD, N_POOL_GROUPS, POOL_GROUP, POOL_GROUP)) * POOL_GROUP ** -0.5,
        "o_pool_scale": 1.0 + 0.1 * n(21, (N_ODD, BRANCH)),
        "o_w_out": n(22, (N_ODD, D_MODEL, D_MODEL)) * (D_MODEL ** -0.5 * BETA),
        "o_ln_g": 1.0 + 0.05 * n(23, (N_ODD, D_MODEL)),
        "o_ln_b": 0.02 * n(24, (N_ODD, D_MODEL)),
    }


def reference(x_prompt, x_sample, cache_conv, cache_fox_k, cache_fox_v, cache_fox_logf,
              cache_pool, cache_dsa_k, cache_dsa_v, cache_dsa_idx_k,
              e_w_in, e_b_f, e_conv_w, e_conv_b, e_conv_ln_g, e_conv_ln_b, e_w_out, e_ln_g, e_ln_b,
              o_w_in, o_pool_w, o_pool_scale, o_w_out, o_ln_g, o_ln_b):
    t_p, t_s = x_prompt.shape[1], x_sample.shape[1]
    past_len = cache_fox_k.shape[2]
    pos_p = jnp.arange(t_p, dtype=jnp.int32)
    kpos_p = pos_p
    pos_s = past_len + jnp.arange(t_s, dtype=jnp.int32)
    kpos_s = jnp.arange(past_len + t_s, dtype=jnp.int32)
    yp, ys = x_prompt, x_sample
    conv_p, conv_s, fk_p, fk_s, fv_p, fv_s, ff_p, ff_s = [], [], [], [], [], [], [], []
    pool_p, pool_s, dk_p, dk_s, dv_p, dv_s, di_p, di_s = [], [], [], [], [], [], [], []
    for l in range(DEPTH):
        i = l // 2
        if l % 2 == 0:
            pe = (e_w_in[i], e_b_f[i], e_conv_w[i], e_conv_b[i], e_conv_ln_g[i], e_conv_ln_b[i],
                  e_w_out[i], e_ln_g[i], e_ln_b[i])
            yp, c_p, k_p, v_p, f_p = even_layer(yp, pos_p, kpos_p, None, None, None, None, *pe)
            ys, c_s, k_s, v_s, f_s = even_layer(ys, pos_s, kpos_s, cache_conv[i], cache_fox_k[i],
                                                cache_fox_v[i], cache_fox_logf[i], *pe)
            conv_p.append(c_p); conv_s.append(c_s); fk_p.append(k_p); fk_s.append(k_s)
            fv_p.append(v_p); fv_s.append(v_s); ff_p.append(f_p); ff_s.append(f_s)
        else:
            po = (o_w_in[i], o_pool_w[i], o_pool_scale[i], o_w_out[i], o_ln_g[i], o_ln_b[i])
            yp, c_p, k_p, v_p, ik_p = odd_layer(yp, pos_p, kpos_p, None, None, None, None, *po)
            ys, c_s, k_s, v_s, ik_s = odd_layer(ys, pos_s, kpos_s, cache_pool[i], cache_dsa_k[i],
                                                cache_dsa_v[i], cache_dsa_idx_k[i], *po)
            pool_p.append(c_p); pool_s.append(c_s); dk_p.append(k_p); dk_s.append(k_s)
            dv_p.append(v_p); dv_s.append(v_s); di_p.append(ik_p); di_s.append(ik_s)
    return (yp, ys,
            jnp.stack(conv_p), jnp.stack(conv_s),
            jnp.stack(fk_p), jnp.stack(fk_s),
            jnp.stack(fv_p), jnp.stack(fv_s),
            jnp.stack(ff_p), jnp.stack(ff_s),
            jnp.stack(pool_p), jnp.stack(pool_s),
            jnp.stack(dk_p), jnp.stack(dk_s),
            jnp.stack(dv_p), jnp.stack(dv_s),
            jnp.stack(di_p), jnp.stack(di_s))
```

```python
import numpy as np
import concourse.bass as bass
import concourse.mybir as mybir
from concourse.bass_utils import run_bass_kernel_spmd
from contextlib import ExitStack

F32 = mybir.dt.float32
BF16 = mybir.dt.bfloat16
AF = mybir.ActivationFunctionType
ALU = mybir.AluOpType
AX = mybir.AxisListType

SEQ = 8192
PAST = 2048
TS = 32
ALPHA = 4.0 ** 0.25
NIT = 12
TBP = 256
NTP = TBP // 128
KCH = 1024
NEGM = -30000.0
PE_SKIP = True


class _Rec:
    def __getattr__(self, name):
        def f(*a, **k):
            self.call = (name, a, k)
            return self
        return f


class Sched:
    SEM_LIMIT = 2000
    NDMA = 24

    def __init__(self, nc, es):
        self.nc = nc
        self.es = es
        self.names = ['pe', 'act', 'dve', 'pool', 'sp']
        self.prog = {e: [] for e in self.names}
        self.cnt = {e: 0 for e in self.names}
        self.waited = {e: {} for e in self.names}
        self.bufs = {}
        self.pe_force = False
        self.pe_mode = None
        self.dsem = []
        for i in range(self.NDMA):
            s = es.enter_context(nc.semaphore(f"dq{i}"))
            self.dsem.append([s, 0])
        self.dnext = 0
        self.ninst = 0

    def _deps(self, reads, writes):
        toks = []
        for k in reads:
            b = self.bufs.get(k)
            if b and b[0] is not None:
                toks.append(b[0])
            if b and k.startswith('pb'):
                toks.extend(b[1])
        for k in writes:
            b = self.bufs.get(k)
            if b:
                if b[0] is not None:
                    toks.append(b[0])
                toks.extend(b[1])
        return toks

    @staticmethod
    def _key(tok):
        return ('E', tok[1]) if tok[0] == 'E' else ('D', id(tok[1]))

    def _emit_waits(self, e, toks):
        need = {}
        for tok in toks:
            k = self._key(tok)
            if PE_SKIP and e == 'pe' and tok[0] == 'E' and tok[1] == 'pe' and not self.pe_force:
                continue
            if self.waited[e].get(k, 0) >= tok[2]:
                continue
            if k not in need or need[k][2] < tok[2]:
                need[k] = tok
        for k, tok in need.items():
            self.waited[e][k] = tok[2]
            self.prog[e].append(('w', tok))

    def _compact(self, toks):
        best = {}
        for tok in toks:
            k = self._key(tok)
            if k not in best or best[k][2] < tok[2]:
                best[k] = tok
        return list(best.values())

    def _update(self, tok, reads, writes):
        for k in reads:
            b = self.bufs.setdefault(k, [None, []])
            b[1].append(tok)
            if len(b[1]) > 12:
                b[1] = self._compact(b[1])
        for k in writes:
            self.bufs[k] = [tok, []]

    def op(self, e, fn, reads=(), writes=()):
        self._emit_waits(e, self._deps(reads, writes))
        rec = _Rec()
        fn(rec)
        name, a, k = rec.call
        if e == 'pe':
            st_ = k.get('lhsT', k.get('in_'))
            r32 = lambda n: 32 if n <= 32 else (64 if n <= 64 else 128)
            fr = 1
            for d in st_.shape[1:]:
                fr *= d
            mode = (name, r32(st_.shape[0]), r32(fr), str(st_.dtype))
            if mode != self.pe_mode and self.cnt['pe'] > 0:
                self.pe_force = True
                self._emit_waits('pe', [('E', 'pe', self.cnt['pe'])])
                self.pe_force = False
            self.pe_mode = mode
        self.cnt[e] += 1
        idx = self.cnt[e]
        self.prog[e].append(('op', name, a, k, idx))
        self._update(('E', e, idx), reads, writes)
        self.ninst += 1

    def dma(self, q, out, in_, reads=(), writes=()):
        slot = self.dsem[self.dnext]
        self.dnext = (self.dnext + 1) % self.NDMA
        s = slot[0]
        toks = self._deps(reads, writes)
        if slot[1] > 0:
            toks.append(('D', s, slot[1]))
        self._emit_waits(q, toks)
        slot[1] += 16
        self.prog[q].append(('dma', out, in_, s))
        self._update(('D', s, slot[1]), reads, writes)
        self.ninst += 1

    def _all_toks(self):
        toks = [('D', s, v) for (s, v) in self.dsem if v > 0]
        for e in ['pe', 'act', 'dve', 'pool']:
            if self.cnt[e] > 0:
                toks.append(('E', e, self.cnt[e]))
        return toks

    def barrier(self):
        toks = self._all_toks()
        for e in self.names:
            self.pe_force = True
            self._emit_waits(e, toks)
            self.pe_force = False

    def finish(self):
        self._emit_waits('sp', self._all_toks())
        nc = self.nc
        need = {e: set() for e in self.names}
        for e in self.names:
            for ent in self.prog[e]:
                if ent[0] == 'w' and ent[1][0] == 'E':
                    need[ent[1][1]].add(ent[1][2])
        sig = {}
        sems = {}
        self.nsem = 0
        for p in self.names:
            for r, idx in enumerate(sorted(need[p])):
                ep = r // self.SEM_LIMIT
                if (p, ep) not in sems:
                    sems[(p, ep)] = self.es.enter_context(nc.semaphore(f"s{p}{ep}"))
                    self.nsem += 1
                sig[(p, idx)] = (sems[(p, ep)], r % self.SEM_LIMIT + 1)

        def replay(e, eng):
            for ent in self.prog[e]:
                if ent[0] == 'w':
                    tok = ent[1]
                    if tok[0] == 'E':
                        sm, v = sig[(tok[1], tok[2])]
                        eng.wait_ge(sm, v)
                    else:
                        eng.wait_ge(tok[1], tok[2])
                elif ent[0] == 'op':
                    _, name, a, k, idx = ent
                    ins = getattr(eng, name)(*a, **k)
                    if (e, idx) in sig:
                        ins.then_inc(sig[(e, idx)][0], 1)
                else:
                    _, out, in_, s = ent
                    eng.dma_start(out=out, in_=in_).then_inc(s, 16)

        with nc.Block() as block:
            @block.tensor
            def _(eng):
                replay('pe', eng)

            @block.scalar
            def _(eng):
                replay('act', eng)

            @block.vector
            def _(eng):
                replay('dve', eng)

            @block.gpsimd
            def _(eng):
                replay('pool', eng)

            @block.sync
            def _(eng):
                replay('sp', eng)


class _Stop(Exception):
    pass


STOP = [0]


def CP(n):
    if STOP[0] == n:
        raise _Stop()


class Rot:
    def __init__(self, tiles, name):
        self.tiles = tiles
        self.name = name
        self.i = 0

    def next(self):
        j = self.i % len(self.tiles)
        self.i += 1
        return self.tiles[j], f"{self.name}{j}"


E_IN = 3592
O_IN = 3396
W_SPECS = None


def build(nblk=SEQ // TBP, do_sample=True):
    nc = bass.Bass("TRN2", target_bir_lowering=False)
    SEQ = nblk * TBP
    din = lambda n, s: nc.dram_tensor(n, list(s), F32, kind="ExternalInput").ap()
    dout = lambda n, s: nc.dram_tensor(n, list(s), F32, kind="ExternalOutput").ap()
    dscr = lambda n, s, dt: nc.dram_tensor(n, list(s), dt).ap()

    I = {}
    for n, s in [("xp", (SEQ, 1024)), ("xs", (TS, 1024)), ("c_conv", (30, 512)), ("c_fk", (PAST, 512)),
                 ("c_fv", (PAST, 512)), ("c_ff", (PAST, 8)), ("c_pool", (15, 512)), ("c_dk", (PAST, 512)),
                 ("c_dv", (PAST, 512)), ("c_di", (PAST, 64)),
                 ("e_w_in", (1024, E_IN)), ("e_b_f", (8, 1)), ("e_w_out", (1024, 1024)),
                 ("o_w_in", (1024, O_IN)), ("o_pool_w", (4, 128, 128)), ("o_w_out", (1024, 1024)),
                 ("pvec", (40, 1024)),
                 ("k_cmask", (NTP, 128, TBP)), ("k_adm", (NTP, 128, TBP)), ("k_cos", (SEQ + TS, 256)),
                 ("k_sin", (SEQ + TS, 256)), ("k_poolfix", (128, 4, 16))]:
        I[n] = din(n, s)
    O = {}
    for n, s in [("y_p", (SEQ, 1024)), ("y_s", (TS, 1024)), ("conv_p", (30, 512)), ("conv_s", (30, 512)),
                 ("fk_p", (SEQ, 512)), ("fk_s", (TS, 512)), ("fv_p", (SEQ, 512)), ("fv_s", (TS, 512)),
                 ("ff_p", (SEQ, 8)), ("ff_s", (TS, 8)), ("pool_p", (15, 512)), ("pool_s", (15, 512)),
                 ("dk_p", (SEQ, 512)), ("dk_s", (TS, 512)), ("dv_p", (SEQ, 512)), ("dv_s", (TS, 512)),
                 ("di_p", (SEQ, 64)), ("di_s", (TS, 64))]:
        O[n] = dout(n, s)

    NKS = PAST + TS
    SC = {}
    for sq, nk in [("P", SEQ), ("S", NKS)]:
        nkt = (nk + 127) // 128
        for l in (0, 1):
            SC[f"KT{l}{sq}"] = dscr(f"KT{l}{sq}", (4, 128, nk), BF16)
            SC[f"VS{l}{sq}"] = dscr(f"VS{l}{sq}", (4, 128, nkt, 130), BF16)
        SC[f"IK{sq}"] = dscr(f"IK{sq}", (64, nk), BF16)
    WS = {"e_in": dscr("ws_e_in", (29, 128, 8, 128), BF16), "o_in": dscr("ws_o_in", (27, 128, 8, 128), BF16),
          "e_oa": dscr("ws_e_oa", (8, 128, 4, 128), BF16), "e_oo": dscr("ws_e_oo", (8, 64, 8, 128), BF16),
          "o_oa": dscr("ws_o_oa", (8, 128, 4, 128), BF16), "o_oo": dscr("ws_o_oo", (8, 64, 8, 128), BF16),
          "cd": dscr("ws_cd", (4, 128, 31, 128), BF16)}

    with ExitStack() as es:
        S = Sched(nc, es)
        T = lambda name, shape, dt=F32: es.enter_context(nc.sbuf_tensor(name, list(shape), dt))
        banks = [es.enter_context(nc.psum_tensor(f"pb{i}", [128, 512], F32)) for i in range(8)]
        PS = Rot(banks[:6], "pb")
        PO = [(banks[6], "pb6"), (banks[7], "pb7")]
        op = S.op
        dma = S.dma
        dq = ['sp', 'pool']
        dqi = [0]

        def q():
            dqi[0] += 1
            return 'sp'

        def body():
            scores = T("scores", [128, 8192])
            notsel = T("notsel", [128, NTP, 8192], BF16)
            ident = T("ident", [128, 128])
            negI = T("negI", [128, 128], BF16)
            negI2 = T("negI2", [128, 2, 128], BF16)
            ones_f = T("ones_f", [128, 128])
            onesD = {512: T("ones512", [128, 128], BF16), 1024: T("ones1024", [128, 128], BF16)}
            onesrow = T("onesrow", [8, 512])
            op('pool', lambda e: e.memset(ident[:], 1.0), writes=['ident'])
            op('pool', lambda e: e.affine_select(out=ident[:], in_=ident[:], pattern=[[-1, 128]], compare_op=ALU.is_equal,
                                                 fill=0.0, base=0, channel_multiplier=1), reads=['ident'], writes=['ident'])
            op('dve', lambda e: e.tensor_scalar(out=negI[:], in0=ident[:], scalar1=NEGM, scalar2=None, op0=ALU.mult),
               reads=['ident'], writes=['negI'])
            for i2 in range(2):
                op('dve', lambda e: e.tensor_scalar(out=negI2[:, i2, :], in0=ident[:], scalar1=NEGM, scalar2=None,
                                                    op0=ALU.mult), reads=['ident'], writes=['negI'])
            op('pool', lambda e: e.memset(ones_f[:], 1.0), writes=['ones_f'])
            op('pool', lambda e: e.memset(onesD[512][:], 1.0 / 512), writes=['onesD'])
            op('pool', lambda e: e.memset(onesD[1024][:], 1.0 / 1024), writes=['onesD'])
            op('pool', lambda e: e.memset(onesrow[:], 1.0), writes=['onesrow'])

            cmask_f = scores[:, 4096:4096 + NTP * TBP].rearrange("p (a q) -> p a q", q=TBP)
            cmask = T("cmask", [128, NTP, TBP], BF16)
            cmask2 = T("cmask2", [128, NTP, 2, TBP], BF16)
            dma('sp', cmask_f, I["k_cmask"].rearrange("a p q -> p a q"), writes=['cmask_f'])
            op('dve', lambda e: e.tensor_copy(out=cmask[:], in_=cmask_f), reads=['cmask_f'], writes=['cmask'])
            for i2 in range(2):
                op('dve', lambda e: e.tensor_copy(out=cmask2[:, :, i2, :], in_=cmask_f), reads=['cmask_f'], writes=['cmask'])
            admn = T("admn", [128, NTP, TBP])
            dma('sp', admn[:], I["k_adm"].rearrange("a p q -> p a q"), writes=['admn'])
            poolfix = T("poolfix", [128, 4, 16])
            dma('sp', poolfix[:], I["k_poolfix"], writes=['poolfix'])
            bf_col = T("bf_col", [8, 1])
            dma('sp', bf_col[:], I["e_b_f"], writes=['bf_col'])
            nbf_col = T("nbf_col", [8, 1])
            op('dve', lambda e: e.tensor_scalar(out=nbf_col[:], in0=bf_col[:], scalar1=-1.0, scalar2=None, op0=ALU.mult),
               reads=['bf_col'], writes=['nbf_col'])
            pvt = scores[:, 2048:3072]
            pcol = T("pcol", [128, 8, 40])
            dma('sp', pvt[0:40, :], I["pvec"], writes=['pvt'])
            for c in range(8):
                ps, pk = PS.next()
                op('pe', lambda e, ps=ps, c=c: e.transpose(out=ps[:, 0:40], in_=pvt[0:40, c * 128:(c + 1) * 128],
                                                            identity=ident[0:40, 0:40]), reads=['pvt', 'ident'], writes=[pk])
                op('act', lambda e, ps=ps, c=c: e.copy(out=pcol[:, c, :], in_=ps[:, 0:40]), reads=[pk], writes=['pcol'])
            R_CB, R_CG, R_CBB, R_EG, R_EB, R_PS, R_OG, R_OB = 31, 32, 33, 34, 35, 36, 37, 38
            poolw_f = scores[:, 3072:3584].rearrange("p (g d) -> p g d", d=128)
            poolw = T("poolw", [128, 4, 128], BF16)
            dma('sp', poolw_f, I["o_pool_w"].rearrange("g c d -> c g d"), writes=['poolw_f'])
            op('dve', lambda e: e.tensor_copy(out=poolw[:], in_=poolw_f), reads=['poolw_f'], writes=['poolw'])

            CP(1)
            v8 = lambda ap_: ap_.rearrange("p (k c) -> p k c", c=128)
            stg_f = Rot([v8(scores[:, i * 1024:(i + 1) * 1024]) for i in range(2)], "stgf")
            stg_b = Rot([v8(notsel[:, 0, i * 1024:(i + 1) * 1024]) for i in range(2)], "stgb")
            ceng = ['dve', 'act', 'pool']
            cei = [0]

            def cast(out, in_, reads, writes):
                e = ceng[cei[0] % 3]
                cei[0] += 1
                if e == 'act':
                    op('act', lambda g: g.copy(out=out, in_=in_), reads=reads, writes=writes)
                else:
                    op(e, lambda g: g.tensor_copy(out=out, in_=in_), reads=reads, writes=writes)

            for wn, key, NC in [("e_w_in", "e_in", E_IN), ("o_w_in", "o_in", O_IN)]:
                for cc in range((NC + 127) // 128):
                    ncol = min(128, NC - cc * 128)
                    sf, sfk = stg_f.next()
                    sb, sbk = stg_b.next()
                    if ncol < 128:
                        op('pool', lambda e: e.memset(sf, 0.0), writes=[sfk])
                    dma('sp', sf[:, :, :ncol], I[wn][:, cc * 128:cc * 128 + ncol].rearrange("(k p) c -> p k c", p=128),
                        writes=[sfk])
                    cast(sb, sf, [sfk], [sbk])
                    dma('sp', WS[key][cc], sb, reads=[sbk], writes=['ws_' + key])
            for wn, ka, ko in [("e_w_out", "e_oa", "e_oo"), ("o_w_out", "o_oa", "o_oo")]:
                for cc in range(8):
                    sf, sfk = stg_f.next()
                    sb, sbk = stg_b.next()
                    dma('sp', sf[:, 0:4, :], I[wn][0:512, cc * 128:(cc + 1) * 128].rearrange("(k p) c -> p k c", p=128),
                        writes=[sfk])
                    cast(sb[:, 0:4, :], sf[:, 0:4, :], [sfk], [sbk])
                    dma('sp', WS[ka][cc], sb[:, 0:4, :], reads=[sbk], writes=['ws_' + ka])
                    sf, sfk = stg_f.next()
                    sb, sbk = stg_b.next()
                    dma('sp', sf[0:64, :, :], I[wn][512:1024, cc * 128:(cc + 1) * 128].rearrange("(h p) c -> p h c", p=64),
                        writes=[sfk])
                    cast(sb[0:64, :, :], sf[0:64, :, :], [sfk], [sbk])
                    dma('sp', WS[ko][cc], sb[0:64, :, :], reads=[sbk], writes=['ws_' + ko])
            cdst = notsel[:, 1, 0:31 * 128].rearrange("p (w c) -> p w c", c=128)
            for c in range(4):
                for w in range(31):
                    eng = 'dve' if w % 2 == 0 else 'pool'
                    op(eng, lambda e, c=c, w=w: e.tensor_scalar(out=cdst[:, w, :], in0=ident[:], scalar1=pcol[:, c, w:w + 1],
                                                                scalar2=None, op0=ALU.mult),
                       reads=['ident', 'pcol'], writes=['cdst'])
                dma('sp', WS["cd"][c], cdst, reads=['cdst'], writes=['ws_cd'])

            CP(2)
            S.barrier()
            xT_f = T("xT_f", [128, 8, TBP])
            xT_b = T("xT_b", [128, 8, TBP], BF16)
            zT = T("zT", [128, 8, TBP])
            zflat = zT[:].rearrange("p c t -> p (c t)")
            xtk = lambda t, a, b: zflat[:, t * 1024 + a:t * 1024 + b]
            xtokb = T("xtokb", [128, NTP, 1024])
            wnar = Rot([T(f"wnar{i}", [128, 8, 128], BF16) for i in range(4)], "wnar")
            wwide = Rot([T(f"wwide{i}", [128, 8, 512], BF16) for i in range(2)], "wwide")
            cdw = Rot([T(f"cdw{i}", [128, 31, 128], BF16) for i in range(1)], "cdw")
            a_bf = {sq: T(f"a_bf{sq}", [128, 4, 30 + (TBP if sq == "P" else TS)], BF16) for sq in "PS"}
            cv_f = {sq: T(f"cv_f{sq}", [128, 4, 15 + (TBP if sq == "P" else TS)]) for sq in "PS"}
            sg = T("sg", [128, TBP])
            gA = T("gA", [128, 4, TBP], BF16)
            gB = T("gB", [64, 8, TBP], BF16)
            QTb = {"P": T("QTbP", [128, 4, 2 * TBP], BF16), "S": T("QTbS", [128, 4, 2 * TS], BF16)}
            KTn = T("KTn", [128, 4, TBP], BF16)
            uA = T("uA", [128, 4, TBP], BF16)
            uO = T("uO", [64, 8, TBP], BF16)
            Lrow = T("Lrow", [8, 512])
            CKrow = T("CKrow", [8, 512])
            ckcar = {sq: T(f"ckcar{sq}", [8, 1]) for sq in "PS"}
            CKT = {sq: T(f"CKT{sq}", [128, 64 if sq == "P" else 17, 8]) for sq in "PS"}
            biasT = T("biasT", [128, 64, 8])
            cref = T("cref", [128, 8])
            tmr = Rot([T(f"tmr{i}", [128, 512]) for i in range(3)], "tmr")
            tmb = Rot([T(f"tmb{i}", [128, 512], BF16) for i in range(3)], "tmb")
            vb = Rot([T(f"vb{i}", [128, 4, 130], BF16) for i in range(2)], "vb")
            kbuf = Rot([T(f"kbuf{i}", [128, KCH], BF16) for i in range(2)], "kbuf")
            vbuf = Rot([T(f"vbuf{i}", [128, KCH // 128, 130], BF16) for i in range(2)], "vbuf")
            osb = Rot([T(f"osb{i}", [65, TBP]) for i in range(2)], "osb")
            stat = {n: T("st_" + n, [128, TBP]) for n in ["mean", "m2", "rstd"]}
            ropeC = T("ropeC", [128, 256])
            ropeS = T("ropeS", [128, 256])
            rp = Rot([T(f"rp{i}", [128, 512]) for i in range(2)], "rp")
            rq = Rot([T(f"rq{i}", [128, 256]) for i in range(2)], "rq")
            IQT = T("IQT", [64, 4, TBP], BF16)
            IKn = T("IKn", [64, TBP], BF16)
            wq = T("wq", [128, NTP, 4])
            bs = {n: T("bs_" + n, [128, 8]) for n in ["hi", "lo", "w", "mid", "cnt", "ge", "mn", "sa", "nm"]}
            junkA = wwide.tiles[0][:].rearrange("p k c -> p (k c)")
            lf = T("lf_s", [128, 16, 8])

            for sq in "PS":
                op('pool', lambda e, sq=sq: e.memset(QTb[sq][:], 0.0), writes=['QT'])
                op('pool', lambda e, sq=sq: e.memset(a_bf[sq][:], 0.0), writes=['a_bf' + sq])
                op('pool', lambda e, sq=sq: e.memset(cv_f[sq][:], 0.0), writes=['cv_f' + sq])
                op('pool', lambda e, sq=sq: e.memset(ckcar[sq][:], 0.0), writes=['ckcar' + sq])
                op('pool', lambda e, sq=sq: e.memset(CKT[sq][:], 0.0), writes=['CKT' + sq])
            for (vt, vk) in [(vb.tiles[0], 'vb0'), (vb.tiles[1], 'vb1')]:
                op('pool', lambda e, vt=vt: e.memset(vt[:], 1.0), writes=[vk])

            CP(30)
            wcache = {}

            def load_w(key, cc):
                if wcache.get('n') == (key, cc):
                    return wcache['t']
                w, wk = wnar.next()
                dma(q(), w[:], WS[key][cc], reads=['ws_' + key], writes=[wk])
                wcache['n'] = (key, cc)
                wcache['t'] = (w, wk)
                return w, wk

            def proj_fm(key, cc, lo, ncol, TB, evac):
                w, wk = load_w(key, cc)
                ps, pk = PS.next()
                for kc in range(8):
                    op('pe', lambda e, kc=kc: e.matmul(ps[:ncol, :TB], lhsT=w[:, kc, lo:lo + ncol], rhs=xT_b[:, kc, :TB],
                                                        start=(kc == 0), stop=(kc == 7)), reads=[wk, 'xT_b'], writes=[pk])
                evac(ps, pk)

            def load_wide(key, cc0, ncols):
                w, wk = wwide.next()
                for j in range((ncols + 127) // 128):
                    n = min(128, ncols - j * 128)
                    dma(q(), w[:, :, j * 128:j * 128 + n], WS[key][cc0 + j, :, :, :n], reads=['ws_' + key], writes=[wk])
                return w, wk

            def proj_tm(w, wk, ncols, t, rows):
                ps, pk = PS.next()
                for kc in range(8):
                    op('pe', lambda e, kc=kc: e.matmul(ps[:rows, :ncols], lhsT=xT_b[:, kc, t * 128:t * 128 + rows],
                                                        rhs=w[:, kc, :ncols], start=(kc == 0), stop=(kc == 7)),
                       reads=[wk, 'xT_b'], writes=[pk])
                return ps, pk

            def ln_fm(C, D, rg, rb, func, TB, outs):
                mps, mk = PS.next()
                sps, sk = PS.next()
                for c in range(C):
                    zb, zbk = tmb.next()
                    op('dve', lambda e, c=c, zb=zb: e.tensor_copy(out=zb[:, :TB], in_=zT[:, c, :TB]), reads=['zT'], writes=[zbk])
                    op('pe', lambda e, c=c, zb=zb: e.matmul(mps[:, :TB], lhsT=onesD[D][:], rhs=zb[:, :TB], start=(c == 0),
                                                             stop=(c == C - 1)), reads=[zbk, 'onesD'], writes=[mk])
                    zs, zsk = tmb.next()
                    op('act', lambda e, c=c, zs=zs: e.activation(out=zs[:, :TB], in_=zT[:, c, :TB], func=AF.Square),
                       reads=['zT'], writes=[zsk])
                    op('pe', lambda e, c=c, zs=zs: e.matmul(sps[:, :TB], lhsT=onesD[D][:], rhs=zs[:, :TB], start=(c == 0),
                                                             stop=(c == C - 1)), reads=[zsk, 'onesD'], writes=[sk])
                mean, m2, rstd = stat["mean"], stat["m2"], stat["rstd"]
                op('act', lambda e: e.copy(out=mean[:, :TB], in_=mps[:, :TB]), reads=[mk], writes=['st_mean'])
                op('dve', lambda e: e.tensor_tensor(out=m2[:, :TB], in0=mean[:, :TB], in1=mean[:, :TB], op=ALU.mult),
                   reads=['st_mean'], writes=['st_m2'])
                op('dve', lambda e: e.tensor_tensor(out=m2[:, :TB], in0=sps[:, :TB], in1=m2[:, :TB], op=ALU.subtract),
                   reads=[sk, 'st_m2'], writes=['st_m2'])
                op('dve', lambda e: e.tensor_scalar(out=m2[:, :TB], in0=m2[:, :TB], scalar1=1e-5, scalar2=None, op0=ALU.add),
                   reads=['st_m2'], writes=['st_m2'])
                op('act', lambda e: e.activation(out=rstd[:, :TB], in_=m2[:, :TB], func=AF.Sqrt), reads=['st_m2'],
                   writes=['st_rstd'])
                op('dve', lambda e: e.reciprocal(out=rstd[:, :TB], in_=rstd[:, :TB]), reads=['st_rstd'], writes=['st_rstd'])
                bc = lambda tl: tl[:, :TB].unsqueeze(1).to_broadcast([128, C, TB])
                op('pool', lambda e: e.tensor_tensor(out=zT[:, :C, :TB], in0=zT[:, :C, :TB], in1=bc(mean), op=ALU.subtract),
                   reads=['zT', 'st_mean'], writes=['zT'])
                op('dve', lambda e: e.tensor_tensor(out=zT[:, :C, :TB], in0=zT[:, :C, :TB], in1=bc(rstd), op=ALU.mult),
                   reads=['zT', 'st_rstd'], writes=['zT'])
                ot, ok = outs[0]
                for c in range(C):
                    op('act', lambda e, c=c: e.activation(out=ot[:, c, :TB], in_=zT[:, c, :TB], func=func,
                                                          bias=pcol[:, c, rb:rb + 1], scale=pcol[:, c, rg:rg + 1]),
                       reads=['zT', 'pcol'], writes=[ok])
                for (o2, o2k) in outs[1:]:
                    op('dve', lambda e: e.tensor_copy(out=o2[:, :C, :TB], in_=ot[:, :C, :TB]), reads=[ok], writes=[o2k])

            def out_proj_ln(ka, ko, rg, rb, TB):
                for cc in range(8):
                    wa, wak = wnar.next()
                    dma(q(), wa[:, 0:4, :], WS[ka][cc], reads=['ws_' + ka], writes=[wak])
                    wo, wok = wnar.next()
                    dma(q(), wo[0:64, :, :], WS[ko][cc], reads=['ws_' + ko], writes=[wok])
                    CP(60 + cc)
                    ps, pk = PS.next()
                    for kc in range(4):
                        op('pe', lambda e, kc=kc: e.matmul(ps[:, :TB], lhsT=wa[:, kc, :], rhs=uA[:, kc, :TB], start=(kc == 0),
                                                            stop=False), reads=[wak, 'uA'], writes=[pk])
                    CP(70 + cc)
                    for h in range(8):
                        op('pe', lambda e, h=h: e.matmul(ps[:, :TB], lhsT=wo[0:64, h, :], rhs=uO[0:64, h, :TB], start=False,
                                                          stop=(h == 7)), reads=[wok, 'uO'], writes=[pk])
                    CP(80 + cc)
                    op('dve', lambda e, cc=cc: e.scalar_tensor_tensor(out=zT[:, cc, :TB], in0=xT_f[:, cc, :TB], scalar=ALPHA,
                                                                      in1=ps[:, :TB], op0=ALU.mult, op1=ALU.add),
                       reads=[pk, 'xT_f'], writes=['zT'])
                    CP(44 + cc)
                wcache.clear()
                CP(40)
                ln_fm(8, 1024, rg, rb, AF.Identity, TB, [(xT_f, 'xT_f'), (xT_b, 'xT_b')])

            def attend(l, sq, TB, pos0, fox, nqt):
                KTS, VSS = SC[f"KT{l}{sq}"], SC[f"VS{l}{sq}"]
                nk = pos0 + TB
                nktot = (nk + 127) // 128
                tpc = KCH // 128
                LA = 3
                for pair in range(4):
                    chunk = {}
                    st = {}

                    def get_chunk(ch):
                        if ch not in chunk:
                            k0 = ch * KCH
                            n = min(KCH, nk - k0)
                            nkt = (n + 127) // 128
                            kb, kbk = kbuf.next()
                            vv, vvk = vbuf.next()
                            dma(q(), kb[:, :n], KTS[pair, :, k0:k0 + n], reads=[f"KT{l}{sq}"], writes=[kbk])
                            nfull = n // 128
                            if nfull > 0:
                                dma(q(), vv[:, :nfull, :], VSS[pair, :, k0 // 128:k0 // 128 + nfull, :], reads=[f"VS{l}{sq}"],
                                    writes=[vvk])
                            if nfull < nkt:
                                rr = n - nfull * 128
                                dma(q(), vv[:rr, nfull, :], VSS[pair, :rr, k0 // 128 + nfull, :], reads=[f"VS{l}{sq}"],
                                    writes=[vvk])
                            chunk[ch] = (kb, kbk, vv, vvk)
                        return chunk[ch]

                    def stA(kt):
                        ch, j = kt // tpc, kt % tpc
                        ks = min(128, nk - kt * 128)
                        kb, kbk, vv, vvk = get_chunk(ch)
                        ps, pk = PS.next()
                        diag = (kt * 128 >= pos0)
                        masked = (not fox) or diag
                        op('pe', lambda e: e.matmul(ps[:ks, :2 * TB], lhsT=kb[:, j * 128:j * 128 + ks], rhs=QTb[sq][:, pair, :2 * TB],
                                                    start=True, stop=not masked), reads=[kbk, 'QT'], writes=[pk])
                        if fox and diag:
                            kl = (kt * 128 - pos0) // 128
                            if TB == TBP:
                                op('pe', lambda e: e.matmul(ps[:ks, :2 * TB], lhsT=negI[:ks, :ks],
                                                            rhs=cmask2[:ks, kl, :, :].rearrange("p h q -> p (h q)"),
                                                            start=False, stop=True), reads=['negI', 'cmask'], writes=[pk])
                            else:
                                for hh in range(2):
                                    op('pe', lambda e: e.matmul(
                                        ps[:ks, hh * TB:hh * TB + TB], lhsT=negI[:ks, :ks], rhs=cmask[:ks, kl, :TB], start=False,
                                        stop=(hh == 1)), reads=['negI', 'cmask'], writes=[pk])
                        if not fox:
                            for qt in range(nqt):
                                qs = min(128, TB - qt * 128)
                                for hh in range(2):
                                    op('pe', lambda e: e.matmul(
                                        ps[:ks, hh * TB + qt * 128:hh * TB + qt * 128 + qs],
                                        lhsT=notsel[:qs, qt, kt * 128:kt * 128 + ks], rhs=negI[:qs, :qs], start=False,
                                        stop=(qt == nqt - 1 and hh == 1)), reads=['negI', 'notsel'], writes=[pk])
                        st[kt] = (ks, j, vv, vvk, ps, pk)

                    def stBC(kt):
                        ks, j, vv, vvk, ps, pk = st.pop(kt)
                        pt, ptk = tmb.next()
                        if fox:
                            for hh in range(2):
                                h = 2 * pair + hh
                                op('act', lambda e: e.activation(
                                    out=pt[:ks, hh * TB:hh * TB + TB], in_=ps[:ks, hh * TB:hh * TB + TB], func=AF.Exp,
                                    bias=biasT[:ks, kt, h:h + 1], scale=0.125), reads=[pk, 'biasT'], writes=[ptk])
                        else:
                            op('act', lambda e: e.activation(out=pt[:ks, :2 * TB], in_=ps[:ks, :2 * TB], func=AF.Exp,
                                                             scale=0.125), reads=[pk], writes=[ptk])
                        for hh in range(2):
                            po, pok = PO[hh]
                            op('pe', lambda e: e.matmul(
                                po[:65, :TB], lhsT=vv[:ks, j, 65 * hh:65 * hh + 65], rhs=pt[:ks, hh * TB:hh * TB + TB],
                                start=(kt == 0), stop=(kt == nktot - 1)), reads=[vvk, ptk], writes=[pok])

                    for i in range(nktot + LA):
                        if i < nktot:
                            stA(i)
                        if i >= LA:
                            stBC(i - LA)
                    for hh in range(2):
                        h = 2 * pair + hh
                        po, pok = PO[hh]
                        ob, obk = osb.next()
                        op('act', lambda e: e.copy(out=ob[:65, :TB], in_=po[:65, :TB]), reads=[pok], writes=[obk])
                        ps, pk = PS.next()
                        op('pe', lambda e: e.matmul(ps[:64, :TB], lhsT=ones_f[64:65, 0:64], rhs=ob[64:65, :TB],
                                                    start=True, stop=True), reads=[obk, 'ones_f'], writes=[pk])
                        t1, t1k = tmr.next()
                        op('dve', lambda e: e.reciprocal(out=t1[:64, :TB], in_=ps[:64, :TB]), reads=[pk], writes=[t1k])
                        op('dve', lambda e: e.tensor_tensor(out=t1[:64, :TB], in0=t1[:64, :TB], in1=ob[:64, :TB],
                                                            op=ALU.mult), reads=[t1k, obk], writes=[t1k])
                        op('pool', lambda e: e.tensor_tensor(out=uO[:64, h, :TB], in0=t1[:64, :TB], in1=gB[:64, h, :TB],
                                                             op=ALU.mult), reads=[t1k, 'gB'], writes=['uO'])

            def transposes_to(dst_fn, src, srck, rows, nchunks, cw, reads_extra=()):
                for j in range(nchunks):
                    ps, pk = PS.next()
                    op('pe', lambda e, j=j, ps=ps: e.transpose(out=ps[:cw, :rows], in_=src[:rows, j * cw:(j + 1) * cw],
                                                                identity=ident[:rows, :rows]),
                       reads=[srck, 'ident'] + list(reads_extra), writes=[pk])
                    dst_fn(j, ps, pk)

            def layer0(sq, x_ap, TB, pos0, outn, final):
                nt = (TB + 127) // 128
                rows_of = lambda t: min(128, TB - t * 128)
                sfx = "_p" if sq == "P" else "_s"
                op0_ = pos0 if sq == "P" else 0
                CP(31)
                for kc in range(8):
                    ps, pk = PS.next()
                    for t in range(nt):
                        r = rows_of(t)
                        op('pe', lambda e, kc=kc, t=t, r=r, ps=ps: e.transpose(
                            out=ps[:, t * 128:t * 128 + r], in_=xtokb[:r, t, kc * 128:(kc + 1) * 128], identity=ident[:r, :r]),
                           reads=['xtokb', 'ident'], writes=[pk])
                    op('act', lambda e, kc=kc, ps=ps: e.copy(out=xT_f[:, kc, :TB], in_=ps[:, :TB]), reads=[pk], writes=['xT_f'])
                    op('dve', lambda e, kc=kc, ps=ps: e.tensor_copy(out=xT_b[:, kc, :TB], in_=ps[:, :TB]), reads=[pk],
                       writes=['xT_b'])
                CP(3)
                ab = a_bf[sq]
                abk = 'a_bf' + sq
                for c in range(4):
                    proj_fm("e_in", 4 + c, 0, 128, TB,
                            lambda ps, pk: op('act', lambda e: e.activation(out=sg[:, :TB], in_=ps[:, :TB], func=AF.Sigmoid),
                                              reads=[pk], writes=['sg']))
                    proj_fm("e_in", c, 0, 128, TB,
                            lambda ps, pk, c=c: op('dve', lambda e: e.tensor_tensor(out=zT[:, 4 + c, :TB], in0=ps[:, :TB],
                                                                                    in1=sg[:, :TB], op=ALU.mult),
                                                   reads=[pk, 'sg'], writes=['zT']))
                    op('pool', lambda e, c=c: e.tensor_copy(out=ab[:, c, 30:30 + TB], in_=zT[:, 4 + c, :TB]), reads=['zT'],
                       writes=[abk])
                if final:
                    assert TB >= 30
                    for c in range(4):
                        ps, pk = PS.next()
                        op('pe', lambda e, c=c, ps=ps: e.transpose(out=ps[:30, 0:128], in_=zT[:, 4 + c, TB - 30:TB],
                                                                    identity=ident[:, :]), reads=['zT', 'ident'], writes=[pk])
                        t1, t1k = tmr.next()
                        op('act', lambda e, t1=t1, ps=ps: e.copy(out=t1[:30, 0:128], in_=ps[:30, 0:128]), reads=[pk], writes=[t1k])
                        dma(q(), O["conv" + sfx][:, c * 128:(c + 1) * 128], t1[:30, 0:128], reads=[t1k])
                CP(4)
                for c in range(4):
                    proj_fm("e_in", 8 + c, 0, 128, TB,
                            lambda ps, pk, c=c: op('act', lambda e: e.activation(out=gA[:, c, :TB], in_=ps[:, :TB], func=AF.Silu),
                                                   reads=[pk], writes=['gA']))
                for h in range(8):
                    proj_fm("e_in", 24 + h // 2, 64 * (h % 2), 64, TB,
                            lambda ps, pk, h=h: op('act', lambda e: e.activation(out=gB[:64, h, :TB], in_=ps[:64, :TB],
                                                                                 func=AF.Silu), reads=[pk], writes=['gB']))
                for c in range(4):
                    proj_fm("e_in", 12 + c, 0, 128, TB,
                            lambda ps, pk, c=c: (op('dve', lambda e: e.tensor_copy(out=QTb[sq][0:64, c, 0:TB], in_=ps[0:64, :TB]),
                                                    reads=[pk], writes=['QT']),
                                                 op('dve', lambda e: e.tensor_copy(out=QTb[sq][64:128, c, TB:2 * TB],
                                                                                   in_=ps[64:128, :TB]), reads=[pk], writes=['QT'])))
                for c in range(4):
                    proj_fm("e_in", 16 + c, 0, 128, TB,
                            lambda ps, pk, c=c: op('act', lambda e: e.copy(out=KTn[:, c, :TB], in_=ps[:, :TB]), reads=[pk],
                                                   writes=['KTn']))
                dma(q(), SC[f"KT0{sq}"][:, :, pos0:pos0 + TB].rearrange("a p n -> p a n"), KTn[:, :, :TB], reads=['KTn'],
                    writes=[f"KT0{sq}"])
                CP(5)
                def ev_f(ps, pk):
                    op('act', lambda e: e.activation(out=Lrow[:8, :TB], in_=ps[:8, :TB], func=AF.Exp, bias=nbf_col[:8, 0:1],
                                                     scale=-1.0), reads=[pk, 'nbf_col'], writes=['Lrow'])
                    op('act', lambda e: e.activation(out=Lrow[:8, :TB], in_=Lrow[:8, :TB], func=AF.Ln, bias=1.0),
                       reads=['Lrow'], writes=['Lrow'])
                proj_fm("e_in", 28, 0, 8, TB, ev_f)
                wcache.clear()
                op('dve', lambda e: e.tensor_tensor_scan(out=CKrow[:8, :TB], data0=onesrow[:8, :TB], data1=Lrow[:8, :TB],
                                                         initial=ckcar[sq][:8, 0:1], op0=ALU.mult, op1=ALU.add),
                   reads=['Lrow', 'onesrow', 'ckcar' + sq], writes=['CKrow'])
                op('dve', lambda e: e.tensor_copy(out=ckcar[sq][:8, 0:1], in_=CKrow[:8, TB - 1:TB]), reads=['CKrow'],
                   writes=['ckcar' + sq])
                ckt = CKT[sq]
                cktk = 'CKT' + sq
                for t in range(nt):
                    r = rows_of(t)
                    kt = pos0 // 128 + t
                    ps, pk = PS.next()
                    op('pe', lambda e, t=t, r=r, ps=ps: e.transpose(out=ps[:r, 0:8], in_=CKrow[:8, t * 128:t * 128 + r],
                                                                    identity=ident[:8, :8]), reads=['CKrow', 'ident'], writes=[pk])
                    op('pe', lambda e, t=t, r=r, ps=ps: e.transpose(out=ps[:r, 8:16], in_=Lrow[:8, t * 128:t * 128 + r],
                                                                    identity=ident[:8, :8]), reads=['Lrow', 'ident'], writes=[pk])
                    op('dve', lambda e, r=r, kt=kt, ps=ps: e.tensor_copy(out=ckt[:r, kt, :], in_=ps[:r, 0:8]), reads=[pk],
                       writes=[cktk])
                    t1, t1k = tmr.next()
                    op('act', lambda e, r=r, t1=t1, ps=ps: e.activation(out=t1[:r, 0:8], in_=ps[:r, 8:16], func=AF.Copy,
                                                                        scale=-1.0), reads=[pk], writes=[t1k])
                    dma(q(), O["ff" + sfx][op0_ + t * 128:op0_ + t * 128 + r, :], t1[:r, 0:8], reads=[t1k])
                refp = pos0 + (TB // 2 if TB >= 256 else 0)
                ktm = refp // 128
                rowm = refp % 128
                assert rowm in (0, 32, 64)
                ps, pk = PS.next()
                op('pe', lambda e, ps=ps: e.matmul(ps[:, 0:8], lhsT=ones_f[rowm:rowm + 1, :], rhs=ckt[rowm:rowm + 1, ktm, :],
                                                   start=True, stop=True), reads=[cktk, 'ones_f'], writes=[pk])
                op('act', lambda e, ps=ps: e.copy(out=cref[:, :], in_=ps[:, 0:8]), reads=[pk], writes=['cref'])
                nktot = (pos0 + TB + 127) // 128
                for h in range(8):
                    op('dve' if h % 2 else 'pool', lambda e, h=h: e.tensor_scalar(
                        out=biasT[:, :nktot, h], in0=ckt[:, :nktot, h], scalar1=cref[:, h:h + 1], scalar2=None,
                        op0=ALU.subtract), reads=[cktk, 'cref'], writes=['biasT'])
                CP(6)
                for (cc0, name) in [(16, "fk"), (20, "fv")]:
                    w, wk = load_wide("e_in", cc0, 512)
                    for t in range(nt):
                        r = rows_of(t)
                        ps, pk = proj_tm(w, wk, 512, t, r)
                        t1, t1k = tmr.next()
                        op('act', lambda e, r=r, t1=t1, ps=ps: e.copy(out=t1[:r, :], in_=ps[:r, :]), reads=[pk], writes=[t1k])
                        dma(q(), O[name + sfx][op0_ + t * 128:op0_ + t * 128 + r, :], t1[:r, :], reads=[t1k])
                        if name == "fv":
                            v, vk = vb.next()
                            for pr in range(4):
                                op('dve' if pr % 2 else 'pool', lambda e, r=r, v=v, t1=t1, pr=pr: e.tensor_copy(
                                    out=v[:r, pr, :].rearrange("p (h d) -> p h d", d=65)[:, :, 0:64],
                                    in_=t1[:r, pr * 128:(pr + 1) * 128].rearrange("p (h d) -> p h d", d=64)),
                                   reads=[t1k], writes=[vk])
                            dma(q(), SC[f"VS0{sq}"][:, :r, pos0 // 128 + t, :].rearrange("a p d -> p a d"), v[:r, :, :],
                                reads=[vk], writes=[f"VS0{sq}"])
                CP(7)
                for c in range(4):
                    cw_, cwk = cdw.next()
                    dma(q(), cw_[:], WS["cd"][c], reads=['ws_cd'], writes=[cwk])
                    ps, pk = PS.next()
                    for w in range(31):
                        op('pe', lambda e, c=c, w=w, cw_=cw_, ps=ps: e.matmul(ps[:, :TB], lhsT=cw_[:, w, :], rhs=ab[:, c, w:w + TB],
                                                                             start=(w == 0), stop=(w == 30)),
                           reads=[cwk, abk], writes=[pk])
                    op('act', lambda e, c=c, ps=ps: e.activation(out=zT[:, c, :TB], in_=ps[:, :TB], func=AF.Identity,
                                                                 bias=pcol[:, c, R_CB:R_CB + 1]), reads=[pk, 'pcol'], writes=['zT'])
                t1, t1k = tmb.next()
                op('pool', lambda e, t1=t1: e.tensor_copy(out=t1[:, 0:120].rearrange("p (c w) -> p c w", w=30),
                                                          in_=ab[:, :, TB:TB + 30]), reads=[abk], writes=[t1k])
                op('pool', lambda e, t1=t1: e.tensor_copy(out=ab[:, :, 0:30],
                                                          in_=t1[:, 0:120].rearrange("p (c w) -> p c w", w=30)),
                   reads=[t1k], writes=[abk])
                ln_fm(4, 512, R_CG, R_CBB, AF.Silu, TB, [(zT, 'zT')])
                for c in range(4):
                    op('pool', lambda e, c=c: e.tensor_tensor(out=uA[:, c, :TB], in0=zT[:, c, :TB], in1=gA[:, c, :TB],
                                                              op=ALU.mult), reads=['zT', 'gA'], writes=['uA'])
                CP(8)
                attend(0, sq, TB, pos0, True, nt)
                CP(9)
                out_proj_ln("e_oa", "e_oo", R_EG, R_EB, TB)

            def rope(eng, dst, dstk, src, srck, r, ncol):
                H = ncol // 64
                v3 = lambda ap_, a, b: ap_.rearrange("p (h d) -> p h d", d=64)[:, :, a:b]
                c3 = lambda tl: tl[:r, 0:32 * H].rearrange("p (h d) -> p h d", d=32)
                ta, tak = rq.next()
                tb_, tbk = rq.next()
                x1, x2 = v3(src[:r, :ncol], 0, 32), v3(src[:r, :ncol], 32, 64)
                o1, o2 = v3(dst[:r, :ncol], 0, 32), v3(dst[:r, :ncol], 32, 64)
                a3, b3 = c3(ta), c3(tb_)
                rd = [srck, 'ropeC', 'ropeS']
                op(eng, lambda e: e.tensor_tensor(out=a3, in0=x1, in1=c3(ropeC), op=ALU.mult), reads=rd, writes=[tak])
                op(eng, lambda e: e.tensor_tensor(out=b3, in0=x2, in1=c3(ropeS), op=ALU.mult), reads=rd, writes=[tbk])
                op(eng, lambda e: e.tensor_tensor(out=o1, in0=a3, in1=b3, op=ALU.subtract), reads=[tak, tbk], writes=[dstk])
                op(eng, lambda e: e.tensor_tensor(out=a3, in0=x2, in1=c3(ropeC), op=ALU.mult), reads=rd + [dstk], writes=[tak])
                op(eng, lambda e: e.tensor_tensor(out=b3, in0=x1, in1=c3(ropeS), op=ALU.mult), reads=rd, writes=[tbk])
                op(eng, lambda e: e.tensor_tensor(out=o2, in0=a3, in1=b3, op=ALU.add), reads=[tak, tbk], writes=[dstk])

            def layer1(sq, TB, pos0, final):
                nt = (TB + 127) // 128
                rows_of = lambda t: min(128, TB - t * 128)
                sfx = "_p" if sq == "P" else "_s"
                op0_ = pos0 if sq == "P" else 0
                tab0 = pos0 if sq == "P" else SEQ
                cv = cv_f[sq]
                cvk = 'cv_f' + sq
                work = []
                for c in range(4):
                    work.append(lambda c=c: proj_fm("o_in", c, 0, 128, TB,
                                lambda ps, pk, c=c: op('act', lambda e: e.copy(out=cv[:, c, 15:15 + TB], in_=ps[:, :TB]), reads=[pk],
                                                       writes=[cvk])))
                for c in range(4):
                    work.append(lambda c=c: proj_fm("o_in", 4 + c, 0, 128, TB,
                                lambda ps, pk, c=c: op('act', lambda e: e.activation(out=gA[:, c, :TB], in_=ps[:, :TB], func=AF.Silu),
                                                       reads=[pk], writes=['gA'])))
                for h in range(8):
                    work.append(lambda h=h: proj_fm("o_in", 20 + h // 2, 64 * (h % 2), 64, TB,
                                lambda ps, pk, h=h: op('act', lambda e: e.activation(out=gB[:64, h, :TB], in_=ps[:64, :TB],
                                                                                     func=AF.Silu), reads=[pk], writes=['gB'])))
                def _fin():
                    wcache.clear()
                    if final:
                        for c in range(4):
                            ps, pk = PS.next()
                            op('pe', lambda e, c=c, ps=ps: e.transpose(out=ps[:15, 0:128], in_=cv[:, c, TB:TB + 15],
                                                                        identity=ident[:, :]), reads=[cvk, 'ident'], writes=[pk])
                            t1, t1k = tmr.next()
                            op('act', lambda e, t1=t1, ps=ps: e.copy(out=t1[:15, 0:128], in_=ps[:15, 0:128]), reads=[pk], writes=[t1k])
                            dma(q(), O["pool" + sfx][:, c * 128:(c + 1) * 128], t1[:15, 0:128], reads=[t1k])
                work.append(_fin)
                W_ = 15 + TB
                def _pool(g):
                    src = cv[:, g, :]
                    srck = cvk
                    pp = [tmr.next(), tmr.next()]
                    for s in range(g + 1):
                        sh = 1 << s
                        dst_, dstk_ = pp[s % 2]
                        eng = 'pool'
                        op(eng, lambda e, src=src, dst_=dst_, sh=sh: e.tensor_tensor(
                            out=dst_[:, sh:W_], in0=src[:, sh:W_], in1=src[:, 0:W_ - sh], op=ALU.add),
                           reads=[srck], writes=[dstk_])
                        src = dst_
                        srck = dstk_
                    wdw = 2 << g
                    t1, t1k = tmr.next()
                    op('pool', lambda e, src=src, t1=t1, wdw=wdw: e.tensor_scalar(out=t1[:, :TB], in0=src[:, 15:15 + TB],
                                                                                  scalar1=1.0 / wdw, scalar2=None, op0=ALU.mult),
                       reads=[srck], writes=[t1k])
                    if sq == "P" and pos0 == 0:
                        op('pool', lambda e, src=src, t1=t1, g=g: e.tensor_tensor(out=t1[:, 0:16], in0=src[:, 15:31],
                                                                                 in1=poolfix[:, g, :], op=ALU.mult),
                           reads=[srck, 'poolfix', t1k], writes=[t1k])
                    pb_, pbk = tmb.next()
                    op('pool', lambda e, t1=t1, pb_=pb_, g=g: e.tensor_tensor(out=pb_[:, :TB], in0=t1[:, :TB],
                                                                             in1=cv[:, g, 15:15 + TB], op=ALU.subtract),
                       reads=[t1k, cvk], writes=[pbk])
                    ps, pk = PS.next()
                    op('pe', lambda e, g=g, pb_=pb_, ps=ps: e.matmul(ps[:, :TB], lhsT=poolw[:, g, :], rhs=pb_[:, :TB], start=True,
                                                                      stop=True), reads=['poolw', pbk], writes=[pk])
                    t2, t2k = tmr.next()
                    op('act', lambda e, g=g, t2=t2, ps=ps: e.activation(out=t2[:, :TB], in_=ps[:, :TB], func=AF.Copy,
                                                                        scale=pcol[:, g, R_PS:R_PS + 1]), reads=[pk, 'pcol'],
                       writes=[t2k])
                    op('pool', lambda e, g=g, t2=t2: e.tensor_tensor(out=uA[:, g, :TB], in0=t2[:, :TB], in1=gA[:, g, :TB],
                                                                     op=ALU.mult), reads=[t2k, 'gA'], writes=['uA'])
                for g in range(4):
                    work.append(lambda g=g: _pool(g))
                def _carry():
                    t1, t1k = tmr.next()
                    op('pool', lambda e, t1=t1: e.tensor_copy(out=t1[:, 0:60].rearrange("p (c w) -> p c w", w=15),
                                                              in_=cv[:, :, TB:TB + 15]), reads=[cvk], writes=[t1k])
                    op('pool', lambda e, t1=t1: e.tensor_copy(out=cv[:, :, 0:15],
                                                              in_=t1[:, 0:60].rearrange("p (c w) -> p c w", w=15)),
                       reads=[t1k], writes=[cvk])
                work.append(_carry)
                CP(12)
                for (cc0, ncols, what) in [(8, 512, "q"), (12, 512, "k"), (16, 512, "v"), (24, 324, "i")]:
                    w, wk = load_wide("o_in", cc0, ncols)
                    for t in range(nt):
                        r = rows_of(t)
                        if what != "v":
                            dma(q(), ropeC[:r, :], I["k_cos"][tab0 + t * 128:tab0 + t * 128 + r, :], writes=['ropeC'])
                            dma(q(), ropeS[:r, :], I["k_sin"][tab0 + t * 128:tab0 + t * 128 + r, :], writes=['ropeS'])
                        ps, pk = proj_tm(w, wk, ncols, t, r)
                        x_, xk_ = rp.next()
                        op('act', lambda e, r=r, x_=x_, ps=ps, ncols=ncols: e.copy(out=x_[:r, :ncols], in_=ps[:r, :ncols]),
                           reads=[pk], writes=[xk_])
                        orow = slice(op0_ + t * 128, op0_ + t * 128 + r)
                        if what == "v":
                            dma(q(), O["dv" + sfx][orow, :], x_[:r, :], reads=[xk_])
                            v, vk = vb.next()
                            for pr in range(4):
                                op('dve' if pr % 2 else 'pool', lambda e, r=r, v=v, x_=x_, pr=pr: e.tensor_copy(
                                    out=v[:r, pr, :].rearrange("p (h d) -> p h d", d=65)[:, :, 0:64],
                                    in_=x_[:r, pr * 128:(pr + 1) * 128].rearrange("p (h d) -> p h d", d=64)),
                                   reads=[xk_], writes=[vk])
                            dma(q(), SC[f"VS1{sq}"][:, :r, pos0 // 128 + t, :].rearrange("a p d -> p a d"), v[:r, :, :],
                                reads=[vk], writes=[f"VS1{sq}"])
                            continue
                        y_, yk_ = rp.next()
                        if what == "q":
                            rope('dve', y_, yk_, x_, xk_, r, 512)
                            transposes_to(lambda j, ps, pk, t=t, r=r: (
                                op('act', lambda e: e.copy(out=QTb[sq][0:64, j, t * 128:t * 128 + r], in_=ps[0:64, :r]),
                                   reads=[pk], writes=['QT']),
                                op('act', lambda e: e.copy(out=QTb[sq][64:128, j, TB + t * 128:TB + t * 128 + r],
                                                           in_=ps[64:128, :r]), reads=[pk], writes=['QT'])), y_, yk_, r, 4, 128)
                        elif what == "k":
                            rope('pool', y_, yk_, x_, xk_, r, 512)
                            dma(q(), O["dk" + sfx][orow, :], y_[:r, :], reads=[yk_])
                            transposes_to(lambda j, ps, pk, t=t, r=r: op('dve', lambda e: e.tensor_copy(
                                out=KTn[:, j, t * 128:t * 128 + r], in_=ps[:, :r]), reads=[pk], writes=['KTn']), y_, yk_, r, 4, 128)
                        else:
                            rope('dve', y_, yk_, x_, xk_, r, 320)
                            dma(q(), O["di" + sfx][orow, :], y_[:r, 256:320], reads=[yk_])
                            transposes_to(lambda j, ps, pk, t=t, r=r: op('act', lambda e: e.copy(
                                out=IQT[:, j, t * 128:t * 128 + r], in_=ps[:64, :r]), reads=[pk], writes=['IQT']), y_, yk_, r, 4, 64)
                            ps2, pk2 = PS.next()
                            op('pe', lambda e, r=r, y_=y_, ps2=ps2: e.transpose(out=ps2[:64, :r], in_=y_[:r, 256:320],
                                                                                identity=ident[:r, :r]), reads=[yk_, 'ident'],
                               writes=[pk2])
                            op('dve', lambda e, t=t, r=r, ps2=ps2: e.tensor_copy(out=IKn[:, t * 128:t * 128 + r], in_=ps2[:64, :r]),
                               reads=[pk2], writes=['IKn'])
                            op('dve', lambda e, t=t, r=r, x_=x_: e.tensor_scalar(out=wq[:r, t, :], in0=x_[:r, 320:324], scalar1=0.5,
                                                                                 scalar2=None, op0=ALU.mult), reads=[xk_],
                               writes=['wq'])
                dma(q(), SC[f"KT1{sq}"][:, :, pos0:pos0 + TB].rearrange("a p n -> p a n"), KTn[:, :, :TB], reads=['KTn'],
                    writes=[f"KT1{sq}"])
                dma(q(), SC[f"IK{sq}"][:, pos0:pos0 + TB], IKn[:, :TB], reads=['IKn'], writes=[f"IK{sq}"])
                CP(13)
                nk = pos0 + TB
                nkb = (nk + 511) // 512
                for qt in range(nt):
                    r = rows_of(qt)
                    for ch in range((nk + KCH - 1) // KCH):
                        k0 = ch * KCH
                        n = min(KCH, nk - k0)
                        ib, ibk = kbuf.next()
                        dma(q(), ib[0:64, :n], SC[f"IK{sq}"][:, k0:k0 + n], reads=[f"IK{sq}"], writes=[ibk])
                        for kb_ in range((n + 511) // 512):
                            c0 = kb_ * 512
                            m = min(512, n - c0)
                            g0 = k0 + c0
                            diag = (sq == "P") and (g0 >= pos0)
                            pss = []
                            for hi in range(4):
                                ps, pk = PS.next()
                                op('pe', lambda e, hi=hi, r=r, qt=qt, ib=ib, c0=c0, m=m, ps=ps: e.matmul(
                                    ps[:r, :m], lhsT=IQT[:, hi, qt * 128:qt * 128 + r], rhs=ib[0:64, c0:c0 + m], start=True,
                                    stop=True), reads=['IQT', ibk], writes=[pk])
                                pss.append((ps, pk))
                            for hi in range(4):
                                ps, pk = pss[hi]
                                rl, rlk = tmr.next()
                                op('act', lambda e, r=r, m=m, rl=rl, ps=ps: e.activation(out=rl[:r, :m], in_=ps[:r, :m],
                                                                                         func=AF.Relu, scale=0.125),
                                   reads=[pk], writes=[rlk])
                                dst_ = scores[:r, g0:g0 + m]
                                if hi == 0 and diag:
                                    op('dve', lambda e, r=r, m=m, rl=rl, dst_=dst_, qt=qt: e.scalar_tensor_tensor(
                                        out=dst_, in0=rl[:r, :m], scalar=wq[:r, qt, 0:1], in1=admn[:r, qt, :m], op0=ALU.mult,
                                        op1=ALU.add), reads=[rlk, 'wq', 'admn', 'scores'], writes=['scores'])
                                elif hi == 0:
                                    op('dve', lambda e, r=r, m=m, rl=rl, dst_=dst_, qt=qt: e.tensor_scalar(
                                        out=dst_, in0=rl[:r, :m], scalar1=wq[:r, qt, 0:1], scalar2=None, op0=ALU.mult),
                                       reads=[rlk, 'wq', 'scores'], writes=['scores'])
                                else:
                                    op('dve', lambda e, r=r, m=m, rl=rl, dst_=dst_, qt=qt, hi=hi: e.scalar_tensor_tensor(
                                        out=dst_, in0=rl[:r, :m], scalar=wq[:r, qt, hi:hi + 1], in1=dst_, op0=ALU.mult,
                                        op1=ALU.add), reads=[rlk, 'wq', 'scores'], writes=['scores'])
                    hi_, lo_, w_, mid_, cnt_, ge_, mn_ = (bs[n_] for n_ in ["hi", "lo", "w", "mid", "cnt", "ge", "mn"])
                    B = lambda tl, c=0: tl[:r, c:c + 1]
                    op('dve', lambda e: e.tensor_reduce(out=B(hi_), in_=scores[:r, :nk], axis=AX.X, op=ALU.max),
                       reads=['scores'], writes=['bs_hi'])
                    if sq == "P":
                        nd = pos0
                        t1, t1k = tmr.next()
                        op('dve', lambda e, t1=t1: e.scalar_tensor_tensor(out=t1[:r, :TB], in0=admn[:r, qt, :TB], scalar=-2.0,
                                                                          in1=scores[:r, nd:nd + TB], op0=ALU.mult, op1=ALU.add),
                           reads=['scores', 'admn'], writes=[t1k])
                        op('dve', lambda e, t1=t1: e.tensor_reduce(out=B(lo_), in_=t1[:r, :TB], axis=AX.X, op=ALU.min),
                           reads=[t1k], writes=['bs_lo'])
                        if nd > 0:
                            op('dve', lambda e: e.tensor_reduce(out=B(mn_), in_=scores[:r, :nd], axis=AX.X, op=ALU.min),
                               reads=['scores'], writes=['bs_mn'])
                            op('dve', lambda e: e.tensor_tensor(out=B(lo_), in0=B(lo_), in1=B(mn_), op=ALU.min),
                               reads=['bs_lo', 'bs_mn'], writes=['bs_lo'])
                    else:
                        op('dve', lambda e: e.tensor_reduce(out=B(lo_), in_=scores[:r, :nk], axis=AX.X, op=ALU.min),
                           reads=['scores'], writes=['bs_lo'])
                    op('dve', lambda e: e.tensor_tensor(out=B(w_), in0=B(hi_), in1=B(lo_), op=ALU.subtract),
                       reads=['bs_hi', 'bs_lo'], writes=['bs_w'])
                    spl = nk if nk < 1024 else max(((nk * 9 // 20) // 512) * 512, nk - 4096, 512)
                    nA = nk - spl
                    npc = (spl + 2047) // 2048
                    sA_, nm_ = bs["sa"], bs["nm"]
                    for it in range(NIT):
                        op('dve', lambda e: e.tensor_scalar(out=B(w_), in0=B(w_), scalar1=0.5, scalar2=None, op0=ALU.mult),
                           reads=['bs_w'], writes=['bs_w'])
                        op('dve', lambda e: e.tensor_tensor(out=B(mid_), in0=B(lo_), in1=B(w_), op=ALU.add),
                           reads=['bs_lo', 'bs_w'], writes=['bs_mid'])
                        if nA > 0:
                            op('dve', lambda e: e.tensor_scalar(out=B(nm_), in0=B(mid_), scalar1=-1.0, scalar2=None, op0=ALU.mult),
                               reads=['bs_mid'], writes=['bs_nm'])
                            op('act', lambda e: e.activation(out=junkA[:r, :nA], in_=scores[:r, spl:nk], func=AF.Sign,
                                                             bias=B(nm_), scale=1.0, accum_out=B(sA_)),
                               reads=['scores', 'bs_nm', 'wwide0'], writes=['wwide0', 'bs_sa'])
                        for pc in range(npc):
                            c0 = pc * 2048
                            m = min(2048, spl - c0)
                            op('dve', lambda e, pc=pc, c0=c0, m=m: e.tensor_scalar(
                                out=notsel[:r, qt, c0:c0 + m], in0=scores[:r, c0:c0 + m], scalar1=B(mid_), scalar2=None,
                                op0=ALU.is_ge, op1=ALU.add, accum_out=B(cnt_, pc)), reads=['scores', 'bs_mid', 'notsel'],
                               writes=['notsel', 'bs_cnt'])
                        if work:
                            work.pop(0)()
                        if npc > 1:
                            op('dve', lambda e: e.tensor_reduce(out=B(ge_), in_=cnt_[:r, 0:npc], axis=AX.X, op=ALU.add),
                               reads=['bs_cnt'], writes=['bs_ge'])
                            cs = B(ge_)
                        else:
                            cs = B(cnt_)
                        thr = 255.5
                        if nA > 0:
                            op('dve', lambda e, cs=cs: e.scalar_tensor_tensor(out=B(ge_), in0=B(sA_), scalar=0.5, in1=cs,
                                                                              op0=ALU.mult, op1=ALU.add),
                               reads=['bs_sa', 'bs_cnt', 'bs_ge'], writes=['bs_ge'])
                            cs = B(ge_)
                            thr = 255.5 - nA / 2.0
                        op('dve', lambda e, cs=cs: e.tensor_scalar(out=B(ge_), in0=cs, scalar1=thr, scalar2=None, op0=ALU.is_ge),
                           reads=['bs_cnt', 'bs_ge'], writes=['bs_ge'])
                        op('dve', lambda e: e.scalar_tensor_tensor(out=B(lo_), in0=B(ge_), scalar=B(w_), in1=B(lo_), op0=ALU.mult,
                                                                   op1=ALU.add), reads=['bs_ge', 'bs_w', 'bs_lo'], writes=['bs_lo'])
                    npc = (nk + 2047) // 2048
                    for pc in range(npc):
                        c0 = pc * 2048
                        m = min(2048, nk - c0)
                        op('dve' if pc % 2 == 0 else 'pool', lambda e, c0=c0, m=m: e.tensor_scalar(
                            out=notsel[:r, qt, c0:c0 + m], in0=scores[:r, c0:c0 + m], scalar1=B(lo_), scalar2=None,
                            op0=ALU.is_lt), reads=['scores', 'bs_lo'], writes=['notsel'])
                while work:
                    work.pop(0)()
                CP(14)
                attend(1, sq, TB, pos0, False, nt)
                CP(15)
                out_proj_ln("o_oa", "o_oo", R_OG, R_OB, TB)
                CP(16)
                for t in range(nt):
                    r = rows_of(t)
                    for half in range(2):
                        ps, pk = PS.next()
                        for j in range(4):
                            kc = half * 4 + j
                            op('pe', lambda e, kc=kc, j=j, t=t, r=r, ps=ps: e.transpose(
                                out=ps[:r, j * 128:(j + 1) * 128], in_=xT_f[:, kc, t * 128:t * 128 + r], identity=ident[:, :]),
                               reads=['xT_f', 'ident'], writes=[pk])
                        op('act' if half else 'dve',
                           (lambda e, t=t, r=r, ps=ps, half=half: e.copy(out=xtk(t, half * 512, (half + 1) * 512)[:r, :], in_=ps[:r, :]))
                           if half else
                           (lambda e, t=t, r=r, ps=ps, half=half: e.tensor_copy(out=xtk(t, half * 512, (half + 1) * 512)[:r, :],
                                                                                in_=ps[:r, :])),
                           reads=[pk], writes=['zT'])
                    dma(q(), O["y" + sfx][op0_ + t * 128:op0_ + t * 128 + r, :], xtk(t, 0, 1024)[:r, :], reads=['zT'])

            def stage_sample():
                npt = PAST // 128
                t1, t1k = tmr.next()
                dma(q(), t1[:30, :], I["c_conv"], writes=[t1k])
                for c in range(4):
                    ps, pk = PS.next()
                    op('pe', lambda e, c=c, ps=ps, t1=t1: e.transpose(out=ps[:, 0:30], in_=t1[:30, c * 128:(c + 1) * 128],
                                                                      identity=ident[:30, :30]), reads=[t1k, 'ident'], writes=[pk])
                    op('act', lambda e, c=c, ps=ps: e.copy(out=a_bf["S"][:, c, 0:30], in_=ps[:, 0:30]), reads=[pk],
                       writes=['a_bfS'])
                t2, t2k = tmr.next()
                dma(q(), t2[:15, :], I["c_pool"], writes=[t2k])
                for c in range(4):
                    ps, pk = PS.next()
                    op('pe', lambda e, c=c, ps=ps, t2=t2: e.transpose(out=ps[:, 0:15], in_=t2[:15, c * 128:(c + 1) * 128],
                                                                      identity=ident[:15, :15]), reads=[t2k, 'ident'], writes=[pk])
                    op('act', lambda e, c=c, ps=ps: e.copy(out=cv_f["S"][:, c, 0:15], in_=ps[:, 0:15]), reads=[pk],
                       writes=['cv_fS'])
                dma(q(), lf[:], I["c_ff"].rearrange("(t p) h -> p t h", p=128), writes=['lf_s'])
                for g4 in range(4):
                    ps, pk = PS.next()
                    for j in range(4):
                        t = g4 * 4 + j
                        op('pe', lambda e, t=t, j=j, ps=ps: e.transpose(out=ps[:8, j * 128:(j + 1) * 128], in_=lf[:, t, :],
                                                                        identity=ident[:, :]), reads=['lf_s', 'ident'], writes=[pk])
                    op('act', lambda e, ps=ps: e.activation(out=Lrow[:8, :], in_=ps[:8, :], func=AF.Copy, scale=-1.0), reads=[pk],
                       writes=['Lrow'])
                    op('dve', lambda e: e.tensor_tensor_scan(out=CKrow[:8, :], data0=onesrow[:8, :], data1=Lrow[:8, :],
                                                             initial=ckcar["S"][:8, 0:1], op0=ALU.mult, op1=ALU.add),
                       reads=['Lrow', 'onesrow', 'ckcarS'], writes=['CKrow'])
                    op('dve', lambda e: e.tensor_copy(out=ckcar["S"][:8, 0:1], in_=CKrow[:8, 511:512]), reads=['CKrow'],
                       writes=['ckcarS'])
                    for j in range(4):
                        t = g4 * 4 + j
                        ps2, pk2 = PS.next()
                        op('pe', lambda e, j=j, ps2=ps2: e.transpose(out=ps2[:, 0:8], in_=CKrow[:8, j * 128:(j + 1) * 128],
                                                                     identity=ident[:8, :8]), reads=['CKrow', 'ident'], writes=[pk2])
                        op('dve', lambda e, t=t, ps2=ps2: e.tensor_copy(out=CKT["S"][:, t, :], in_=ps2[:, 0:8]), reads=[pk2],
                           writes=['CKTS'])
                for (src, l) in [("c_fk", 0), ("c_dk", 1)]:
                    for t in range(npt):
                        x_, xk_ = rp.next()
                        dma(q(), x_[:, :], I[src][t * 128:(t + 1) * 128, :], writes=[xk_])
                        ps, pk = PS.next()
                        for j in range(4):
                            op('pe', lambda e, j=j, ps=ps, x_=x_: e.transpose(out=ps[:, j * 128:(j + 1) * 128],
                                                                              in_=x_[:, j * 128:(j + 1) * 128], identity=ident[:, :]),
                               reads=[xk_, 'ident'], writes=[pk])
                        kb, kbk = tmb.next()
                        op('act' if t % 2 else 'dve',
                           (lambda e, kb=kb, ps=ps: e.copy(out=kb[:, :], in_=ps[:, :])) if t % 2 else
                           (lambda e, kb=kb, ps=ps: e.tensor_copy(out=kb[:, :], in_=ps[:, :])), reads=[pk], writes=[kbk])
                        dma(q(), SC[f"KT{l}S"][:, :, t * 128:(t + 1) * 128].rearrange("a p n -> p a n"),
                            kb[:, :].rearrange("p (a n) -> p a n", n=128), reads=[kbk], writes=[f"KT{l}S"])
                for (src, l) in [("c_fv", 0), ("c_dv", 1)]:
                    for t in range(npt):
                        x_, xk_ = rp.next()
                        dma(q(), x_[:, :], I[src][t * 128:(t + 1) * 128, :], writes=[xk_])
                        v, vk = vb.next()
                        for pr in range(4):
                            op('dve' if pr % 2 else 'pool', lambda e, v=v, x_=x_, pr=pr: e.tensor_copy(
                                out=v[:, pr, :].rearrange("p (h d) -> p h d", d=65)[:, :, 0:64],
                                in_=x_[:, pr * 128:(pr + 1) * 128].rearrange("p (h d) -> p h d", d=64)), reads=[xk_], writes=[vk])
                        dma(q(), SC[f"VS{l}S"][:, :, t, :].rearrange("a p d -> p a d"), v[:, :, :], reads=[vk], writes=[f"VS{l}S"])
                for t in range(npt):
                    x_, xk_ = rp.next()
                    dma(q(), x_[:, 0:64], I["c_di"][t * 128:(t + 1) * 128, :], writes=[xk_])
                    ps, pk = PS.next()
                    op('pe', lambda e, ps=ps, x_=x_: e.transpose(out=ps[:64, 0:128], in_=x_[:, 0:64], identity=ident[:, :]),
                       reads=[xk_, 'ident'], writes=[pk])
                    kb, kbk = tmb.next()
                    op('act', lambda e, kb=kb, ps=ps: e.copy(out=kb[:64, 0:128], in_=ps[:64, 0:128]), reads=[pk], writes=[kbk])
                    dma(q(), SC["IKS"][:, t * 128:(t + 1) * 128], kb[:64, 0:128], reads=[kbk], writes=["IKS"])

            def x_load(x_ap, TB):
                for t in range((TB + 127) // 128):
                    r = min(128, TB - t * 128)
                    dma(q(), xtokb[:r, t, :], x_ap[t * 128:t * 128 + r, :], writes=['xtokb'])

            try:
              x_load(I["xp"][0:TBP, :], TBP)
              for blk in range(nblk):
                fin = (blk == nblk - 1)
                layer0("P", I["xp"][blk * TBP:(blk + 1) * TBP, :], TBP, blk * TBP, None, fin)
                CP(10)
                if blk + 1 < nblk:
                    x_load(I["xp"][(blk + 1) * TBP:(blk + 2) * TBP, :], TBP)
                elif do_sample:
                    x_load(I["xs"], TS)
                layer1("P", TBP, blk * TBP, fin)
              if do_sample:
                stage_sample()
                CP(20)
                layer0("S", I["xs"], TS, PAST, None, True)
                CP(21)
                layer1("S", TS, PAST, True)
            except _Stop:
                pass
        try:
            body()
        except _Stop:
            pass
        S.finish()
        build.ninst = S.ninst
    return nc


def _consts(SEQ):
    kk = np.arange(128)[:, None]
    qq = np.arange(TBP)[None, :]
    cmask = np.stack([((128 * a + kk) > qq).astype(np.float32) for a in range(NTP)])
    q2 = np.arange(128)[:, None]
    k2 = np.arange(TBP)[None, :]
    adm = np.stack([np.where((k2 // 64) > ((128 * a + q2) // 64), -1e30, 0.0).astype(np.float32) for a in range(NTP)])
    pos = np.concatenate([np.arange(SEQ), PAST + np.arange(TS)]).astype(np.float32)
    inv = (10000.0 ** (-np.arange(0, 64, 2, dtype=np.float32) / 64)).astype(np.float32)
    ang = pos[:, None] * inv[None, :]
    cos = np.tile(np.cos(ang).astype(np.float32), (1, 8))
    sin = np.tile(np.sin(ang).astype(np.float32), (1, 8))
    fix = np.zeros((128, 4, 16), np.float32)
    for g, w in enumerate((2, 4, 8, 16)):
        fix[:, g, :] = 1.0 / np.minimum(np.arange(16) + 1, w)
    return {"k_cmask": cmask, "k_adm": adm, "k_cos": cos, "k_sin": sin, "k_poolfix": fix}


_NC_CACHE = {}


def kernel(x_prompt, x_sample, cache_conv, cache_fox_k, cache_fox_v, cache_fox_logf, cache_pool, cache_dsa_k,
           cache_dsa_v, cache_dsa_idx_k, e_w_in, e_b_f, e_conv_w, e_conv_b, e_conv_ln_g, e_conv_ln_b, e_w_out, e_ln_g,
           e_ln_b, o_w_in, o_pool_w, o_pool_scale, o_w_out, o_ln_g, o_ln_b, _nblk=SEQ // TBP, _sample=True):
    f = lambda a: np.ascontiguousarray(np.asarray(a, dtype=np.float32))
    if (_nblk, _sample) not in _NC_CACHE:
        _NC_CACHE[(_nblk, _sample)] = build(_nblk, _sample)
    nc = _NC_CACHE[(_nblk, _sample)]
    pvec = np.zeros((40, 1024), np.float32)
    pvec[0:31, 0:512] = f(e_conv_w)[0]
    pvec[31, 0:512] = f(e_conv_b)[0]
    pvec[32, 0:512] = f(e_conv_ln_g)[0]
    pvec[33, 0:512] = f(e_conv_ln_b)[0]
    pvec[34] = f(e_ln_g)[0]
    pvec[35] = f(e_ln_b)[0]
    pvec[36, 0:512] = f(o_pool_scale)[0]
    pvec[37] = f(o_ln_g)[0]
    pvec[38] = f(o_ln_b)[0]
    SEQE = _nblk * TBP
    cst = _consts(SEQE)
    shared = {"e_w_in": f(e_w_in)[0], "e_b_f": f(e_b_f)[0].reshape(8, 1), "e_w_out": f(e_w_out)[0],
              "o_w_in": f(o_w_in)[0], "o_pool_w": f(o_pool_w)[0], "o_w_out": f(o_w_out)[0], "pvec": pvec}
    shared.update(cst)
    in_maps = []
    for c in range(8):
        m = dict(shared)
        m["xp"] = f(x_prompt)[c // 4][:SEQE]
        m["xs"] = f(x_sample)[c]
        m["c_conv"] = f(cache_conv)[0, c]
        m["c_fk"] = f(cache_fox_k)[0, c].reshape(PAST, 512)
        m["c_fv"] = f(cache_fox_v)[0, c].reshape(PAST, 512)
        m["c_ff"] = f(cache_fox_logf)[0, c]
        m["c_pool"] = f(cache_pool)[0, c]
        m["c_dk"] = f(cache_dsa_k)[0, c].reshape(PAST, 512)
        m["c_dv"] = f(cache_dsa_v)[0, c].reshape(PAST, 512)
        m["c_di"] = f(cache_dsa_idx_k)[0, c]
        in_maps.append(m)
    res = run_bass_kernel_spmd(nc, in_maps, core_ids=list(range(8)))
    R = res.results
    P = lambda n: np.stack([R[0][n], R[4][n]])
    Sm = lambda n: np.stack([R[c][n] for c in range(8)])
    out = (P("y_p"), Sm("y_s"),
           P("conv_p")[None], Sm("conv_s")[None],
           P("fk_p").reshape(1, 2, SEQE, 8, 64), Sm("fk_s").reshape(1, 8, TS, 8, 64),
           P("fv_p").reshape(1, 2, SEQE, 8, 64), Sm("fv_s").reshape(1, 8, TS, 8, 64),
           P("ff_p")[None], Sm("ff_s")[None],
           P("pool_p")[None], Sm("pool_s")[None],
           P("dk_p").reshape(1, 2, SEQE, 8, 64), Sm("dk_s").reshape(1, 8, TS, 8, 64),
           P("dv_p").reshape(1, 2, SEQE, 8, 64), Sm("dv_s").reshape(1, 8, TS, 8, 64),
           P("di_p")[None], Sm("di_s")[None])
    return tuple(np.ascontiguousarray(o, dtype=np.float32) for o in out)
```

```python
import numpy as np
import concourse.bass as bass
import concourse.mybir as mybir
from concourse.bass_utils import run_bass_kernel_spmd
from contextlib import ExitStack

F32 = mybir.dt.float32
BF16 = mybir.dt.bfloat16
AF = mybir.ActivationFunctionType
ALU = mybir.AluOpType
AX = mybir.AxisListType

SEQ = 8192
PAST = 2048
TS = 32
ALPHA = 4.0 ** 0.25
NIT = 12
TBP = 256
NTP = TBP // 128
KCH = 1024
NEGM = -30000.0
PE_SKIP = True


class _Rec:
    def __getattr__(self, name):
        def f(*a, **k):
            self.call = (name, a, k)
            return self
        return f


class Sched:
    SEM_LIMIT = 2000
    NDMA = 24

    def __init__(self, nc, es):
        self.nc = nc
        self.es = es
        self.names = ['pe', 'act', 'dve', 'pool', 'sp']
        self.prog = {e: [] for e in self.names}
        self.cnt = {e: 0 for e in self.names}
        self.waited = {e: {} for e in self.names}
        self.bufs = {}
        self.pe_force = False
        self.pe_mode = None
        self.dsem = []
        for i in range(self.NDMA):
            s = es.enter_context(nc.semaphore(f"dq{i}"))
            self.dsem.append([s, 0])
        self.dnext = 0
        self.ninst = 0

    def _deps(self, reads, writes):
        toks = []
        for k in reads:
            b = self.bufs.get(k)
            if b and b[0] is not None:
                toks.append(b[0])
            if b and k.startswith('pb'):
                toks.extend(b[1])
        for k in writes:
            b = self.bufs.get(k)
            if b:
                if b[0] is not None:
                    toks.append(b[0])
                toks.extend(b[1])
        return toks

    @staticmethod
    def _key(tok):
        return ('E', tok[1]) if tok[0] == 'E' else ('D', id(tok[1]))

    def _emit_waits(self, e, toks):
        need = {}
        for tok in toks:
            k = self._key(tok)
            if PE_SKIP and e == 'pe' and tok[0] == 'E' and tok[1] == 'pe' and not self.pe_force:
                continue
            if self.waited[e].get(k, 0) >= tok[2]:
                continue
            if k not in need or need[k][2] < tok[2]:
                need[k] = tok
        for k, tok in need.items():
            self.waited[e][k] = tok[2]
            self.prog[e].append(('w', tok))

    def _compact(self, toks):
        best = {}
        for tok in toks:
            k = self._key(tok)
            if k not in best or best[k][2] < tok[2]:
                best[k] = tok
        return list(best.values())

    def _update(self, tok, reads, writes):
        for k in reads:
            b = self.bufs.setdefault(k, [None, []])
            b[1].append(tok)
            if len(b[1]) > 12:
                b[1] = self._compact(b[1])
        for k in writes:
            self.bufs[k] = [tok, []]

    def op(self, e, fn, reads=(), writes=()):
        self._emit_waits(e, self._deps(reads, writes))
        rec = _Rec()
        fn(rec)
        name, a, k = rec.call
        if e == 'pe':
            st_ = k.get('lhsT', k.get('in_'))
            r32 = lambda n: 32 if n <= 32 else (64 if n <= 64 else 128)
            fr = 1
            for d in st_.shape[1:]:
                fr *= d
            mode = (name, r32(st_.shape[0]), r32(fr), str(st_.dtype))
            if mode != self.pe_mode and self.cnt['pe'] > 0:
                self.pe_force = True
                self._emit_waits('pe', [('E', 'pe', self.cnt['pe'])])
                self.pe_force = False
            self.pe_mode = mode
        self.cnt[e] += 1
        idx = self.cnt[e]
        self.prog[e].append(('op', name, a, k, idx))
        self._update(('E', e, idx), reads, writes)
        self.ninst += 1

    def dma(self, q, out, in_, reads=(), writes=()):
        slot = self.dsem[self.dnext]
        self.dnext = (self.dnext + 1) % self.NDMA
        s = slot[0]
        toks = self._deps(reads, writes)
        if slot[1] > 0:
            toks.append(('D', s, slot[1]))
        self._emit_waits(q, toks)
        slot[1] += 16
        self.prog[q].append(('dma', out, in_, s))
        self._update(('D', s, slot[1]), reads, writes)
        self.ninst += 1

    def _all_toks(self):
        toks = [('D', s, v) for (s, v) in self.dsem if v > 0]
        for e in ['pe', 'act', 'dve', 'pool']:
            if self.cnt[e] > 0:
                toks.append(('E', e, self.cnt[e]))
        return toks

    def barrier(self):
        toks = self._all_toks()
        for e in self.names:
            self.pe_force = True
            self._emit_waits(e, toks)
            self.pe_force = False

    def finish(self):
        self._emit_waits('sp', self._all_toks())
        nc = self.nc
        need = {e: set() for e in self.names}
        for e in self.names:
            for ent in self.prog[e]:
                if ent[0] == 'w' and ent[1][0] == 'E':
                    need[ent[1][1]].add(ent[1][2])
        sig = {}
        sems = {}
        self.nsem = 0
        for p in self.names:
            for r, idx in enumerate(sorted(need[p])):
                ep = r // self.SEM_LIMIT
                if (p, ep) not in sems:
                    sems[(p, ep)] = self.es.enter_context(nc.semaphore(f"s{p}{ep}"))
                    self.nsem += 1
                sig[(p, idx)] = (sems[(p, ep)], r % self.SEM_LIMIT + 1)

        def replay(e, eng):
            for ent in self.prog[e]:
                if ent[0] == 'w':
                    tok = ent[1]
                    if tok[0] == 'E':
                        sm, v = sig[(tok[1], tok[2])]
                        eng.wait_ge(sm, v)
                    else:
                        eng.wait_ge(tok[1], tok[2])
                elif ent[0] == 'op':
                    _, name, a, k, idx = ent
                    ins = getattr(eng, name)(*a, **k)
                    if (e, idx) in sig:
                        ins.then_inc(sig[(e, idx)][0], 1)
                else:
                    _, out, in_, s = ent
                    eng.dma_start(out=out, in_=in_).then_inc(s, 16)

        with nc.Block() as block:
            @block.tensor
            def _(eng):
                replay('pe', eng)

            @block.scalar
            def _(eng):
                replay('act', eng)

            @block.vector
            def _(eng):
                replay('dve', eng)

            @block.gpsimd
            def _(eng):
                replay('pool', eng)

            @block.sync
            def _(eng):
                replay('sp', eng)


class _Stop(Exception):
    pass


STOP = [0]


def CP(n):
    if STOP[0] == n:
        raise _Stop()


class Rot:
    def __init__(self, tiles, name):
        self.tiles = tiles
        self.name = name
        self.i = 0

    def next(self):
        j = self.i % len(self.tiles)
        self.i += 1
        return self.tiles[j], f"{self.name}{j}"


E_IN = 3592
O_IN = 3396
W_SPECS = None


def build(nblk=SEQ // TBP, do_sample=True):
    nc = bass.Bass("TRN2", target_bir_lowering=False)
    SEQ = nblk * TBP
    din = lambda n, s: nc.dram_tensor(n, list(s), F32, kind="ExternalInput").ap()
    dout = lambda n, s: nc.dram_tensor(n, list(s), F32, kind="ExternalOutput").ap()
    dscr = lambda n, s, dt: nc.dram_tensor(n, list(s), dt).ap()

    I = {}
    for n, s in [("xp", (SEQ, 1024)), ("xs", (TS, 1024)), ("c_conv", (30, 512)), ("c_fk", (PAST, 512)),
                 ("c_fv", (PAST, 512)), ("c_ff", (PAST, 8)), ("c_pool", (15, 512)), ("c_dk", (PAST, 512)),
                 ("c_dv", (PAST, 512)), ("c_di", (PAST, 64)),
                 ("e_w_in", (1024, E_IN)), ("e_b_f", (8, 1)), ("e_w_out", (1024, 1024)),
                 ("o_w_in", (1024, O_IN)), ("o_pool_w", (4, 128, 128)), ("o_w_out", (1024, 1024)),
                 ("pvec", (40, 1024)),
                 ("k_cmask", (NTP, 128, TBP)), ("k_adm", (NTP, 128, TBP)), ("k_cos", (SEQ + TS, 256)),
                 ("k_sin", (SEQ + TS, 256)), ("k_poolfix", (128, 4, 16))]:
        I[n] = din(n, s)
    O = {}
    for n, s in [("y_p", (SEQ, 1024)), ("y_s", (TS, 1024)), ("conv_p", (30, 512)), ("conv_s", (30, 512)),
                 ("fk_p", (SEQ, 512)), ("fk_s", (TS, 512)), ("fv_p", (SEQ, 512)), ("fv_s", (TS, 512)),
                 ("ff_p", (SEQ, 8)), ("ff_s", (TS, 8)), ("pool_p", (15, 512)), ("pool_s", (15, 512)),
                 ("dk_p", (SEQ, 512)), ("dk_s", (TS, 512)), ("dv_p", (SEQ, 512)), ("dv_s", (TS, 512)),
                 ("di_p", (SEQ, 64)), ("di_s", (TS, 64))]:
        O[n] = dout(n, s)

    NKS = PAST + TS
    SC = {}
    for sq, nk in [("P", SEQ), ("S", NKS)]:
        nkt = (nk + 127) // 128
        for l in (0, 1):
            SC[f"KT{l}{sq}"] = dscr(f"KT{l}{sq}", (4, 128, nk), BF16)
            SC[f"VS{l}{sq}"] = dscr(f"VS{l}{sq}", (4, 128, nkt, 130), BF16)
        SC[f"IK{sq}"] = dscr(f"IK{sq}", (64, nk), BF16)
    WS = {"e_in": dscr("ws_e_in", (29, 128, 8, 128), BF16), "o_in": dscr("ws_o_in", (27, 128, 8, 128), BF16),
          "e_oa": dscr("ws_e_oa", (8, 128, 4, 128), BF16), "e_oo": dscr("ws_e_oo", (8, 64, 8, 128), BF16),
          "o_oa": dscr("ws_o_oa", (8, 128, 4, 128), BF16), "o_oo": dscr("ws_o_oo", (8, 64, 8, 128), BF16),
          "cd": dscr("ws_cd", (4, 128, 31, 128), BF16)}

    with ExitStack() as es:
        S = Sched(nc, es)
        T = lambda name, shape, dt=F32: es.enter_context(nc.sbuf_tensor(name, list(shape), dt))
        banks = [es.enter_context(nc.psum_tensor(f"pb{i}", [128, 512], F32)) for i in range(8)]
        PS = Rot(banks[:6], "pb")
        PO = [(banks[6], "pb6"), (banks[7], "pb7")]
        op = S.op
        dma = S.dma
        dq = ['sp', 'pool']
        dqi = [0]

        def q():
            dqi[0] += 1
            return 'sp'

        def body():
            scores = T("scores", [128, 8192])
            notsel = T("notsel", [128, NTP, 8192], BF16)
            ident = T("ident", [128, 128])
            negI = T("negI", [128, 128], BF16)
            negI2 = T("negI2", [128, 2, 128], BF16)
            ones_f = T("ones_f", [128, 128])
            onesD = {512: T("ones512", [128, 128], BF16), 1024: T("ones1024", [128, 128], BF16)}
            onesrow = T("onesrow", [8, 512])
            op('pool', lambda e: e.memset(ident[:], 1.0), writes=['ident'])
            op('pool', lambda e: e.affine_select(out=ident[:], in_=ident[:], pattern=[[-1, 128]], compare_op=ALU.is_equal,
                                                 fill=0.0, base=0, channel_multiplier=1), reads=['ident'], writes=['ident'])
            op('dve', lambda e: e.tensor_scalar(out=negI[:], in0=ident[:], scalar1=NEGM, scalar2=None, op0=ALU.mult),
               reads=['ident'], writes=['negI'])
            for i2 in range(2):
                op('dve', lambda e: e.tensor_scalar(out=negI2[:, i2, :], in0=ident[:], scalar1=NEGM, scalar2=None,
                                                    op0=ALU.mult), reads=['ident'], writes=['negI'])
            op('pool', lambda e: e.memset(ones_f[:], 1.0), writes=['ones_f'])
            op('pool', lambda e: e.memset(onesD[512][:], 1.0 / 512), writes=['onesD'])
            op('pool', lambda e: e.memset(onesD[1024][:], 1.0 / 1024), writes=['onesD'])
            op('pool', lambda e: e.memset(onesrow[:], 1.0), writes=['onesrow'])

            cmask_f = scores[:, 4096:4096 + NTP * TBP].rearrange("p (a q) -> p a q", q=TBP)
            cmask = T("cmask", [128, NTP, TBP], BF16)
            cmask2 = T("cmask2", [128, NTP, 2, TBP], BF16)
            dma('sp', cmask_f, I["k_cmask"].rearrange("a p q -> p a q"), writes=['cmask_f'])
            op('dve', lambda e: e.tensor_copy(out=cmask[:], in_=cmask_f), reads=['cmask_f'], writes=['cmask'])
            for i2 in range(2):
                op('dve', lambda e: e.tensor_copy(out=cmask2[:, :, i2, :], in_=cmask_f), reads=['cmask_f'], writes=['cmask'])
            admn = T("admn", [128, NTP, TBP])
            dma('sp', admn[:], I["k_adm"].rearrange("a p q -> p a q"), writes=['admn'])
            poolfix = T("poolfix", [128, 4, 16])
            dma('sp', poolfix[:], I["k_poolfix"], writes=['poolfix'])
            bf_col = T("bf_col", [8, 1])
            dma('sp', bf_col[:], I["e_b_f"], writes=['bf_col'])
            nbf_col = T("nbf_col", [8, 1])
            op('dve', lambda e: e.tensor_scalar(out=nbf_col[:], in0=bf_col[:], scalar1=-1.0, scalar2=None, op0=ALU.mult),
               reads=['bf_col'], writes=['nbf_col'])
            pvt = scores[:, 2048:3072]
            pcol = T("pcol", [128, 8, 40])
            dma('sp', pvt[0:40, :], I["pvec"], writes=['pvt'])
            for c in range(8):
                ps, pk = PS.next()
                op('pe', lambda e, ps=ps, c=c: e.transpose(out=ps[:, 0:40], in_=pvt[0:40, c * 128:(c + 1) * 128],
                                                            identity=ident[0:40, 0:40]), reads=['pvt', 'ident'], writes=[pk])
                op('act', lambda e, ps=ps, c=c: e.copy(out=pcol[:, c, :], in_=ps[:, 0:40]), reads=[pk], writes=['pcol'])
            R_CB, R_CG, R_CBB, R_EG, R_EB, R_PS, R_OG, R_OB = 31, 32, 33, 34, 35, 36, 37, 38
            poolw_f = scores[:, 3072:3584].rearrange("p (g d) -> p g d", d=128)
            poolw = T("poolw", [128, 4, 128], BF16)
            dma('sp', poolw_f, I["o_pool_w"].rearrange("g c d -> c g d"), writes=['poolw_f'])
            op('dve', lambda e: e.tensor_copy(out=poolw[:], in_=poolw_f), reads=['poolw_f'], writes=['poolw'])

            CP(1)
            v8 = lambda ap_: ap_.rearrange("p (k c) -> p k c", c=128)
            stg_f = Rot([v8(scores[:, i * 1024:(i + 1) * 1024]) for i in range(2)], "stgf")
            stg_b = Rot([v8(notsel[:, 0, i * 1024:(i + 1) * 1024]) for i in range(2)], "stgb")
            ceng = ['dve', 'act', 'pool']
            cei = [0]

            def cast(out, in_, reads, writes):
                e = ceng[cei[0] % 3]
                cei[0] += 1
                if e == 'act':
                    op('act', lambda g: g.copy(out=out, in_=in_), reads=reads, writes=writes)
                else:
                    op(e, lambda g: g.tensor_copy(out=out, in_=in_), reads=reads, writes=writes)

            for wn, key, NC in [("e_w_in", "e_in", E_IN), ("o_w_in", "o_in", O_IN)]:
                for cc in range((NC + 127) // 128):
                    ncol = min(128, NC - cc * 128)
                    sf, sfk = stg_f.next()
                    sb, sbk = stg_b.next()
                    if ncol < 128:
                        op('pool', lambda e: e.memset(sf, 0.0), writes=[sfk])
                    dma('sp', sf[:, :, :ncol], I[wn][:, cc * 128:cc * 128 + ncol].rearrange("(k p) c -> p k c", p=128),
                        writes=[sfk])
                    cast(sb, sf, [sfk], [sbk])
                    dma('sp', WS[key][cc], sb, reads=[sbk], writes=['ws_' + key])
            for wn, ka, ko in [("e_w_out", "e_oa", "e_oo"), ("o_w_out", "o_oa", "o_oo")]:
                for cc in range(8):
                    sf, sfk = stg_f.next()
                    sb, sbk = stg_b.next()
                    dma('sp', sf[:, 0:4, :], I[wn][0:512, cc * 128:(cc + 1) * 128].rearrange("(k p) c -> p k c", p=128),
                        writes=[sfk])
                    cast(sb[:, 0:4, :], sf[:, 0:4, :], [sfk], [sbk])
                    dma('sp', WS[ka][cc], sb[:, 0:4, :], reads=[sbk], writes=['ws_' + ka])
                    sf, sfk = stg_f.next()
                    sb, sbk = stg_b.next()
                    dma('sp', sf[0:64, :, :], I[wn][512:1024, cc * 128:(cc + 1) * 128].rearrange("(h p) c -> p h c", p=64),
                        writes=[sfk])
                    cast(sb[0:64, :, :], sf[0:64, :, :], [sfk], [sbk])
                    dma('sp', WS[ko][cc], sb[0:64, :, :], reads=[sbk], writes=['ws_' + ko])
            cdst = notsel[:, 1, 0:31 * 128].rearrange("p (w c) -> p w c", c=128)
            for c in range(4):
                for w in range(31):
                    eng = 'dve' if w % 2 == 0 else 'pool'
                    op(eng, lambda e, c=c, w=w: e.tensor_scalar(out=cdst[:, w, :], in0=ident[:], scalar1=pcol[:, c, w:w + 1],
                                                                scalar2=None, op0=ALU.mult),
                       reads=['ident', 'pcol'], writes=['cdst'])
                dma('sp', WS["cd"][c], cdst, reads=['cdst'], writes=['ws_cd'])

            CP(2)
            S.barrier()
            xT_f = T("xT_f", [128, 8, TBP])
            xT_b = T("xT_b", [128, 8, TBP], BF16)
            zT = T("zT", [128, 8, TBP])
            zflat = zT[:].rearrange("p c t -> p (c t)")
            xtk = lambda t, a, b: zflat[:, t * 1024 + a:t * 1024 + b]
            xtokb = T("xtokb", [128, NTP, 1024])
            wnar = Rot([T(f"wnar{i}", [128, 8, 128], BF16) for i in range(4)], "wnar")
            wwide = Rot([T(f"wwide{i}", [128, 8, 512], BF16) for i in range(2)], "wwide")
            cdw = Rot([T(f"cdw{i}", [128, 31, 128], BF16) for i in range(1)], "cdw")
            a_bf = {sq: T(f"a_bf{sq}", [128, 4, 30 + (TBP if sq == "P" else TS)], BF16) for sq in "PS"}
            cv_f = {sq: T(f"cv_f{sq}", [128, 4, 15 + (TBP if sq == "P" else TS)]) for sq in "PS"}
            sg = T("sg", [128, TBP])
            gA = T("gA", [128, 4, TBP], BF16)
            gB = T("gB", [64, 8, TBP], BF16)
            QTb = {"P": T("QTbP", [128, 4, 2 * TBP], BF16), "S": T("QTbS", [128, 4, 2 * TS], BF16)}
            KTn = T("KTn", [128, 4, TBP], BF16)
            uA = T("uA", [128, 4, TBP], BF16)
            uO = T("uO", [64, 8, TBP], BF16)
            Lrow = T("Lrow", [8, 512])
            CKrow = T("CKrow", [8, 512])
            ckcar = {sq: T(f"ckcar{sq}", [8, 1]) for sq in "PS"}
            CKT = {sq: T(f"CKT{sq}", [128, 64 if sq == "P" else 17, 8]) for sq in "PS"}
            biasT = T("biasT", [128, 64, 8])
            cref = T("cref", [128, 8])
            tmr = Rot([T(f"tmr{i}", [128, 512]) for i in range(3)], "tmr")
            tmb = Rot([T(f"tmb{i}", [128, 512], BF16) for i in range(3)], "tmb")
            vb = Rot([T(f"vb{i}", [128, 4, 130], BF16) for i in range(2)], "vb")
            kbuf = Rot([T(f"kbuf{i}", [128, KCH], BF16) for i in range(2)], "kbuf")
            vbuf = Rot([T(f"vbuf{i}", [128, KCH // 128, 130], BF16) for i in range(2)], "vbuf")
            osb = Rot([T(f"osb{i}", [65, TBP]) for i in range(2)], "osb")
            stat = {n: T("st_" + n, [128, TBP]) for n in ["mean", "m2", "rstd"]}
            ropeC = T("ropeC", [128, 256])
            ropeS = T("ropeS", [128, 256])
            rp = Rot([T(f"rp{i}", [128, 512]) for i in range(2)], "rp")
            rq = Rot([T(f"rq{i}", [128, 256]) for i in range(2)], "rq")
            IQT = T("IQT", [64, 4, TBP], BF16)
            IKn = T("IKn", [64, TBP], BF16)
            wq = T("wq", [128, NTP, 4])
            bs = {n: T("bs_" + n, [128, 8]) for n in ["hi", "lo", "w", "mid", "cnt", "ge", "mn", "sa", "nm"]}
            bs["wt"] = T("bs_wt", [128, 16])
            pow2 = T("pow2", [128, 16])
            junkA = wwide.tiles[0][:].rearrange("p k c -> p (k c)")
            lf = T("lf_s", [128, 16, 8])

            for k2 in range(16):
                op('pool', lambda e, k2=k2: e.memset(pow2[:, k2:k2 + 1], 0.5 ** (k2 + 1)), writes=['pow2'])
            for sq in "PS":
                op('pool', lambda e, sq=sq: e.memset(QTb[sq][:], 0.0), writes=['QT'])
                op('pool', lambda e, sq=sq: e.memset(a_bf[sq][:], 0.0), writes=['a_bf' + sq])
                op('pool', lambda e, sq=sq: e.memset(cv_f[sq][:], 0.0), writes=['cv_f' + sq])
                op('pool', lambda e, sq=sq: e.memset(ckcar[sq][:], 0.0), writes=['ckcar' + sq])
                op('pool', lambda e, sq=sq: e.memset(CKT[sq][:], 0.0), writes=['CKT' + sq])
            for (vt, vk) in [(vb.tiles[0], 'vb0'), (vb.tiles[1], 'vb1')]:
                op('pool', lambda e, vt=vt: e.memset(vt[:], 1.0), writes=[vk])

            CP(30)
            wcache = {}

            def load_w(key, cc):
                if wcache.get('n') == (key, cc):
                    return wcache['t']
                w, wk = wnar.next()
                dma(q(), w[:], WS[key][cc], reads=['ws_' + key], writes=[wk])
                wcache['n'] = (key, cc)
                wcache['t'] = (w, wk)
                return w, wk

            def proj_fm(key, cc, lo, ncol, TB, evac):
                w, wk = load_w(key, cc)
                ps, pk = PS.next()
                for kc in range(8):
                    op('pe', lambda e, kc=kc: e.matmul(ps[:ncol, :TB], lhsT=w[:, kc, lo:lo + ncol], rhs=xT_b[:, kc, :TB],
                                                        start=(kc == 0), stop=(kc == 7)), reads=[wk, 'xT_b'], writes=[pk])
                evac(ps, pk)

            def load_wide(key, cc0, ncols):
                w, wk = wwide.next()
                for j in range((ncols + 127) // 128):
                    n = min(128, ncols - j * 128)
                    dma(q(), w[:, :, j * 128:j * 128 + n], WS[key][cc0 + j, :, :, :n], reads=['ws_' + key], writes=[wk])
                return w, wk

            def proj_tm(w, wk, ncols, t, rows):
                ps, pk = PS.next()
                for kc in range(8):
                    op('pe', lambda e, kc=kc: e.matmul(ps[:rows, :ncols], lhsT=xT_b[:, kc, t * 128:t * 128 + rows],
                                                        rhs=w[:, kc, :ncols], start=(kc == 0), stop=(kc == 7)),
                       reads=[wk, 'xT_b'], writes=[pk])
                return ps, pk

            def ln_fm(C, D, rg, rb, func, TB, outs):
                mps, mk = PS.next()
                sps, sk = PS.next()
                for c in range(C):
                    zb, zbk = tmb.next()
                    op('dve', lambda e, c=c, zb=zb: e.tensor_copy(out=zb[:, :TB], in_=zT[:, c, :TB]), reads=['zT'], writes=[zbk])
                    op('pe', lambda e, c=c, zb=zb: e.matmul(mps[:, :TB], lhsT=onesD[D][:], rhs=zb[:, :TB], start=(c == 0),
                                                             stop=(c == C - 1)), reads=[zbk, 'onesD'], writes=[mk])
                    zs, zsk = tmb.next()
                    op('act', lambda e, c=c, zs=zs: e.activation(out=zs[:, :TB], in_=zT[:, c, :TB], func=AF.Square),
                       reads=['zT'], writes=[zsk])
                    op('pe', lambda e, c=c, zs=zs: e.matmul(sps[:, :TB], lhsT=onesD[D][:], rhs=zs[:, :TB], start=(c == 0),
                                                             stop=(c == C - 1)), reads=[zsk, 'onesD'], writes=[sk])
                mean, m2, rstd = stat["mean"], stat["m2"], stat["rstd"]
                op('act', lambda e: e.copy(out=mean[:, :TB], in_=mps[:, :TB]), reads=[mk], writes=['st_mean'])
                op('dve', lambda e: e.tensor_tensor(out=m2[:, :TB], in0=mean[:, :TB], in1=mean[:, :TB], op=ALU.mult),
                   reads=['st_mean'], writes=['st_m2'])
                op('dve', lambda e: e.tensor_tensor(out=m2[:, :TB], in0=sps[:, :TB], in1=m2[:, :TB], op=ALU.subtract),
                   reads=[sk, 'st_m2'], writes=['st_m2'])
                op('dve', lambda e: e.tensor_scalar(out=m2[:, :TB], in0=m2[:, :TB], scalar1=1e-5, scalar2=None, op0=ALU.add),
                   reads=['st_m2'], writes=['st_m2'])
                op('act', lambda e: e.activation(out=rstd[:, :TB], in_=m2[:, :TB], func=AF.Sqrt), reads=['st_m2'],
                   writes=['st_rstd'])
                op('dve', lambda e: e.reciprocal(out=rstd[:, :TB], in_=rstd[:, :TB]), reads=['st_rstd'], writes=['st_rstd'])
                bc = lambda tl: tl[:, :TB].unsqueeze(1).to_broadcast([128, C, TB])
                op('pool', lambda e: e.tensor_tensor(out=zT[:, :C, :TB], in0=zT[:, :C, :TB], in1=bc(mean), op=ALU.subtract),
                   reads=['zT', 'st_mean'], writes=['zT'])
                op('dve', lambda e: e.tensor_tensor(out=zT[:, :C, :TB], in0=zT[:, :C, :TB], in1=bc(rstd), op=ALU.mult),
                   reads=['zT', 'st_rstd'], writes=['zT'])
                ot, ok = outs[0]
                for c in range(C):
                    op('act', lambda e, c=c: e.activation(out=ot[:, c, :TB], in_=zT[:, c, :TB], func=func,
                                                          bias=pcol[:, c, rb:rb + 1], scale=pcol[:, c, rg:rg + 1]),
                       reads=['zT', 'pcol'], writes=[ok])
                for (o2, o2k) in outs[1:]:
                    op('dve', lambda e: e.tensor_copy(out=o2[:, :C, :TB], in_=ot[:, :C, :TB]), reads=[ok], writes=[o2k])

            def out_proj_ln(ka, ko, rg, rb, TB):
                for cc in range(8):
                    wa, wak = wnar.next()
                    dma(q(), wa[:, 0:4, :], WS[ka][cc], reads=['ws_' + ka], writes=[wak])
                    wo, wok = wnar.next()
                    dma(q(), wo[0:64, :, :], WS[ko][cc], reads=['ws_' + ko], writes=[wok])
                    CP(60 + cc)
                    ps, pk = PS.next()
                    for kc in range(4):
                        op('pe', lambda e, kc=kc: e.matmul(ps[:, :TB], lhsT=wa[:, kc, :], rhs=uA[:, kc, :TB], start=(kc == 0),
                                                            stop=False), reads=[wak, 'uA'], writes=[pk])
                    CP(70 + cc)
                    for h in range(8):
                        op('pe', lambda e, h=h: e.matmul(ps[:, :TB], lhsT=wo[0:64, h, :], rhs=uO[0:64, h, :TB], start=False,
                                                          stop=(h == 7)), reads=[wok, 'uO'], writes=[pk])
                    CP(80 + cc)
                    op('dve', lambda e, cc=cc: e.scalar_tensor_tensor(out=zT[:, cc, :TB], in0=xT_f[:, cc, :TB], scalar=ALPHA,
                                                                      in1=ps[:, :TB], op0=ALU.mult, op1=ALU.add),
                       reads=[pk, 'xT_f'], writes=['zT'])
                    CP(44 + cc)
                wcache.clear()
                CP(40)
                ln_fm(8, 1024, rg, rb, AF.Identity, TB, [(xT_f, 'xT_f'), (xT_b, 'xT_b')])

            def attend(l, sq, TB, pos0, fox, nqt):
                KTS, VSS = SC[f"KT{l}{sq}"], SC[f"VS{l}{sq}"]
                nk = pos0 + TB
                nktot = (nk + 127) // 128
                tpc = KCH // 128
                LA = 3
                for pair in range(4):
                    chunk = {}
                    st = {}

                    def get_chunk(ch):
                        if ch not in chunk:
                            k0 = ch * KCH
                            n = min(KCH, nk - k0)
                            nkt = (n + 127) // 128
                            kb, kbk = kbuf.next()
                            vv, vvk = vbuf.next()
                            dma(q(), kb[:, :n], KTS[pair, :, k0:k0 + n], reads=[f"KT{l}{sq}"], writes=[kbk])
                            nfull = n // 128
                            if nfull > 0:
                                dma(q(), vv[:, :nfull, :], VSS[pair, :, k0 // 128:k0 // 128 + nfull, :], reads=[f"VS{l}{sq}"],
                                    writes=[vvk])
                            if nfull < nkt:
                                rr = n - nfull * 128
                                dma(q(), vv[:rr, nfull, :], VSS[pair, :rr, k0 // 128 + nfull, :], reads=[f"VS{l}{sq}"],
                                    writes=[vvk])
                            chunk[ch] = (kb, kbk, vv, vvk)
                        return chunk[ch]

                    def stA(kt):
                        ch, j = kt // tpc, kt % tpc
                        ks = min(128, nk - kt * 128)
                        kb, kbk, vv, vvk = get_chunk(ch)
                        ps, pk = PS.next()
                        diag = (kt * 128 >= pos0)
                        masked = (not fox) or diag
                        op('pe', lambda e: e.matmul(ps[:ks, :2 * TB], lhsT=kb[:, j * 128:j * 128 + ks], rhs=QTb[sq][:, pair, :2 * TB],
                                                    start=True, stop=not masked), reads=[kbk, 'QT'], writes=[pk])
                        if fox and diag:
                            kl = (kt * 128 - pos0) // 128
                            if TB == TBP:
                                op('pe', lambda e: e.matmul(ps[:ks, :2 * TB], lhsT=negI[:ks, :ks],
                                                            rhs=cmask2[:ks, kl, :, :].rearrange("p h q -> p (h q)"),
                                                            start=False, stop=True), reads=['negI', 'cmask'], writes=[pk])
                            else:
                                for hh in range(2):
                                    op('pe', lambda e: e.matmul(
                                        ps[:ks, hh * TB:hh * TB + TB], lhsT=negI[:ks, :ks], rhs=cmask[:ks, kl, :TB], start=False,
                                        stop=(hh == 1)), reads=['negI', 'cmask'], writes=[pk])
                        if not fox:
                            for qt in range(nqt):
                                qs = min(128, TB - qt * 128)
                                for hh in range(2):
                                    op('pe', lambda e: e.matmul(
                                        ps[:ks, hh * TB + qt * 128:hh * TB + qt * 128 + qs],
                                        lhsT=notsel[:qs, qt, kt * 128:kt * 128 + ks], rhs=negI[:qs, :qs], start=False,
                                        stop=(qt == nqt - 1 and hh == 1)), reads=['negI', 'notsel'], writes=[pk])
                        st[kt] = (ks, j, vv, vvk, ps, pk)

                    def stBC(kt):
                        ks, j, vv, vvk, ps, pk = st.pop(kt)
                        pt, ptk = tmb.next()
                        if fox:
                            for hh in range(2):
                                h = 2 * pair + hh
                                op('act', lambda e: e.activation(
                                    out=pt[:ks, hh * TB:hh * TB + TB], in_=ps[:ks, hh * TB:hh * TB + TB], func=AF.Exp,
                                    bias=biasT[:ks, kt, h:h + 1], scale=0.125), reads=[pk, 'biasT'], writes=[ptk])
                        else:
                            op('act', lambda e: e.activation(out=pt[:ks, :2 * TB], in_=ps[:ks, :2 * TB], func=AF.Exp,
                                                             scale=0.125), reads=[pk], writes=[ptk])
                        for hh in range(2):
                            po, pok = PO[hh]
                            op('pe', lambda e: e.matmul(
                                po[:65, :TB], lhsT=vv[:ks, j, 65 * hh:65 * hh + 65], rhs=pt[:ks, hh * TB:hh * TB + TB],
                                start=(kt == 0), stop=(kt == nktot - 1)), reads=[vvk, ptk], writes=[pok])

                    for i in range(nktot + LA):
                        if i < nktot:
                            stA(i)
                        if i >= LA:
                            stBC(i - LA)
                    for hh in range(2):
                        h = 2 * pair + hh
                        po, pok = PO[hh]
                        ob, obk = osb.next()
                        op('act', lambda e: e.copy(out=ob[:65, :TB], in_=po[:65, :TB]), reads=[pok], writes=[obk])
                        ps, pk = PS.next()
                        op('pe', lambda e: e.matmul(ps[:64, :TB], lhsT=ones_f[64:65, 0:64], rhs=ob[64:65, :TB],
                                                    start=True, stop=True), reads=[obk, 'ones_f'], writes=[pk])
                        t1, t1k = tmr.next()
                        op('dve', lambda e: e.reciprocal(out=t1[:64, :TB], in_=ps[:64, :TB]), reads=[pk], writes=[t1k])
                        op('dve', lambda e: e.tensor_tensor(out=t1[:64, :TB], in0=t1[:64, :TB], in1=ob[:64, :TB],
                                                            op=ALU.mult), reads=[t1k, obk], writes=[t1k])
                        op('pool', lambda e: e.tensor_tensor(out=uO[:64, h, :TB], in0=t1[:64, :TB], in1=gB[:64, h, :TB],
                                                             op=ALU.mult), reads=[t1k, 'gB'], writes=['uO'])

            def transposes_to(dst_fn, src, srck, rows, nchunks, cw, reads_extra=()):
                for j in range(nchunks):
                    ps, pk = PS.next()
                    op('pe', lambda e, j=j, ps=ps: e.transpose(out=ps[:cw, :rows], in_=src[:rows, j * cw:(j + 1) * cw],
                                                                identity=ident[:rows, :rows]),
                       reads=[srck, 'ident'] + list(reads_extra), writes=[pk])
                    dst_fn(j, ps, pk)

            def layer0(sq, x_ap, TB, pos0, outn, final):
                nt = (TB + 127) // 128
                rows_of = lambda t: min(128, TB - t * 128)
                sfx = "_p" if sq == "P" else "_s"
                op0_ = pos0 if sq == "P" else 0
                CP(31)
                for kc in range(8):
                    ps, pk = PS.next()
                    for t in range(nt):
                        r = rows_of(t)
                        op('pe', lambda e, kc=kc, t=t, r=r, ps=ps: e.transpose(
                            out=ps[:, t * 128:t * 128 + r], in_=xtokb[:r, t, kc * 128:(kc + 1) * 128], identity=ident[:r, :r]),
                           reads=['xtokb', 'ident'], writes=[pk])
                    op('act', lambda e, kc=kc, ps=ps: e.copy(out=xT_f[:, kc, :TB], in_=ps[:, :TB]), reads=[pk], writes=['xT_f'])
                    op('dve', lambda e, kc=kc, ps=ps: e.tensor_copy(out=xT_b[:, kc, :TB], in_=ps[:, :TB]), reads=[pk],
                       writes=['xT_b'])
                CP(3)
                ab = a_bf[sq]
                abk = 'a_bf' + sq
                wkv = [load_wide("e_in", 16, 512), load_wide("e_in", 20, 512)]
                for c in range(4):
                    proj_fm("e_in", 4 + c, 0, 128, TB,
                            lambda ps, pk: op('act', lambda e: e.activation(out=sg[:, :TB], in_=ps[:, :TB], func=AF.Sigmoid),
                                              reads=[pk], writes=['sg']))
                    proj_fm("e_in", c, 0, 128, TB,
                            lambda ps, pk, c=c: op('dve', lambda e: e.tensor_tensor(out=zT[:, 4 + c, :TB], in0=ps[:, :TB],
                                                                                    in1=sg[:, :TB], op=ALU.mult),
                                                   reads=[pk, 'sg'], writes=['zT']))
                    op('pool', lambda e, c=c: e.tensor_copy(out=ab[:, c, 30:30 + TB], in_=zT[:, 4 + c, :TB]), reads=['zT'],
                       writes=[abk])
                if final:
                    assert TB >= 30
                    for c in range(4):
                        ps, pk = PS.next()
                        op('pe', lambda e, c=c, ps=ps: e.transpose(out=ps[:30, 0:128], in_=zT[:, 4 + c, TB - 30:TB],
                                                                    identity=ident[:, :]), reads=['zT', 'ident'], writes=[pk])
                        t1, t1k = tmr.next()
                        op('act', lambda e, t1=t1, ps=ps: e.copy(out=t1[:30, 0:128], in_=ps[:30, 0:128]), reads=[pk], writes=[t1k])
                        dma(q(), O["conv" + sfx][:, c * 128:(c + 1) * 128], t1[:30, 0:128], reads=[t1k])
                CP(4)
                for c in range(4):
                    proj_fm("e_in", 8 + c, 0, 128, TB,
                            lambda ps, pk, c=c: op('act', lambda e: e.activation(out=gA[:, c, :TB], in_=ps[:, :TB], func=AF.Silu),
                                                   reads=[pk], writes=['gA']))
                for h in range(8):
                    proj_fm("e_in", 24 + h // 2, 64 * (h % 2), 64, TB,
                            lambda ps, pk, h=h: op('act', lambda e: e.activation(out=gB[:64, h, :TB], in_=ps[:64, :TB],
                                                                                 func=AF.Silu), reads=[pk], writes=['gB']))
                for c in range(4):
                    proj_fm("e_in", 12 + c, 0, 128, TB,
                            lambda ps, pk, c=c: (op('dve', lambda e: e.tensor_copy(out=QTb[sq][0:64, c, 0:TB], in_=ps[0:64, :TB]),
                                                    reads=[pk], writes=['QT']),
                                                 op('dve', lambda e: e.tensor_copy(out=QTb[sq][64:128, c, TB:2 * TB],
                                                                                   in_=ps[64:128, :TB]), reads=[pk], writes=['QT'])))
                for c in range(4):
                    proj_fm("e_in", 16 + c, 0, 128, TB,
                            lambda ps, pk, c=c: op('act', lambda e: e.copy(out=KTn[:, c, :TB], in_=ps[:, :TB]), reads=[pk],
                                                   writes=['KTn']))
                dma(q(), SC[f"KT0{sq}"][:, :, pos0:pos0 + TB].rearrange("a p n -> p a n"), KTn[:, :, :TB], reads=['KTn'],
                    writes=[f"KT0{sq}"])
                CP(5)
                def ev_f(ps, pk):
                    op('act', lambda e: e.activation(out=Lrow[:8, :TB], in_=ps[:8, :TB], func=AF.Exp, bias=nbf_col[:8, 0:1],
                                                     scale=-1.0), reads=[pk, 'nbf_col'], writes=['Lrow'])
                    op('act', lambda e: e.activation(out=Lrow[:8, :TB], in_=Lrow[:8, :TB], func=AF.Ln, bias=1.0),
                       reads=['Lrow'], writes=['Lrow'])
                proj_fm("e_in", 28, 0, 8, TB, ev_f)
                wcache.clear()
                op('dve', lambda e: e.tensor_tensor_scan(out=CKrow[:8, :TB], data0=onesrow[:8, :TB], data1=Lrow[:8, :TB],
                                                         initial=ckcar[sq][:8, 0:1], op0=ALU.mult, op1=ALU.add),
                   reads=['Lrow', 'onesrow', 'ckcar' + sq], writes=['CKrow'])
                op('dve', lambda e: e.tensor_copy(out=ckcar[sq][:8, 0:1], in_=CKrow[:8, TB - 1:TB]), reads=['CKrow'],
                   writes=['ckcar' + sq])
                ckt = CKT[sq]
                cktk = 'CKT' + sq
                for t in range(nt):
                    r = rows_of(t)
                    kt = pos0 // 128 + t
                    ps, pk = PS.next()
                    op('pe', lambda e, t=t, r=r, ps=ps: e.transpose(out=ps[:r, 0:8], in_=CKrow[:8, t * 128:t * 128 + r],
                                                                    identity=ident[:8, :8]), reads=['CKrow', 'ident'], writes=[pk])
                    op('pe', lambda e, t=t, r=r, ps=ps: e.transpose(out=ps[:r, 8:16], in_=Lrow[:8, t * 128:t * 128 + r],
                                                                    identity=ident[:8, :8]), reads=['Lrow', 'ident'], writes=[pk])
                    op('dve', lambda e, r=r, kt=kt, ps=ps: e.tensor_copy(out=ckt[:r, kt, :], in_=ps[:r, 0:8]), reads=[pk],
                       writes=[cktk])
                    t1, t1k = tmr.next()
                    op('act', lambda e, r=r, t1=t1, ps=ps: e.activation(out=t1[:r, 0:8], in_=ps[:r, 8:16], func=AF.Copy,
                                                                        scale=-1.0), reads=[pk], writes=[t1k])
                    dma(q(), O["ff" + sfx][op0_ + t * 128:op0_ + t * 128 + r, :], t1[:r, 0:8], reads=[t1k])
                refp = pos0 + (TB // 2 if TB >= 256 else 0)
                ktm = refp // 128
                rowm = refp % 128
                assert rowm in (0, 32, 64)
                ps, pk = PS.next()
                op('pe', lambda e, ps=ps: e.matmul(ps[:, 0:8], lhsT=ones_f[rowm:rowm + 1, :], rhs=ckt[rowm:rowm + 1, ktm, :],
                                                   start=True, stop=True), reads=[cktk, 'ones_f'], writes=[pk])
                op('act', lambda e, ps=ps: e.copy(out=cref[:, :], in_=ps[:, 0:8]), reads=[pk], writes=['cref'])
                nktot = (pos0 + TB + 127) // 128
                for h in range(8):
                    op('dve' if h % 2 else 'pool', lambda e, h=h: e.tensor_scalar(
                        out=biasT[:, :nktot, h], in0=ckt[:, :nktot, h], scalar1=cref[:, h:h + 1], scalar2=None,
                        op0=ALU.subtract), reads=[cktk, 'cref'], writes=['biasT'])
                CP(6)
                for gi_, (cc0, name) in enumerate([(16, "fk"), (20, "fv")]):
                    w, wk = wkv[gi_]
                    for t in range(nt):
                        r = rows_of(t)
                        ps, pk = proj_tm(w, wk, 512, t, r)
                        t1, t1k = tmr.next()
                        op('act', lambda e, r=r, t1=t1, ps=ps: e.copy(out=t1[:r, :], in_=ps[:r, :]), reads=[pk], writes=[t1k])
                        dma(q(), O[name + sfx][op0_ + t * 128:op0_ + t * 128 + r, :], t1[:r, :], reads=[t1k])
                        if name == "fv":
                            v, vk = vb.next()
                            for pr in range(4):
                                op('dve' if pr % 2 else 'pool', lambda e, r=r, v=v, t1=t1, pr=pr: e.tensor_copy(
                                    out=v[:r, pr, :].rearrange("p (h d) -> p h d", d=65)[:, :, 0:64],
                                    in_=t1[:r, pr * 128:(pr + 1) * 128].rearrange("p (h d) -> p h d", d=64)),
                                   reads=[t1k], writes=[vk])
                            dma(q(), SC[f"VS0{sq}"][:, :r, pos0 // 128 + t, :].rearrange("a p d -> p a d"), v[:r, :, :],
                                reads=[vk], writes=[f"VS0{sq}"])
                CP(7)
                for c in range(4):
                    cw_, cwk = cdw.next()
                    dma(q(), cw_[:], WS["cd"][c], reads=['ws_cd'], writes=[cwk])
                    ps, pk = PS.next()
                    for w in range(31):
                        op('pe', lambda e, c=c, w=w, cw_=cw_, ps=ps: e.matmul(ps[:, :TB], lhsT=cw_[:, w, :], rhs=ab[:, c, w:w + TB],
                                                                             start=(w == 0), stop=(w == 30)),
                           reads=[cwk, abk], writes=[pk])
                    op('act', lambda e, c=c, ps=ps: e.activation(out=zT[:, c, :TB], in_=ps[:, :TB], func=AF.Identity,
                                                                 bias=pcol[:, c, R_CB:R_CB + 1]), reads=[pk, 'pcol'], writes=['zT'])
                t1, t1k = tmb.next()
                op('pool', lambda e, t1=t1: e.tensor_copy(out=t1[:, 0:120].rearrange("p (c w) -> p c w", w=30),
                                                          in_=ab[:, :, TB:TB + 30]), reads=[abk], writes=[t1k])
                op('pool', lambda e, t1=t1: e.tensor_copy(out=ab[:, :, 0:30],
                                                          in_=t1[:, 0:120].rearrange("p (c w) -> p c w", w=30)),
                   reads=[t1k], writes=[abk])
                ln_fm(4, 512, R_CG, R_CBB, AF.Silu, TB, [(zT, 'zT')])
                for c in range(4):
                    op('pool', lambda e, c=c: e.tensor_tensor(out=uA[:, c, :TB], in0=zT[:, c, :TB], in1=gA[:, c, :TB],
                                                              op=ALU.mult), reads=['zT', 'gA'], writes=['uA'])
                CP(8)
                attend(0, sq, TB, pos0, True, nt)
                CP(9)
                out_proj_ln("e_oa", "e_oo", R_EG, R_EB, TB)

            def rope(eng, dst, dstk, src, srck, r, ncol):
                H = ncol // 64
                v3 = lambda ap_, a, b: ap_.rearrange("p (h d) -> p h d", d=64)[:, :, a:b]
                c3 = lambda tl: tl[:r, 0:32 * H].rearrange("p (h d) -> p h d", d=32)
                ta, tak = rq.next()
                tb_, tbk = rq.next()
                x1, x2 = v3(src[:r, :ncol], 0, 32), v3(src[:r, :ncol], 32, 64)
                o1, o2 = v3(dst[:r, :ncol], 0, 32), v3(dst[:r, :ncol], 32, 64)
                a3, b3 = c3(ta), c3(tb_)
                rd = [srck, 'ropeC', 'ropeS']
                op(eng, lambda e: e.tensor_tensor(out=a3, in0=x1, in1=c3(ropeC), op=ALU.mult), reads=rd, writes=[tak])
                op(eng, lambda e: e.tensor_tensor(out=b3, in0=x2, in1=c3(ropeS), op=ALU.mult), reads=rd, writes=[tbk])
                op(eng, lambda e: e.tensor_tensor(out=o1, in0=a3, in1=b3, op=ALU.subtract), reads=[tak, tbk], writes=[dstk])
                op(eng, lambda e: e.tensor_tensor(out=a3, in0=x2, in1=c3(ropeC), op=ALU.mult), reads=rd + [dstk], writes=[tak])
                op(eng, lambda e: e.tensor_tensor(out=b3, in0=x1, in1=c3(ropeS), op=ALU.mult), reads=rd, writes=[tbk])
                op(eng, lambda e: e.tensor_tensor(out=o2, in0=a3, in1=b3, op=ALU.add), reads=[tak, tbk], writes=[dstk])

            def layer1(sq, TB, pos0, final):
                nt = (TB + 127) // 128
                rows_of = lambda t: min(128, TB - t * 128)
                sfx = "_p" if sq == "P" else "_s"
                op0_ = pos0 if sq == "P" else 0
                tab0 = pos0 if sq == "P" else SEQ
                cv = cv_f[sq]
                cvk = 'cv_f' + sq
                work = []
                for c in range(4):
                    work.append(lambda c=c: proj_fm("o_in", c, 0, 128, TB,
                                lambda ps, pk, c=c: op('act', lambda e: e.copy(out=cv[:, c, 15:15 + TB], in_=ps[:, :TB]), reads=[pk],
                                                       writes=[cvk])))
                for c in range(4):
                    work.append(lambda c=c: proj_fm("o_in", 4 + c, 0, 128, TB,
                                lambda ps, pk, c=c: op('act', lambda e: e.activation(out=gA[:, c, :TB], in_=ps[:, :TB], func=AF.Silu),
                                                       reads=[pk], writes=['gA'])))
                for h in range(8):
                    work.append(lambda h=h: proj_fm("o_in", 20 + h // 2, 64 * (h % 2), 64, TB,
                                lambda ps, pk, h=h: op('act', lambda e: e.activation(out=gB[:64, h, :TB], in_=ps[:64, :TB],
                                                                                     func=AF.Silu), reads=[pk], writes=['gB'])))
                def _fin():
                    wcache.clear()
                    if final:
                        for c in range(4):
                            ps, pk = PS.next()
                            op('pe', lambda e, c=c, ps=ps: e.transpose(out=ps[:15, 0:128], in_=cv[:, c, TB:TB + 15],
                                                                        identity=ident[:, :]), reads=[cvk, 'ident'], writes=[pk])
                            t1, t1k = tmr.next()
                            op('act', lambda e, t1=t1, ps=ps: e.copy(out=t1[:15, 0:128], in_=ps[:15, 0:128]), reads=[pk], writes=[t1k])
                            dma(q(), O["pool" + sfx][:, c * 128:(c + 1) * 128], t1[:15, 0:128], reads=[t1k])
                work.append(_fin)
                W_ = 15 + TB
                def _pool(g):
                    src = cv[:, g, :]
                    srck = cvk
                    pp = [tmr.next(), tmr.next()]
                    for s in range(g + 1):
                        sh = 1 << s
                        dst_, dstk_ = pp[s % 2]
                        eng = 'pool'
                        op(eng, lambda e, src=src, dst_=dst_, sh=sh: e.tensor_tensor(
                            out=dst_[:, sh:W_], in0=src[:, sh:W_], in1=src[:, 0:W_ - sh], op=ALU.add),
                           reads=[srck], writes=[dstk_])
                        src = dst_
                        srck = dstk_
                    wdw = 2 << g
                    t1, t1k = tmr.next()
                    op('pool', lambda e, src=src, t1=t1, wdw=wdw: e.tensor_scalar(out=t1[:, :TB], in0=src[:, 15:15 + TB],
                                                                                  scalar1=1.0 / wdw, scalar2=None, op0=ALU.mult),
                       reads=[srck], writes=[t1k])
                    if sq == "P" and pos0 == 0:
                        op('pool', lambda e, src=src, t1=t1, g=g: e.tensor_tensor(out=t1[:, 0:16], in0=src[:, 15:31],
                                                                                 in1=poolfix[:, g, :], op=ALU.mult),
                           reads=[srck, 'poolfix', t1k], writes=[t1k])
                    pb_, pbk = tmb.next()
                    op('pool', lambda e, t1=t1, pb_=pb_, g=g: e.tensor_tensor(out=pb_[:, :TB], in0=t1[:, :TB],
                                                                             in1=cv[:, g, 15:15 + TB], op=ALU.subtract),
                       reads=[t1k, cvk], writes=[pbk])
                    ps, pk = PS.next()
                    op('pe', lambda e, g=g, pb_=pb_, ps=ps: e.matmul(ps[:, :TB], lhsT=poolw[:, g, :], rhs=pb_[:, :TB], start=True,
                                                                      stop=True), reads=['poolw', pbk], writes=[pk])
                    t2, t2k = tmr.next()
                    op('act', lambda e, g=g, t2=t2, ps=ps: e.activation(out=t2[:, :TB], in_=ps[:, :TB], func=AF.Copy,
                                                                        scale=pcol[:, g, R_PS:R_PS + 1]), reads=[pk, 'pcol'],
                       writes=[t2k])
                    op('pool', lambda e, g=g, t2=t2: e.tensor_tensor(out=uA[:, g, :TB], in0=t2[:, :TB], in1=gA[:, g, :TB],
                                                                     op=ALU.mult), reads=[t2k, 'gA'], writes=['uA'])
                for g in range(4):
                    work.append(lambda g=g: _pool(g))
                def _carry():
                    t1, t1k = tmr.next()
                    op('pool', lambda e, t1=t1: e.tensor_copy(out=t1[:, 0:60].rearrange("p (c w) -> p c w", w=15),
                                                              in_=cv[:, :, TB:TB + 15]), reads=[cvk], writes=[t1k])
                    op('pool', lambda e, t1=t1: e.tensor_copy(out=cv[:, :, 0:15],
                                                              in_=t1[:, 0:60].rearrange("p (c w) -> p c w", w=15)),
                       reads=[t1k], writes=[cvk])
                work.append(_carry)
                CP(12)
                grps = [(8, 512, "q"), (12, 512, "k"), (16, 512, "v"), (24, 324, "i")]
                wpre = load_wide("o_in", grps[0][0], grps[0][1])
                for gi_, (cc0, ncols, what) in enumerate(grps):
                    w, wk = wpre
                    if gi_ + 1 < len(grps):
                        wpre = load_wide("o_in", grps[gi_ + 1][0], grps[gi_ + 1][1])
                    for t in range(nt):
                        r = rows_of(t)
                        if what != "v":
                            dma(q(), ropeC[:r, :], I["k_cos"][tab0 + t * 128:tab0 + t * 128 + r, :], writes=['ropeC'])
                            dma(q(), ropeS[:r, :], I["k_sin"][tab0 + t * 128:tab0 + t * 128 + r, :], writes=['ropeS'])
                        ps, pk = proj_tm(w, wk, ncols, t, r)
                        x_, xk_ = rp.next()
                        op('act', lambda e, r=r, x_=x_, ps=ps, ncols=ncols: e.copy(out=x_[:r, :ncols], in_=ps[:r, :ncols]),
                           reads=[pk], writes=[xk_])
                        orow = slice(op0_ + t * 128, op0_ + t * 128 + r)
                        if what == "v":
                            dma(q(), O["dv" + sfx][orow, :], x_[:r, :], reads=[xk_])
                            v, vk = vb.next()
                            for pr in range(4):
                                op('dve' if pr % 2 else 'pool', lambda e, r=r, v=v, x_=x_, pr=pr: e.tensor_copy(
                                    out=v[:r, pr, :].rearrange("p (h d) -> p h d", d=65)[:, :, 0:64],
                                    in_=x_[:r, pr * 128:(pr + 1) * 128].rearrange("p (h d) -> p h d", d=64)),
                                   reads=[xk_], writes=[vk])
                            dma(q(), SC[f"VS1{sq}"][:, :r, pos0 // 128 + t, :].rearrange("a p d -> p a d"), v[:r, :, :],
                                reads=[vk], writes=[f"VS1{sq}"])
                            continue
                        y_, yk_ = rp.next()
                        if what == "q":
                            rope('dve', y_, yk_, x_, xk_, r, 512)
                            transposes_to(lambda j, ps, pk, t=t, r=r: (
                                op('act', lambda e: e.copy(out=QTb[sq][0:64, j, t * 128:t * 128 + r], in_=ps[0:64, :r]),
                                   reads=[pk], writes=['QT']),
                                op('act', lambda e: e.copy(out=QTb[sq][64:128, j, TB + t * 128:TB + t * 128 + r],
                                                           in_=ps[64:128, :r]), reads=[pk], writes=['QT'])), y_, yk_, r, 4, 128)
                        elif what == "k":
                            rope('pool', y_, yk_, x_, xk_, r, 512)
                            dma(q(), O["dk" + sfx][orow, :], y_[:r, :], reads=[yk_])
                            transposes_to(lambda j, ps, pk, t=t, r=r: op('dve', lambda e: e.tensor_copy(
                                out=KTn[:, j, t * 128:t * 128 + r], in_=ps[:, :r]), reads=[pk], writes=['KTn']), y_, yk_, r, 4, 128)
                        else:
                            rope('dve', y_, yk_, x_, xk_, r, 320)
                            dma(q(), O["di" + sfx][orow, :], y_[:r, 256:320], reads=[yk_])
                            transposes_to(lambda j, ps, pk, t=t, r=r: op('act', lambda e: e.copy(
                                out=IQT[:, j, t * 128:t * 128 + r], in_=ps[:64, :r]), reads=[pk], writes=['IQT']), y_, yk_, r, 4, 64)
                            ps2, pk2 = PS.next()
                            op('pe', lambda e, r=r, y_=y_, ps2=ps2: e.transpose(out=ps2[:64, :r], in_=y_[:r, 256:320],
                                                                                identity=ident[:r, :r]), reads=[yk_, 'ident'],
                               writes=[pk2])
                            op('dve', lambda e, t=t, r=r, ps2=ps2: e.tensor_copy(out=IKn[:, t * 128:t * 128 + r], in_=ps2[:64, :r]),
                               reads=[pk2], writes=['IKn'])
                            op('dve', lambda e, t=t, r=r, x_=x_: e.tensor_scalar(out=wq[:r, t, :], in0=x_[:r, 320:324], scalar1=0.5,
                                                                                 scalar2=None, op0=ALU.mult), reads=[xk_],
                               writes=['wq'])
                dma(q(), SC[f"KT1{sq}"][:, :, pos0:pos0 + TB].rearrange("a p n -> p a n"), KTn[:, :, :TB], reads=['KTn'],
                    writes=[f"KT1{sq}"])
                dma(q(), SC[f"IK{sq}"][:, pos0:pos0 + TB], IKn[:, :TB], reads=['IKn'], writes=[f"IK{sq}"])
                CP(13)
                nk = pos0 + TB
                nkb = (nk + 511) // 512
                for qt in range(nt):
                    r = rows_of(qt)
                    for ch in range((nk + KCH - 1) // KCH):
                        k0 = ch * KCH
                        n = min(KCH, nk - k0)
                        ib, ibk = kbuf.next()
                        dma(q(), ib[0:64, :n], SC[f"IK{sq}"][:, k0:k0 + n], reads=[f"IK{sq}"], writes=[ibk])
                        for kb_ in range((n + 511) // 512):
                            c0 = kb_ * 512
                            m = min(512, n - c0)
                            g0 = k0 + c0
                            diag = (sq == "P") and (g0 >= pos0)
                            pss = []
                            for hi in range(4):
                                ps, pk = PS.next()
                                op('pe', lambda e, hi=hi, r=r, qt=qt, ib=ib, c0=c0, m=m, ps=ps: e.matmul(
                                    ps[:r, :m], lhsT=IQT[:, hi, qt * 128:qt * 128 + r], rhs=ib[0:64, c0:c0 + m], start=True,
                                    stop=True), reads=['IQT', ibk], writes=[pk])
                                pss.append((ps, pk))
                            for hi in range(4):
                                ps, pk = pss[hi]
                                rl, rlk = tmr.next()
                                op('act', lambda e, r=r, m=m, rl=rl, ps=ps: e.activation(out=rl[:r, :m], in_=ps[:r, :m],
                                                                                         func=AF.Relu, scale=0.125),
                                   reads=[pk], writes=[rlk])
                                dst_ = scores[:r, g0:g0 + m]
                                if hi == 0 and diag:
                                    op('dve', lambda e, r=r, m=m, rl=rl, dst_=dst_, qt=qt: e.scalar_tensor_tensor(
                                        out=dst_, in0=rl[:r, :m], scalar=wq[:r, qt, 0:1], in1=admn[:r, qt, :m], op0=ALU.mult,
                                        op1=ALU.add), reads=[rlk, 'wq', 'admn', 'scores'], writes=['scores'])
                                elif hi == 0:
                                    op('dve', lambda e, r=r, m=m, rl=rl, dst_=dst_, qt=qt: e.tensor_scalar(
                                        out=dst_, in0=rl[:r, :m], scalar1=wq[:r, qt, 0:1], scalar2=None, op0=ALU.mult),
                                       reads=[rlk, 'wq', 'scores'], writes=['scores'])
                                else:
                                    op('dve', lambda e, r=r, m=m, rl=rl, dst_=dst_, qt=qt, hi=hi: e.scalar_tensor_tensor(
                                        out=dst_, in0=rl[:r, :m], scalar=wq[:r, qt, hi:hi + 1], in1=dst_, op0=ALU.mult,
                                        op1=ALU.add), reads=[rlk, 'wq', 'scores'], writes=['scores'])
                    hi_, lo_, w_, mid_, cnt_, ge_, mn_ = (bs[n_] for n_ in ["hi", "lo", "w", "mid", "cnt", "ge", "mn"])
                    B = lambda tl, c=0: tl[:r, c:c + 1]
                    op('dve', lambda e: e.tensor_reduce(out=B(hi_), in_=scores[:r, :nk], axis=AX.X, op=ALU.max),
                       reads=['scores'], writes=['bs_hi'])
                    if sq == "P":
                        nd = pos0
                        t1, t1k = tmr.next()
                        op('dve', lambda e, t1=t1: e.scalar_tensor_tensor(out=t1[:r, :TB], in0=admn[:r, qt, :TB], scalar=-2.0,
                                                                          in1=scores[:r, nd:nd + TB], op0=ALU.mult, op1=ALU.add),
                           reads=['scores', 'admn'], writes=[t1k])
                        op('dve', lambda e, t1=t1: e.tensor_reduce(out=B(lo_), in_=t1[:r, :TB], axis=AX.X, op=ALU.min),
                           reads=[t1k], writes=['bs_lo'])
                        if nd > 0:
                            op('dve', lambda e: e.tensor_reduce(out=B(mn_), in_=scores[:r, :nd], axis=AX.X, op=ALU.min),
                               reads=['scores'], writes=['bs_mn'])
                            op('dve', lambda e: e.tensor_tensor(out=B(lo_), in0=B(lo_), in1=B(mn_), op=ALU.min),
                               reads=['bs_lo', 'bs_mn'], writes=['bs_lo'])
                    else:
                        op('dve', lambda e: e.tensor_reduce(out=B(lo_), in_=scores[:r, :nk], axis=AX.X, op=ALU.min),
                           reads=['scores'], writes=['bs_lo'])
                    op('dve', lambda e: e.tensor_tensor(out=B(w_), in0=B(hi_), in1=B(lo_), op=ALU.subtract),
                       reads=['bs_hi', 'bs_lo'], writes=['bs_w'])
                    spl = nk if nk < 1024 else max(((nk * 9 // 20) // 512) * 512, nk - 4096, 512)
                    nA = nk - spl
                    npc = (spl + 2047) // 2048
                    sA_, nm_, wt_ = bs["sa"], bs["nm"], bs["wt"]
                    op('dve', lambda e: e.tensor_scalar(out=wt_[:r, 0:NIT], in0=pow2[:r, 0:NIT], scalar1=B(w_), scalar2=None,
                                                        op0=ALU.mult), reads=['bs_w', 'pow2'], writes=['bs_wt'])
                    for it in range(NIT):
                        wk = wt_[:r, it:it + 1]
                        op('dve', lambda e: e.tensor_tensor(out=B(mid_), in0=B(lo_), in1=wk, op=ALU.add),
                           reads=['bs_lo', 'bs_wt'], writes=['bs_mid'])
                        if nA > 0:
                            op('pool', lambda e: e.tensor_scalar(out=B(nm_), in0=B(mid_), scalar1=-1.0, scalar2=None, op0=ALU.mult),
                               reads=['bs_mid'], writes=['bs_nm'])
                            op('act', lambda e: e.activation(out=junkA[:r, :nA], in_=scores[:r, spl:nk], func=AF.Sign,
                                                             bias=B(nm_), scale=1.0, accum_out=B(sA_)),
                               reads=['scores', 'bs_nm', 'wwide0'], writes=['wwide0', 'bs_sa'])
                        for pc in range(npc):
                            c0 = pc * 2048
                            m = min(2048, spl - c0)
                            op('dve', lambda e, pc=pc, c0=c0, m=m: e.tensor_scalar(
                                out=notsel[:r, qt, c0:c0 + m], in0=scores[:r, c0:c0 + m], scalar1=B(mid_), scalar2=None,
                                op0=ALU.is_ge, op1=ALU.add, accum_out=B(cnt_, pc)), reads=['scores', 'bs_mid', 'notsel'],
                               writes=['notsel', 'bs_cnt'])
                        if work:
                            work.pop(0)()
                        if npc > 1:
                            op('dve', lambda e: e.tensor_reduce(out=B(ge_), in_=cnt_[:r, 0:npc], axis=AX.X, op=ALU.add),
                               reads=['bs_cnt'], writes=['bs_ge'])
                            cs = B(ge_)
                        else:
                            cs = B(cnt_)
                        thr = 255.5
                        if nA > 0:
                            op('dve', lambda e, cs=cs: e.scalar_tensor_tensor(out=B(ge_), in0=B(sA_), scalar=0.5, in1=cs,
                                                                              op0=ALU.mult, op1=ALU.add),
                               reads=['bs_sa', 'bs_cnt', 'bs_ge'], writes=['bs_ge'])
                            cs = B(ge_)
                            thr = 255.5 - nA / 2.0
                        op('dve', lambda e, cs=cs: e.tensor_scalar(out=B(ge_), in0=cs, scalar1=thr, scalar2=wk, op0=ALU.is_ge,
                                                                   op1=ALU.mult), reads=['bs_cnt', 'bs_ge', 'bs_wt'],
                           writes=['bs_ge'])
                        op('dve', lambda e: e.tensor_tensor(out=B(lo_), in0=B(lo_), in1=B(ge_), op=ALU.add),
                           reads=['bs_ge', 'bs_lo'], writes=['bs_lo'])
                    npc = (nk + 2047) // 2048
                    for pc in range(npc):
                        c0 = pc * 2048
                        m = min(2048, nk - c0)
                        op('dve' if pc % 2 == 0 else 'pool', lambda e, c0=c0, m=m: e.tensor_scalar(
                            out=notsel[:r, qt, c0:c0 + m], in0=scores[:r, c0:c0 + m], scalar1=B(lo_), scalar2=None,
                            op0=ALU.is_lt), reads=['scores', 'bs_lo'], writes=['notsel'])
                while work:
                    work.pop(0)()
                CP(14)
                attend(1, sq, TB, pos0, False, nt)
                CP(15)
                out_proj_ln("o_oa", "o_oo", R_OG, R_OB, TB)
                CP(16)
                for t in range(nt):
                    r = rows_of(t)
                    for half in range(2):
                        ps, pk = PS.next()
                        for j in range(4):
                            kc = half * 4 + j
                            op('pe', lambda e, kc=kc, j=j, t=t, r=r, ps=ps: e.transpose(
                                out=ps[:r, j * 128:(j + 1) * 128], in_=xT_f[:, kc, t * 128:t * 128 + r], identity=ident[:, :]),
                               reads=['xT_f', 'ident'], writes=[pk])
                        op('act' if half else 'dve',
                           (lambda e, t=t, r=r, ps=ps, half=half: e.copy(out=xtk(t, half * 512, (half + 1) * 512)[:r, :], in_=ps[:r, :]))
                           if half else
                           (lambda e, t=t, r=r, ps=ps, half=half: e.tensor_copy(out=xtk(t, half * 512, (half + 1) * 512)[:r, :],
                                                                                in_=ps[:r, :])),
                           reads=[pk], writes=['zT'])
                    dma(q(), O["y" + sfx][op0_ + t * 128:op0_ + t * 128 + r, :], xtk(t, 0, 1024)[:r, :], reads=['zT'])

            def stage_sample():
                npt = PAST // 128
                t1, t1k = tmr.next()
                dma(q(), t1[:30, :], I["c_conv"], writes=[t1k])
                for c in range(4):
                    ps, pk = PS.next()
                    op('pe', lambda e, c=c, ps=ps, t1=t1: e.transpose(out=ps[:, 0:30], in_=t1[:30, c * 128:(c + 1) * 128],
                                                                      identity=ident[:30, :30]), reads=[t1k, 'ident'], writes=[pk])
                    op('act', lambda e, c=c, ps=ps: e.copy(out=a_bf["S"][:, c, 0:30], in_=ps[:, 0:30]), reads=[pk],
                       writes=['a_bfS'])
                t2, t2k = tmr.next()
                dma(q(), t2[:15, :], I["c_pool"], writes=[t2k])
                for c in range(4):
                    ps, pk = PS.next()
                    op('pe', lambda e, c=c, ps=ps, t2=t2: e.transpose(out=ps[:, 0:15], in_=t2[:15, c * 128:(c + 1) * 128],
                                                                      identity=ident[:15, :15]), reads=[t2k, 'ident'], writes=[pk])
                    op('act', lambda e, c=c, ps=ps: e.copy(out=cv_f["S"][:, c, 0:15], in_=ps[:, 0:15]), reads=[pk],
                       writes=['cv_fS'])
                dma(q(), lf[:], I["c_ff"].rearrange("(t p) h -> p t h", p=128), writes=['lf_s'])
                for g4 in range(4):
                    ps, pk = PS.next()
                    for j in range(4):
                        t = g4 * 4 + j
                        op('pe', lambda e, t=t, j=j, ps=ps: e.transpose(out=ps[:8, j * 128:(j + 1) * 128], in_=lf[:, t, :],
                                                                        identity=ident[:, :]), reads=['lf_s', 'ident'], writes=[pk])
                    op('act', lambda e, ps=ps: e.activation(out=Lrow[:8, :], in_=ps[:8, :], func=AF.Copy, scale=-1.0), reads=[pk],
                       writes=['Lrow'])
                    op('dve', lambda e: e.tensor_tensor_scan(out=CKrow[:8, :], data0=onesrow[:8, :], data1=Lrow[:8, :],
                                                             initial=ckcar["S"][:8, 0:1], op0=ALU.mult, op1=ALU.add),
                       reads=['Lrow', 'onesrow', 'ckcarS'], writes=['CKrow'])
                    op('dve', lambda e: e.tensor_copy(out=ckcar["S"][:8, 0:1], in_=CKrow[:8, 511:512]), reads=['CKrow'],
                       writes=['ckcarS'])
                    for j in range(4):
                        t = g4 * 4 + j
                        ps2, pk2 = PS.next()
                        op('pe', lambda e, j=j, ps2=ps2: e.transpose(out=ps2[:, 0:8], in_=CKrow[:8, j * 128:(j + 1) * 128],
                                                                     identity=ident[:8, :8]), reads=['CKrow', 'ident'], writes=[pk2])
                        op('dve', lambda e, t=t, ps2=ps2: e.tensor_copy(out=CKT["S"][:, t, :], in_=ps2[:, 0:8]), reads=[pk2],
                           writes=['CKTS'])
                for (src, l) in [("c_fk", 0), ("c_dk", 1)]:
                    for t in range(npt):
                        x_, xk_ = rp.next()
                        dma(q(), x_[:, :], I[src][t * 128:(t + 1) * 128, :], writes=[xk_])
                        ps, pk = PS.next()
                        for j in range(4):
                            op('pe', lambda e, j=j, ps=ps, x_=x_: e.transpose(out=ps[:, j * 128:(j + 1) * 128],
                                                                              in_=x_[:, j * 128:(j + 1) * 128], identity=ident[:, :]),
                               reads=[xk_, 'ident'], writes=[pk])
                        kb, kbk = tmb.next()
                        op('act' if t % 2 else 'dve',
                           (lambda e, kb=kb, ps=ps: e.copy(out=kb[:, :], in_=ps[:, :])) if t % 2 else
                           (lambda e, kb=kb, ps=ps: e.tensor_copy(out=kb[:, :], in_=ps[:, :])), reads=[pk], writes=[kbk])
                        dma(q(), SC[f"KT{l}S"][:, :, t * 128:(t + 1) * 128].rearrange("a p n -> p a n"),
                            kb[:, :].rearrange("p (a n) -> p a n", n=128), reads=[kbk], writes=[f"KT{l}S"])
                for (src, l) in [("c_fv", 0), ("c_dv", 1)]:
                    for t in range(npt):
                        x_, xk_ = rp.next()
                        dma(q(), x_[:, :], I[src][t * 128:(t + 1) * 128, :], writes=[xk_])
                        v, vk = vb.next()
                        for pr in range(4):
                            op('dve' if pr % 2 else 'pool', lambda e, v=v, x_=x_, pr=pr: e.tensor_copy(
                                out=v[:, pr, :].rearrange("p (h d) -> p h d", d=65)[:, :, 0:64],
                                in_=x_[:, pr * 128:(pr + 1) * 128].rearrange("p (h d) -> p h d", d=64)), reads=[xk_], writes=[vk])
                        dma(q(), SC[f"VS{l}S"][:, :, t, :].rearrange("a p d -> p a d"), v[:, :, :], reads=[vk], writes=[f"VS{l}S"])
                for t in range(npt):
                    x_, xk_ = rp.next()
                    dma(q(), x_[:, 0:64], I["c_di"][t * 128:(t + 1) * 128, :], writes=[xk_])
                    ps, pk = PS.next()
                    op('pe', lambda e, ps=ps, x_=x_: e.transpose(out=ps[:64, 0:128], in_=x_[:, 0:64], identity=ident[:, :]),
                       reads=[xk_, 'ident'], writes=[pk])
                    kb, kbk = tmb.next()
                    op('act', lambda e, kb=kb, ps=ps: e.copy(out=kb[:64, 0:128], in_=ps[:64, 0:128]), reads=[pk], writes=[kbk])
                    dma(q(), SC["IKS"][:, t * 128:(t + 1) * 128], kb[:64, 0:128], reads=[kbk], writes=["IKS"])

            def x_load(x_ap, TB):
                for t in range((TB + 127) // 128):
                    r = min(128, TB - t * 128)
                    dma(q(), xtokb[:r, t, :], x_ap[t * 128:t * 128 + r, :], writes=['xtokb'])

            try:
              x_load(I["xp"][0:TBP, :], TBP)
              for blk in range(nblk):
                fin = (blk == nblk - 1)
                layer0("P", I["xp"][blk * TBP:(blk + 1) * TBP, :], TBP, blk * TBP, None, fin)
                CP(10)
                if blk + 1 < nblk:
                    x_load(I["xp"][(blk + 1) * TBP:(blk + 2) * TBP, :], TBP)
                elif do_sample:
                    x_load(I["xs"], TS)
                layer1("P", TBP, blk * TBP, fin)
              if do_sample:
                stage_sample()
                CP(20)
                layer0("S", I["xs"], TS, PAST, None, True)
                CP(21)
                layer1("S", TS, PAST, True)
            except _Stop:
                pass
        try:
            body()
        except _Stop:
            pass
        S.finish()
        build.ninst = S.ninst
    return nc


def _consts(SEQ):
    kk = np.arange(128)[:, None]
    qq = np.arange(TBP)[None, :]
    cmask = np.stack([((128 * a + kk) > qq).astype(np.float32) for a in range(NTP)])
    q2 = np.arange(128)[:, None]
    k2 = np.arange(TBP)[None, :]
    adm = np.stack([np.where((k2 // 64) > ((128 * a + q2) // 64), -1e30, 0.0).astype(np.float32) for a in range(NTP)])
    pos = np.concatenate([np.arange(SEQ), PAST + np.arange(TS)]).astype(np.float32)
    inv = (10000.0 ** (-np.arange(0, 64, 2, dtype=np.float32) / 64)).astype(np.float32)
    ang = pos[:, None] * inv[None, :]
    cos = np.tile(np.cos(ang).astype(np.float32), (1, 8))
    sin = np.tile(np.sin(ang).astype(np.float32), (1, 8))
    fix = np.zeros((128, 4, 16), np.float32)
    for g, w in enumerate((2, 4, 8, 16)):
        fix[:, g, :] = 1.0 / np.minimum(np.arange(16) + 1, w)
    return {"k_cmask": cmask, "k_adm": adm, "k_cos": cos, "k_sin": sin, "k_poolfix": fix}


_NC_CACHE = {}


def kernel(x_prompt, x_sample, cache_conv, cache_fox_k, cache_fox_v, cache_fox_logf, cache_pool, cache_dsa_k,
           cache_dsa_v, cache_dsa_idx_k, e_w_in, e_b_f, e_conv_w, e_conv_b, e_conv_ln_g, e_conv_ln_b, e_w_out, e_ln_g,
           e_ln_b, o_w_in, o_pool_w, o_pool_scale, o_w_out, o_ln_g, o_ln_b, _nblk=SEQ // TBP, _sample=True):
    f = lambda a: np.ascontiguousarray(np.asarray(a, dtype=np.float32))
    if (_nblk, _sample) not in _NC_CACHE:
        _NC_CACHE[(_nblk, _sample)] = build(_nblk, _sample)
    nc = _NC_CACHE[(_nblk, _sample)]
    pvec = np.zeros((40, 1024), np.float32)
    pvec[0:31, 0:512] = f(e_conv_w)[0]
    pvec[31, 0:512] = f(e_conv_b)[0]
    pvec[32, 0:512] = f(e_conv_ln_g)[0]
    pvec[33, 0:512] = f(e_conv_ln_b)[0]
    pvec[34] = f(e_ln_g)[0]
    pvec[35] = f(e_ln_b)[0]
    pvec[36, 0:512] = f(o_pool_scale)[0]
    pvec[37] = f(o_ln_g)[0]
    pvec[38] = f(o_ln_b)[0]
    SEQE = _nblk * TBP
    cst = _consts(SEQE)
    shared = {"e_w_in": f(e_w_in)[0], "e_b_f": f(e_b_f)[0].reshape(8, 1), "e_w_out": f(e_w_out)[0],
              "o_w_in": f(o_w_in)[0], "o_pool_w": f(o_pool_w)[0], "o_w_out": f(o_w_out)[0], "pvec": pvec}
    shared.update(cst)
    in_maps = []
    for c in range(8):
        m = dict(shared)
        m["xp"] = f(x_prompt)[c // 4][:SEQE]
        m["xs"] = f(x_sample)[c]
        m["c_conv"] = f(cache_conv)[0, c]
        m["c_fk"] = f(cache_fox_k)[0, c].reshape(PAST, 512)
        m["c_fv"] = f(cache_fox_v)[0, c].reshape(PAST, 512)
        m["c_ff"] = f(cache_fox_logf)[0, c]
        m["c_pool"] = f(cache_pool)[0, c]
        m["c_dk"] = f(cache_dsa_k)[0, c].reshape(PAST, 512)
        m["c_dv"] = f(cache_dsa_v)[0, c].reshape(PAST, 512)
        m["c_di"] = f(cache_dsa_idx_k)[0, c]
        in_maps.append(m)
    res = run_bass_kernel_spmd(nc, in_maps, core_ids=list(range(8)))
    R = res.results
    P = lambda n: np.stack([R[0][n], R[4][n]])
    Sm = lambda n: np.stack([R[c][n] for c in range(8)])
    out = (P("y_p"), Sm("y_s"),
           P("conv_p")[None], Sm("conv_s")[None],
           P("fk_p").reshape(1, 2, SEQE, 8, 64), Sm("fk_s").reshape(1, 8, TS, 8, 64),
           P("fv_p").reshape(1, 2, SEQE, 8, 64), Sm("fv_s").reshape(1, 8, TS, 8, 64),
           P("ff_p")[None], Sm("ff_s")[None],
           P("pool_p")[None], Sm("pool_s")[None],
           P("dk_p").reshape(1, 2, SEQE, 8, 64), Sm("dk_s").reshape(1, 8, TS, 8, 64),
           P("dv_p").reshape(1, 2, SEQE, 8, 64), Sm("dv_s").reshape(1, 8, TS, 8, 64),
           P("di_p")[None], Sm("di_s")[None])
    return tuple(np.ascontiguousarray(o, dtype=np.float32) for o in out)
```

```python
import numpy as np
import concourse.bass as bass
import concourse.mybir as mybir
from concourse.bass_utils import run_bass_kernel_spmd
from contextlib import ExitStack

F32 = mybir.dt.float32
BF16 = mybir.dt.bfloat16
AF = mybir.ActivationFunctionType
ALU = mybir.AluOpType
AX = mybir.AxisListType

SEQ = 8192
PAST = 2048
TS = 32
ALPHA = 4.0 ** 0.25
NIT = 12
TBP = 256
NTP = TBP // 128
KCH = 1024
NEGM = -30000.0
PE_SKIP = True


class _Rec:
    def __getattr__(self, name):
        def f(*a, **k):
            self.call = (name, a, k)
            return self
        return f


class Sched:
    SEM_LIMIT = 2000
    NDMA = 24

    def __init__(self, nc, es):
        self.nc = nc
        self.es = es
        self.names = ['pe', 'act', 'dve', 'pool', 'sp']
        self.prog = {e: [] for e in self.names}
        self.cnt = {e: 0 for e in self.names}
        self.waited = {e: {} for e in self.names}
        self.bufs = {}
        self.pe_force = False
        self.pe_mode = None
        self.dsem = []
        for i in range(self.NDMA):
            s = es.enter_context(nc.semaphore(f"dq{i}"))
            self.dsem.append([s, 0])
        self.dnext = 0
        self.ninst = 0

    def _deps(self, reads, writes):
        toks = []
        for k in reads:
            b = self.bufs.get(k)
            if b and b[0] is not None:
                toks.append(b[0])
            if b and k.startswith('pb'):
                toks.extend(b[1])
        for k in writes:
            b = self.bufs.get(k)
            if b:
                if b[0] is not None:
                    toks.append(b[0])
                toks.extend(b[1])
        return toks

    @staticmethod
    def _key(tok):
        return ('E', tok[1]) if tok[0] == 'E' else ('D', id(tok[1]))

    def _emit_waits(self, e, toks):
        need = {}
        for tok in toks:
            k = self._key(tok)
            if PE_SKIP and e == 'pe' and tok[0] == 'E' and tok[1] == 'pe' and not self.pe_force:
                continue
            if self.waited[e].get(k, 0) >= tok[2]:
                continue
            if k not in need or need[k][2] < tok[2]:
                need[k] = tok
        for k, tok in need.items():
            self.waited[e][k] = tok[2]
            self.prog[e].append(('w', tok))

    def _compact(self, toks):
        best = {}
        for tok in toks:
            k = self._key(tok)
            if k not in best or best[k][2] < tok[2]:
                best[k] = tok
        return list(best.values())

    def _update(self, tok, reads, writes):
        for k in reads:
            b = self.bufs.setdefault(k, [None, []])
            b[1].append(tok)
            if len(b[1]) > 12:
                b[1] = self._compact(b[1])
        for k in writes:
            self.bufs[k] = [tok, []]

    def op(self, e, fn, reads=(), writes=()):
        self._emit_waits(e, self._deps(reads, writes))
        rec = _Rec()
        fn(rec)
        name, a, k = rec.call
        if e == 'pe':
            st_ = k.get('lhsT', k.get('in_'))
            r32 = lambda n: 32 if n <= 32 else (64 if n <= 64 else 128)
            fr = 1
            for d in st_.shape[1:]:
                fr *= d
            mode = (name, r32(st_.shape[0]), r32(fr), str(st_.dtype))
            if mode != self.pe_mode and self.cnt['pe'] > 0:
                self.pe_force = True
                self._emit_waits('pe', [('E', 'pe', self.cnt['pe'])])
                self.pe_force = False
            self.pe_mode = mode
        self.cnt[e] += 1
        idx = self.cnt[e]
        self.prog[e].append(('op', name, a, k, idx))
        self._update(('E', e, idx), reads, writes)
        self.ninst += 1

    def dma(self, q, out, in_, reads=(), writes=()):
        slot = self.dsem[self.dnext]
        self.dnext = (self.dnext + 1) % self.NDMA
        s = slot[0]
        toks = self._deps(reads, writes)
        if slot[1] > 0:
            toks.append(('D', s, slot[1]))
        self._emit_waits(q, toks)
        slot[1] += 16
        self.prog[q].append(('dma', out, in_, s))
        self._update(('D', s, slot[1]), reads, writes)
        self.ninst += 1

    def _all_toks(self):
        toks = [('D', s, v) for (s, v) in self.dsem if v > 0]
        for e in ['pe', 'act', 'dve', 'pool']:
            if self.cnt[e] > 0:
                toks.append(('E', e, self.cnt[e]))
        return toks

    def barrier(self):
        toks = self._all_toks()
        for e in self.names:
            self.pe_force = True
            self._emit_waits(e, toks)
            self.pe_force = False

    def finish(self):
        self._emit_waits('sp', self._all_toks())
        nc = self.nc
        need = {e: set() for e in self.names}
        for e in self.names:
            for ent in self.prog[e]:
                if ent[0] == 'w' and ent[1][0] == 'E':
                    need[ent[1][1]].add(ent[1][2])
        sig = {}
        sems = {}
        self.nsem = 0
        for p in self.names:
            for r, idx in enumerate(sorted(need[p])):
                ep = r // self.SEM_LIMIT
                if (p, ep) not in sems:
                    sems[(p, ep)] = self.es.enter_context(nc.semaphore(f"s{p}{ep}"))
                    self.nsem += 1
                sig[(p, idx)] = (sems[(p, ep)], r % self.SEM_LIMIT + 1)

        def replay(e, eng):
            for ent in self.prog[e]:
                if ent[0] == 'w':
                    tok = ent[1]
                    if tok[0] == 'E':
                        sm, v = sig[(tok[1], tok[2])]
                        eng.wait_ge(sm, v)
                    else:
                        eng.wait_ge(tok[1], tok[2])
                elif ent[0] == 'op':
                    _, name, a, k, idx = ent
                    ins = getattr(eng, name)(*a, **k)
                    if (e, idx) in sig:
                        ins.then_inc(sig[(e, idx)][0], 1)
                else:
                    _, out, in_, s = ent
                    eng.dma_start(out=out, in_=in_).then_inc(s, 16)

        with nc.Block() as block:
            @block.tensor
            def _(eng):
                replay('pe', eng)

            @block.scalar
            def _(eng):
                replay('act', eng)

            @block.vector
            def _(eng):
                replay('dve', eng)

            @block.gpsimd
            def _(eng):
                replay('pool', eng)

            @block.sync
            def _(eng):
                replay('sp', eng)


class _Stop(Exception):
    pass


STOP = [0]


def CP(n):
    if STOP[0] == n:
        raise _Stop()


class Rot:
    def __init__(self, tiles, name):
        self.tiles = tiles
        self.name = name
        self.i = 0

    def next(self):
        j = self.i % len(self.tiles)
        self.i += 1
        return self.tiles[j], f"{self.name}{j}"


E_IN = 3592
O_IN = 3396
W_SPECS = None


def build(nblk=SEQ // TBP, do_sample=True):
    nc = bass.Bass("TRN2", target_bir_lowering=False)
    SEQ = nblk * TBP
    din = lambda n, s: nc.dram_tensor(n, list(s), F32, kind="ExternalInput").ap()
    dout = lambda n, s: nc.dram_tensor(n, list(s), F32, kind="ExternalOutput").ap()
    dscr = lambda n, s, dt: nc.dram_tensor(n, list(s), dt).ap()

    I = {}
    for n, s in [("xp", (SEQ, 1024)), ("xs", (TS, 1024)), ("c_conv", (30, 512)), ("c_fk", (PAST, 512)),
                 ("c_fv", (PAST, 512)), ("c_ff", (PAST, 8)), ("c_pool", (15, 512)), ("c_dk", (PAST, 512)),
                 ("c_dv", (PAST, 512)), ("c_di", (PAST, 64)),
                 ("e_w_in", (1024, E_IN)), ("e_b_f", (8, 1)), ("e_w_out", (1024, 1024)),
                 ("o_w_in", (1024, O_IN)), ("o_pool_w", (4, 128, 128)), ("o_w_out", (1024, 1024)),
                 ("pvec", (40, 1024)),
                 ("k_cmask", (NTP, 128, TBP)), ("k_adm", (NTP, 128, TBP)), ("k_cos", (SEQ + TS, 256)),
                 ("k_sin", (SEQ + TS, 256)), ("k_poolfix", (128, 4, 16))]:
        I[n] = din(n, s)
    O = {}
    for n, s in [("y_p", (SEQ, 1024)), ("y_s", (TS, 1024)), ("conv_p", (30, 512)), ("conv_s", (30, 512)),
                 ("fk_p", (SEQ, 512)), ("fk_s", (TS, 512)), ("fv_p", (SEQ, 512)), ("fv_s", (TS, 512)),
                 ("ff_p", (SEQ, 8)), ("ff_s", (TS, 8)), ("pool_p", (15, 512)), ("pool_s", (15, 512)),
                 ("dk_p", (SEQ, 512)), ("dk_s", (TS, 512)), ("dv_p", (SEQ, 512)), ("dv_s", (TS, 512)),
                 ("di_p", (SEQ, 64)), ("di_s", (TS, 64))]:
        O[n] = dout(n, s)

    NKS = PAST + TS
    SC = {}
    for sq, nk in [("P", SEQ), ("S", NKS)]:
        nkt = (nk + 127) // 128
        for l in (0, 1):
            SC[f"KT{l}{sq}"] = dscr(f"KT{l}{sq}", (4, 128, nk), BF16)
            SC[f"VS{l}{sq}"] = dscr(f"VS{l}{sq}", (4, 128, nkt, 130), BF16)
        SC[f"IK{sq}"] = dscr(f"IK{sq}", (64, nk), BF16)
    WS = {"e_in": dscr("ws_e_in", (29, 128, 8, 128), BF16), "o_in": dscr("ws_o_in", (27, 128, 8, 128), BF16),
          "e_oa": dscr("ws_e_oa", (8, 128, 4, 128), BF16), "e_oo": dscr("ws_e_oo", (8, 64, 8, 128), BF16),
          "o_oa": dscr("ws_o_oa", (8, 128, 4, 128), BF16), "o_oo": dscr("ws_o_oo", (8, 64, 8, 128), BF16),
          "cd": dscr("ws_cd", (4, 128, 31, 128), BF16)}

    with ExitStack() as es:
        S = Sched(nc, es)
        T = lambda name, shape, dt=F32: es.enter_context(nc.sbuf_tensor(name, list(shape), dt))
        banks = [es.enter_context(nc.psum_tensor(f"pb{i}", [128, 512], F32)) for i in range(8)]
        PS = Rot(banks[:6], "pb")
        PO = [(banks[6], "pb6"), (banks[7], "pb7")]
        op = S.op
        dma = S.dma
        dq = ['sp', 'pool']
        dqi = [0]

        def q():
            dqi[0] += 1
            return 'sp'

        def body():
            scores = T("scores", [128, 8192])
            notsel = T("notsel", [128, NTP, 8192], BF16)
            ident = T("ident", [128, 128])
            negI = T("negI", [128, 128], BF16)
            negI2 = T("negI2", [128, 2, 128], BF16)
            ones_f = T("ones_f", [128, 128])
            onesD = {512: T("ones512", [128, 128], BF16), 1024: T("ones1024", [128, 128], BF16)}
            onesrow = T("onesrow", [8, 512])
            op('pool', lambda e: e.memset(ident[:], 1.0), writes=['ident'])
            op('pool', lambda e: e.affine_select(out=ident[:], in_=ident[:], pattern=[[-1, 128]], compare_op=ALU.is_equal,
                                                 fill=0.0, base=0, channel_multiplier=1), reads=['ident'], writes=['ident'])
            op('dve', lambda e: e.tensor_scalar(out=negI[:], in0=ident[:], scalar1=NEGM, scalar2=None, op0=ALU.mult),
               reads=['ident'], writes=['negI'])
            for i2 in range(2):
                op('dve', lambda e: e.tensor_scalar(out=negI2[:, i2, :], in0=ident[:], scalar1=NEGM, scalar2=None,
                                                    op0=ALU.mult), reads=['ident'], writes=['negI'])
            op('pool', lambda e: e.memset(ones_f[:], 1.0), writes=['ones_f'])
            op('pool', lambda e: e.memset(onesD[512][:], 1.0 / 512), writes=['onesD'])
            op('pool', lambda e: e.memset(onesD[1024][:], 1.0 / 1024), writes=['onesD'])
            op('pool', lambda e: e.memset(onesrow[:], 1.0), writes=['onesrow'])

            cmask_f = scores[:, 4096:4096 + NTP * TBP].rearrange("p (a q) -> p a q", q=TBP)
            cmask = T("cmask", [128, NTP, TBP], BF16)
            cmask2 = T("cmask2", [128, NTP, 2, TBP], BF16)
            dma('sp', cmask_f, I["k_cmask"].rearrange("a p q -> p a q"), writes=['cmask_f'])
            op('dve', lambda e: e.tensor_copy(out=cmask[:], in_=cmask_f), reads=['cmask_f'], writes=['cmask'])
            for i2 in range(2):
                op('dve', lambda e: e.tensor_copy(out=cmask2[:, :, i2, :], in_=cmask_f), reads=['cmask_f'], writes=['cmask'])
            admn = T("admn", [128, NTP, TBP])
            dma('sp', admn[:], I["k_adm"].rearrange("a p q -> p a q"), writes=['admn'])
            poolfix = T("poolfix", [128, 4, 16])
            dma('sp', poolfix[:], I["k_poolfix"], writes=['poolfix'])
            bf_col = T("bf_col", [8, 1])
            dma('sp', bf_col[:], I["e_b_f"], writes=['bf_col'])
            nbf_col = T("nbf_col", [8, 1])
            op('dve', lambda e: e.tensor_scalar(out=nbf_col[:], in0=bf_col[:], scalar1=-1.0, scalar2=None, op0=ALU.mult),
               reads=['bf_col'], writes=['nbf_col'])
            pvt = scores[:, 2048:3072]
            pcol = T("pcol", [128, 8, 40])
            dma('sp', pvt[0:40, :], I["pvec"], writes=['pvt'])
            for c in range(8):
                ps, pk = PS.next()
                op('pe', lambda e, ps=ps, c=c: e.transpose(out=ps[:, 0:40], in_=pvt[0:40, c * 128:(c + 1) * 128],
                                                            identity=ident[0:40, 0:40]), reads=['pvt', 'ident'], writes=[pk])
                op('act', lambda e, ps=ps, c=c: e.copy(out=pcol[:, c, :], in_=ps[:, 0:40]), reads=[pk], writes=['pcol'])
            R_CB, R_CG, R_CBB, R_EG, R_EB, R_PS, R_OG, R_OB = 31, 32, 33, 34, 35, 36, 37, 38
            poolw_f = scores[:, 3072:3584].rearrange("p (g d) -> p g d", d=128)
            poolw = T("poolw", [128, 4, 128], BF16)
            dma('sp', poolw_f, I["o_pool_w"].rearrange("g c d -> c g d"), writes=['poolw_f'])
            op('dve', lambda e: e.tensor_copy(out=poolw[:], in_=poolw_f), reads=['poolw_f'], writes=['poolw'])

            CP(1)
            v8 = lambda ap_: ap_.rearrange("p (k c) -> p k c", c=128)
            stg_f = Rot([v8(scores[:, i * 1024:(i + 1) * 1024]) for i in range(2)], "stgf")
            stg_b = Rot([v8(notsel[:, 0, i * 1024:(i + 1) * 1024]) for i in range(2)], "stgb")
            ceng = ['dve', 'act', 'pool']
            cei = [0]

            def cast(out, in_, reads, writes):
                e = ceng[cei[0] % 3]
                cei[0] += 1
                if e == 'act':
                    op('act', lambda g: g.copy(out=out, in_=in_), reads=reads, writes=writes)
                else:
                    op(e, lambda g: g.tensor_copy(out=out, in_=in_), reads=reads, writes=writes)

            for wn, key, NC in [("e_w_in", "e_in", E_IN), ("o_w_in", "o_in", O_IN)]:
                for cc in range((NC + 127) // 128):
                    ncol = min(128, NC - cc * 128)
                    sf, sfk = stg_f.next()
                    sb, sbk = stg_b.next()
                    if ncol < 128:
                        op('pool', lambda e: e.memset(sf, 0.0), writes=[sfk])
                    dma('sp', sf[:, :, :ncol], I[wn][:, cc * 128:cc * 128 + ncol].rearrange("(k p) c -> p k c", p=128),
                        writes=[sfk])
                    cast(sb, sf, [sfk], [sbk])
                    dma('sp', WS[key][cc], sb, reads=[sbk], writes=['ws_' + key])
            for wn, ka, ko in [("e_w_out", "e_oa", "e_oo"), ("o_w_out", "o_oa", "o_oo")]:
                for cc in range(8):
                    sf, sfk = stg_f.next()
                    sb, sbk = stg_b.next()
                    dma('sp', sf[:, 0:4, :], I[wn][0:512, cc * 128:(cc + 1) * 128].rearrange("(k p) c -> p k c", p=128),
                        writes=[sfk])
                    cast(sb[:, 0:4, :], sf[:, 0:4, :], [sfk], [sbk])
                    dma('sp', WS[ka][cc], sb[:, 0:4, :], reads=[sbk], writes=['ws_' + ka])
                    sf, sfk = stg_f.next()
                    sb, sbk = stg_b.next()
                    dma('sp', sf[0:64, :, :], I[wn][512:1024, cc * 128:(cc + 1) * 128].rearrange("(h p) c -> p h c", p=64),
                        writes=[sfk])
                    cast(sb[0:64, :, :], sf[0:64, :, :], [sfk], [sbk])
                    dma('sp', WS[ko][cc], sb[0:64, :, :], reads=[sbk], writes=['ws_' + ko])
            cdst = notsel[:, 1, 0:31 * 128].rearrange("p (w c) -> p w c", c=128)
            for c in range(4):
                for w in range(31):
                    eng = 'dve' if w % 2 == 0 else 'pool'
                    op(eng, lambda e, c=c, w=w: e.tensor_scalar(out=cdst[:, w, :], in0=ident[:], scalar1=pcol[:, c, w:w + 1],
                                                                scalar2=None, op0=ALU.mult),
                       reads=['ident', 'pcol'], writes=['cdst'])
                dma('sp', WS["cd"][c], cdst, reads=['cdst'], writes=['ws_cd'])

            CP(2)
            S.barrier()
            xT_f = T("xT_f", [128, 8, TBP])
            xT_b = T("xT_b", [128, 8, TBP], BF16)
            zT = T("zT", [128, 8, TBP])
            zflat = zT[:].rearrange("p c t -> p (c t)")
            xtk = lambda t, a, b: zflat[:, t * 1024 + a:t * 1024 + b]
            xtokb = T("xtokb", [128, NTP, 1024])
            wnar = Rot([T(f"wnar{i}", [128, 8, 128], BF16) for i in range(4)], "wnar")
            wwide = Rot([T(f"wwide{i}", [128, 8, 512], BF16) for i in range(2)], "wwide")
            cdw = Rot([T(f"cdw{i}", [128, 31, 128], BF16) for i in range(1)], "cdw")
            a_bf = {sq: T(f"a_bf{sq}", [128, 4, 30 + (TBP if sq == "P" else TS)], BF16) for sq in "PS"}
            cv_f = {sq: T(f"cv_f{sq}", [128, 4, 15 + (TBP if sq == "P" else TS)]) for sq in "PS"}
            sg = T("sg", [128, TBP])
            gA = T("gA", [128, 4, TBP], BF16)
            gB = T("gB", [64, 8, TBP], BF16)
            QTb = {"P": T("QTbP", [128, 4, 2 * TBP], BF16), "S": T("QTbS", [128, 4, 2 * TS], BF16)}
            KTn = T("KTn", [128, 4, TBP], BF16)
            uA = T("uA", [128, 4, TBP], BF16)
            uO = T("uO", [64, 8, TBP], BF16)
            Lrow = T("Lrow", [8, 512])
            CKrow = T("CKrow", [8, 512])
            ckcar = {sq: T(f"ckcar{sq}", [8, 1]) for sq in "PS"}
            CKT = {sq: T(f"CKT{sq}", [128, 64 if sq == "P" else 17, 8]) for sq in "PS"}
            biasT = T("biasT", [128, 64, 8])
            cref = T("cref", [128, 8])
            tmr = Rot([T(f"tmr{i}", [128, 512]) for i in range(3)], "tmr")
            tmb = Rot([T(f"tmb{i}", [128, 512], BF16) for i in range(3)], "tmb")
            vb = Rot([T(f"vb{i}", [128, 4, 130], BF16) for i in range(2)], "vb")
            kbuf = Rot([T(f"kbuf{i}", [128, KCH], BF16) for i in range(2)], "kbuf")
            vbuf = Rot([T(f"vbuf{i}", [128, KCH // 128, 130], BF16) for i in range(2)], "vbuf")
            osb = Rot([T(f"osb{i}", [65, TBP]) for i in range(2)], "osb")
            stat = {n: T("st_" + n, [128, TBP]) for n in ["mean", "m2", "rstd"]}
            ropeC = T("ropeC", [128, 256])
            ropeS = T("ropeS", [128, 256])
            rp = Rot([T(f"rp{i}", [128, 512]) for i in range(2)], "rp")
            rq = Rot([T(f"rq{i}", [128, 256]) for i in range(2)], "rq")
            IQT = T("IQT", [64, 4, TBP], BF16)
            IKn = T("IKn", [64, TBP], BF16)
            wq = T("wq", [128, NTP, 4])
            bs = {n: T("bs_" + n, [128, 8]) for n in ["hi", "lo", "w", "mid", "cnt", "ge", "mn", "sa", "nm"]}
            bs["wt"] = T("bs_wt", [128, 16])
            pow2 = T("pow2", [128, 16])
            junkA = wwide.tiles[0][:].rearrange("p k c -> p (k c)")
            lf = T("lf_s", [128, 16, 8])

            for k2 in range(16):
                op('pool', lambda e, k2=k2: e.memset(pow2[:, k2:k2 + 1], 0.5 ** (k2 + 1)), writes=['pow2'])
            for sq in "PS":
                op('pool', lambda e, sq=sq: e.memset(QTb[sq][:], 0.0), writes=['QT'])
                op('pool', lambda e, sq=sq: e.memset(a_bf[sq][:], 0.0), writes=['a_bf' + sq])
                op('pool', lambda e, sq=sq: e.memset(cv_f[sq][:], 0.0), writes=['cv_f' + sq])
                op('pool', lambda e, sq=sq: e.memset(ckcar[sq][:], 0.0), writes=['ckcar' + sq])
                op('pool', lambda e, sq=sq: e.memset(CKT[sq][:], 0.0), writes=['CKT' + sq])
            for (vt, vk) in [(vb.tiles[0], 'vb0'), (vb.tiles[1], 'vb1')]:
                op('pool', lambda e, vt=vt: e.memset(vt[:], 1.0), writes=[vk])

            CP(30)
            wcache = {}

            def load_w(key, cc):
                if wcache.get('n') == (key, cc):
                    return wcache['t']
                w, wk = wnar.next()
                dma(q(), w[:], WS[key][cc], reads=['ws_' + key], writes=[wk])
                wcache['n'] = (key, cc)
                wcache['t'] = (w, wk)
                return w, wk

            def proj_fm(key, cc, lo, ncol, TB, evac):
                w, wk = load_w(key, cc)
                ps, pk = PS.next()
                for kc in range(8):
                    op('pe', lambda e, kc=kc: e.matmul(ps[:ncol, :TB], lhsT=w[:, kc, lo:lo + ncol], rhs=xT_b[:, kc, :TB],
                                                        start=(kc == 0), stop=(kc == 7)), reads=[wk, 'xT_b'], writes=[pk])
                evac(ps, pk)

            def load_wide(key, cc0, ncols):
                w, wk = wwide.next()
                for j in range((ncols + 127) // 128):
                    n = min(128, ncols - j * 128)
                    dma(q(), w[:, :, j * 128:j * 128 + n], WS[key][cc0 + j, :, :, :n], reads=['ws_' + key], writes=[wk])
                return w, wk

            def proj_tm(w, wk, ncols, t, rows):
                ps, pk = PS.next()
                for kc in range(8):
                    op('pe', lambda e, kc=kc: e.matmul(ps[:rows, :ncols], lhsT=xT_b[:, kc, t * 128:t * 128 + rows],
                                                        rhs=w[:, kc, :ncols], start=(kc == 0), stop=(kc == 7)),
                       reads=[wk, 'xT_b'], writes=[pk])
                return ps, pk

            def ln_fm(C, D, rg, rb, func, TB, outs):
                mps, mk = PS.next()
                sps, sk = PS.next()
                for c in range(C):
                    zb, zbk = tmb.next()
                    op('dve', lambda e, c=c, zb=zb: e.tensor_copy(out=zb[:, :TB], in_=zT[:, c, :TB]), reads=['zT'], writes=[zbk])
                    op('pe', lambda e, c=c, zb=zb: e.matmul(mps[:, :TB], lhsT=onesD[D][:], rhs=zb[:, :TB], start=(c == 0),
                                                             stop=(c == C - 1)), reads=[zbk, 'onesD'], writes=[mk])
                    zs, zsk = tmb.next()
                    op('act', lambda e, c=c, zs=zs: e.activation(out=zs[:, :TB], in_=zT[:, c, :TB], func=AF.Square),
                       reads=['zT'], writes=[zsk])
                    op('pe', lambda e, c=c, zs=zs: e.matmul(sps[:, :TB], lhsT=onesD[D][:], rhs=zs[:, :TB], start=(c == 0),
                                                             stop=(c == C - 1)), reads=[zsk, 'onesD'], writes=[sk])
                mean, m2, rstd = stat["mean"], stat["m2"], stat["rstd"]
                op('act', lambda e: e.copy(out=mean[:, :TB], in_=mps[:, :TB]), reads=[mk], writes=['st_mean'])
                op('dve', lambda e: e.tensor_tensor(out=m2[:, :TB], in0=mean[:, :TB], in1=mean[:, :TB], op=ALU.mult),
                   reads=['st_mean'], writes=['st_m2'])
                op('dve', lambda e: e.tensor_tensor(out=m2[:, :TB], in0=sps[:, :TB], in1=m2[:, :TB], op=ALU.subtract),
                   reads=[sk, 'st_m2'], writes=['st_m2'])
                op('dve', lambda e: e.tensor_scalar(out=m2[:, :TB], in0=m2[:, :TB], scalar1=1e-5, scalar2=None, op0=ALU.add),
                   reads=['st_m2'], writes=['st_m2'])
                op('act', lambda e: e.activation(out=rstd[:, :TB], in_=m2[:, :TB], func=AF.Sqrt), reads=['st_m2'],
                   writes=['st_rstd'])
                op('dve', lambda e: e.reciprocal(out=rstd[:, :TB], in_=rstd[:, :TB]), reads=['st_rstd'], writes=['st_rstd'])
                bc = lambda tl: tl[:, :TB].unsqueeze(1).to_broadcast([128, C, TB])
                op('pool', lambda e: e.tensor_tensor(out=zT[:, :C, :TB], in0=zT[:, :C, :TB], in1=bc(mean), op=ALU.subtract),
                   reads=['zT', 'st_mean'], writes=['zT'])
                op('dve', lambda e: e.tensor_tensor(out=zT[:, :C, :TB], in0=zT[:, :C, :TB], in1=bc(rstd), op=ALU.mult),
                   reads=['zT', 'st_rstd'], writes=['zT'])
                ot, ok = outs[0]
                for c in range(C):
                    op('act', lambda e, c=c: e.activation(out=ot[:, c, :TB], in_=zT[:, c, :TB], func=func,
                                                          bias=pcol[:, c, rb:rb + 1], scale=pcol[:, c, rg:rg + 1]),
                       reads=['zT', 'pcol'], writes=[ok])
                for (o2, o2k) in outs[1:]:
                    op('dve', lambda e: e.tensor_copy(out=o2[:, :C, :TB], in_=ot[:, :C, :TB]), reads=[ok], writes=[o2k])

            def out_proj_ln(ka, ko, rg, rb, TB):
                for cc in range(8):
                    wa, wak = wnar.next()
                    dma(q(), wa[:, 0:4, :], WS[ka][cc], reads=['ws_' + ka], writes=[wak])
                    wo, wok = wnar.next()
                    dma(q(), wo[0:64, :, :], WS[ko][cc], reads=['ws_' + ko], writes=[wok])
                    CP(60 + cc)
                    ps, pk = PS.next()
                    for kc in range(4):
                        op('pe', lambda e, kc=kc: e.matmul(ps[:, :TB], lhsT=wa[:, kc, :], rhs=uA[:, kc, :TB], start=(kc == 0),
                                                            stop=False), reads=[wak, 'uA'], writes=[pk])
                    CP(70 + cc)
                    for h in range(8):
                        op('pe', lambda e, h=h: e.matmul(ps[:, :TB], lhsT=wo[0:64, h, :], rhs=uO[0:64, h, :TB], start=False,
                                                          stop=(h == 7)), reads=[wok, 'uO'], writes=[pk])
                    CP(80 + cc)
                    op('dve', lambda e, cc=cc: e.scalar_tensor_tensor(out=zT[:, cc, :TB], in0=xT_f[:, cc, :TB], scalar=ALPHA,
                                                                      in1=ps[:, :TB], op0=ALU.mult, op1=ALU.add),
                       reads=[pk, 'xT_f'], writes=['zT'])
                    CP(44 + cc)
                wcache.clear()
                CP(40)
                ln_fm(8, 1024, rg, rb, AF.Identity, TB, [(xT_f, 'xT_f'), (xT_b, 'xT_b')])

            def attend(l, sq, TB, pos0, fox, nqt):
                KTS, VSS = SC[f"KT{l}{sq}"], SC[f"VS{l}{sq}"]
                nk = pos0 + TB
                nktot = (nk + 127) // 128
                tpc = KCH // 128
                LA = 3
                for pair in range(4):
                    chunk = {}
                    st = {}

                    def get_chunk(ch):
                        if ch not in chunk:
                            k0 = ch * KCH
                            n = min(KCH, nk - k0)
                            nkt = (n + 127) // 128
                            kb, kbk = kbuf.next()
                            vv, vvk = vbuf.next()
                            dma(q(), kb[:, :n], KTS[pair, :, k0:k0 + n], reads=[f"KT{l}{sq}"], writes=[kbk])
                            nfull = n // 128
                            if nfull > 0:
                                dma(q(), vv[:, :nfull, :], VSS[pair, :, k0 // 128:k0 // 128 + nfull, :], reads=[f"VS{l}{sq}"],
                                    writes=[vvk])
                            if nfull < nkt:
                                rr = n - nfull * 128
                                dma(q(), vv[:rr, nfull, :], VSS[pair, :rr, k0 // 128 + nfull, :], reads=[f"VS{l}{sq}"],
                                    writes=[vvk])
                            chunk[ch] = (kb, kbk, vv, vvk)
                        return chunk[ch]

                    def stA(kt):
                        ch, j = kt // tpc, kt % tpc
                        ks = min(128, nk - kt * 128)
                        kb, kbk, vv, vvk = get_chunk(ch)
                        ps, pk = PS.next()
                        diag = (kt * 128 >= pos0)
                        masked = (not fox) or diag
                        op('pe', lambda e: e.matmul(ps[:ks, :2 * TB], lhsT=kb[:, j * 128:j * 128 + ks], rhs=QTb[sq][:, pair, :2 * TB],
                                                    start=True, stop=not masked), reads=[kbk, 'QT'], writes=[pk])
                        if fox and diag:
                            kl = (kt * 128 - pos0) // 128
                            if TB == TBP:
                                op('pe', lambda e: e.matmul(ps[:ks, :2 * TB], lhsT=negI[:ks, :ks],
                                                            rhs=cmask2[:ks, kl, :, :].rearrange("p h q -> p (h q)"),
                                                            start=False, stop=True), reads=['negI', 'cmask'], writes=[pk])
                            else:
                                for hh in range(2):
                                    op('pe', lambda e: e.matmul(
                                        ps[:ks, hh * TB:hh * TB + TB], lhsT=negI[:ks, :ks], rhs=cmask[:ks, kl, :TB], start=False,
                                        stop=(hh == 1)), reads=['negI', 'cmask'], writes=[pk])
                        if not fox:
                            for qt in range(nqt):
                                qs = min(128, TB - qt * 128)
                                for hh in range(2):
                                    op('pe', lambda e: e.matmul(
                                        ps[:ks, hh * TB + qt * 128:hh * TB + qt * 128 + qs],
                                        lhsT=notsel[:qs, qt, kt * 128:kt * 128 + ks], rhs=negI[:qs, :qs], start=False,
                                        stop=(qt == nqt - 1 and hh == 1)), reads=['negI', 'notsel'], writes=[pk])
                        st[kt] = (ks, j, vv, vvk, ps, pk)

                    def stBC(kt):
                        ks, j, vv, vvk, ps, pk = st.pop(kt)
                        pt, ptk = tmb.next()
                        if fox:
                            for hh in range(2):
                                h = 2 * pair + hh
                                op('act', lambda e: e.activation(
                                    out=pt[:ks, hh * TB:hh * TB + TB], in_=ps[:ks, hh * TB:hh * TB + TB], func=AF.Exp,
                                    bias=biasT[:ks, kt, h:h + 1], scale=0.125), reads=[pk, 'biasT'], writes=[ptk])
                        else:
                            op('act', lambda e: e.activation(out=pt[:ks, :2 * TB], in_=ps[:ks, :2 * TB], func=AF.Exp,
                                                             scale=0.125), reads=[pk], writes=[ptk])
                        for hh in range(2):
                            po, pok = PO[hh]
                            op('pe', lambda e: e.matmul(
                                po[:65, :TB], lhsT=vv[:ks, j, 65 * hh:65 * hh + 65], rhs=pt[:ks, hh * TB:hh * TB + TB],
                                start=(kt == 0), stop=(kt == nktot - 1)), reads=[vvk, ptk], writes=[pok])

                    for i in range(nktot + LA):
                        if i < nktot:
                            stA(i)
                        if i >= LA:
                            stBC(i - LA)
                    for hh in range(2):
                        h = 2 * pair + hh
                        po, pok = PO[hh]
                        ob, obk = osb.next()
                        op('act', lambda e: e.copy(out=ob[:65, :TB], in_=po[:65, :TB]), reads=[pok], writes=[obk])
                        ps, pk = PS.next()
                        op('pe', lambda e: e.matmul(ps[:64, :TB], lhsT=ones_f[64:65, 0:64], rhs=ob[64:65, :TB],
                                                    start=True, stop=True), reads=[obk, 'ones_f'], writes=[pk])
                        t1, t1k = tmr.next()
                        op('dve', lambda e: e.reciprocal(out=t1[:64, :TB], in_=ps[:64, :TB]), reads=[pk], writes=[t1k])
                        op('dve', lambda e: e.tensor_tensor(out=t1[:64, :TB], in0=t1[:64, :TB], in1=ob[:64, :TB],
                                                            op=ALU.mult), reads=[t1k, obk], writes=[t1k])
                        op('pool', lambda e: e.tensor_tensor(out=uO[:64, h, :TB], in0=t1[:64, :TB], in1=gB[:64, h, :TB],
                                                             op=ALU.mult), reads=[t1k, 'gB'], writes=['uO'])

            def transposes_to(dst_fn, src, srck, rows, nchunks, cw, reads_extra=()):
                for j in range(nchunks):
                    ps, pk = PS.next()
                    op('pe', lambda e, j=j, ps=ps: e.transpose(out=ps[:cw, :rows], in_=src[:rows, j * cw:(j + 1) * cw],
                                                                identity=ident[:rows, :rows]),
                       reads=[srck, 'ident'] + list(reads_extra), writes=[pk])
                    dst_fn(j, ps, pk)

            def layer0(sq, x_ap, TB, pos0, outn, final):
                nt = (TB + 127) // 128
                rows_of = lambda t: min(128, TB - t * 128)
                sfx = "_p" if sq == "P" else "_s"
                op0_ = pos0 if sq == "P" else 0
                CP(31)
                for kc in range(8):
                    ps, pk = PS.next()
                    for t in range(nt):
                        r = rows_of(t)
                        op('pe', lambda e, kc=kc, t=t, r=r, ps=ps: e.transpose(
                            out=ps[:, t * 128:t * 128 + r], in_=xtokb[:r, t, kc * 128:(kc + 1) * 128], identity=ident[:r, :r]),
                           reads=['xtokb', 'ident'], writes=[pk])
                    op('act', lambda e, kc=kc, ps=ps: e.copy(out=xT_f[:, kc, :TB], in_=ps[:, :TB]), reads=[pk], writes=['xT_f'])
                    op('dve', lambda e, kc=kc, ps=ps: e.tensor_copy(out=xT_b[:, kc, :TB], in_=ps[:, :TB]), reads=[pk],
                       writes=['xT_b'])
                CP(3)
                ab = a_bf[sq]
                abk = 'a_bf' + sq
                wkv = [load_wide("e_in", 16, 512), load_wide("e_in", 20, 512)]
                for c in range(4):
                    proj_fm("e_in", 4 + c, 0, 128, TB,
                            lambda ps, pk: op('act', lambda e: e.activation(out=sg[:, :TB], in_=ps[:, :TB], func=AF.Sigmoid),
                                              reads=[pk], writes=['sg']))
                    proj_fm("e_in", c, 0, 128, TB,
                            lambda ps, pk, c=c: op('dve', lambda e: e.tensor_tensor(out=zT[:, 4 + c, :TB], in0=ps[:, :TB],
                                                                                    in1=sg[:, :TB], op=ALU.mult),
                                                   reads=[pk, 'sg'], writes=['zT']))
                    op('pool', lambda e, c=c: e.tensor_copy(out=ab[:, c, 30:30 + TB], in_=zT[:, 4 + c, :TB]), reads=['zT'],
                       writes=[abk])
                if final:
                    assert TB >= 30
                    for c in range(4):
                        ps, pk = PS.next()
                        op('pe', lambda e, c=c, ps=ps: e.transpose(out=ps[:30, 0:128], in_=zT[:, 4 + c, TB - 30:TB],
                                                                    identity=ident[:, :]), reads=['zT', 'ident'], writes=[pk])
                        t1, t1k = tmr.next()
                        op('act', lambda e, t1=t1, ps=ps: e.copy(out=t1[:30, 0:128], in_=ps[:30, 0:128]), reads=[pk], writes=[t1k])
                        dma(q(), O["conv" + sfx][:, c * 128:(c + 1) * 128], t1[:30, 0:128], reads=[t1k])
                CP(4)
                for c in range(4):
                    proj_fm("e_in", 8 + c, 0, 128, TB,
                            lambda ps, pk, c=c: op('act', lambda e: e.activation(out=gA[:, c, :TB], in_=ps[:, :TB], func=AF.Silu),
                                                   reads=[pk], writes=['gA']))
                for h in range(8):
                    proj_fm("e_in", 24 + h // 2, 64 * (h % 2), 64, TB,
                            lambda ps, pk, h=h: op('act', lambda e: e.activation(out=gB[:64, h, :TB], in_=ps[:64, :TB],
                                                                                 func=AF.Silu), reads=[pk], writes=['gB']))
                for c in range(4):
                    proj_fm("e_in", 12 + c, 0, 128, TB,
                            lambda ps, pk, c=c: (op('dve', lambda e: e.tensor_copy(out=QTb[sq][0:64, c, 0:TB], in_=ps[0:64, :TB]),
                                                    reads=[pk], writes=['QT']),
                                                 op('dve', lambda e: e.tensor_copy(out=QTb[sq][64:128, c, TB:2 * TB],
                                                                                   in_=ps[64:128, :TB]), reads=[pk], writes=['QT'])))
                for c in range(4):
                    proj_fm("e_in", 16 + c, 0, 128, TB,
                            lambda ps, pk, c=c: op('act', lambda e: e.copy(out=KTn[:, c, :TB], in_=ps[:, :TB]), reads=[pk],
                                                   writes=['KTn']))
                dma(q(), SC[f"KT0{sq}"][:, :, pos0:pos0 + TB].rearrange("a p n -> p a n"), KTn[:, :, :TB], reads=['KTn'],
                    writes=[f"KT0{sq}"])
                CP(5)
                def ev_f(ps, pk):
                    op('act', lambda e: e.activation(out=Lrow[:8, :TB], in_=ps[:8, :TB], func=AF.Exp, bias=nbf_col[:8, 0:1],
                                                     scale=-1.0), reads=[pk, 'nbf_col'], writes=['Lrow'])
                    op('act', lambda e: e.activation(out=Lrow[:8, :TB], in_=Lrow[:8, :TB], func=AF.Ln, bias=1.0),
                       reads=['Lrow'], writes=['Lrow'])
                proj_fm("e_in", 28, 0, 8, TB, ev_f)
                wcache.clear()
                op('dve', lambda e: e.tensor_tensor_scan(out=CKrow[:8, :TB], data0=onesrow[:8, :TB], data1=Lrow[:8, :TB],
                                                         initial=ckcar[sq][:8, 0:1], op0=ALU.mult, op1=ALU.add),
                   reads=['Lrow', 'onesrow', 'ckcar' + sq], writes=['CKrow'])
                op('dve', lambda e: e.tensor_copy(out=ckcar[sq][:8, 0:1], in_=CKrow[:8, TB - 1:TB]), reads=['CKrow'],
                   writes=['ckcar' + sq])
                ckt = CKT[sq]
                cktk = 'CKT' + sq
                for t in range(nt):
                    r = rows_of(t)
                    kt = pos0 // 128 + t
                    ps, pk = PS.next()
                    op('pe', lambda e, t=t, r=r, ps=ps: e.transpose(out=ps[:r, 0:8], in_=CKrow[:8, t * 128:t * 128 + r],
                                                                    identity=ident[:8, :8]), reads=['CKrow', 'ident'], writes=[pk])
                    op('pe', lambda e, t=t, r=r, ps=ps: e.transpose(out=ps[:r, 8:16], in_=Lrow[:8, t * 128:t * 128 + r],
                                                                    identity=ident[:8, :8]), reads=['Lrow', 'ident'], writes=[pk])
                    op('dve', lambda e, r=r, kt=kt, ps=ps: e.tensor_copy(out=ckt[:r, kt, :], in_=ps[:r, 0:8]), reads=[pk],
                       writes=[cktk])
                    t1, t1k = tmr.next()
                    op('act', lambda e, r=r, t1=t1, ps=ps: e.activation(out=t1[:r, 0:8], in_=ps[:r, 8:16], func=AF.Copy,
                                                                        scale=-1.0), reads=[pk], writes=[t1k])
                    dma('act', O["ff" + sfx][op0_ + t * 128:op0_ + t * 128 + r, :], t1[:r, 0:8], reads=[t1k])
                refp = pos0 + (TB // 2 if TB >= 256 else 0)
                ktm = refp // 128
                rowm = refp % 128
                assert rowm in (0, 32, 64)
                ps, pk = PS.next()
                op('pe', lambda e, ps=ps: e.matmul(ps[:, 0:8], lhsT=ones_f[rowm:rowm + 1, :], rhs=ckt[rowm:rowm + 1, ktm, :],
                                                   start=True, stop=True), reads=[cktk, 'ones_f'], writes=[pk])
                op('act', lambda e, ps=ps: e.copy(out=cref[:, :], in_=ps[:, 0:8]), reads=[pk], writes=['cref'])
                nktot = (pos0 + TB + 127) // 128
                for h in range(8):
                    op('dve' if h % 2 else 'pool', lambda e, h=h: e.tensor_scalar(
                        out=biasT[:, :nktot, h], in0=ckt[:, :nktot, h], scalar1=cref[:, h:h + 1], scalar2=None,
                        op0=ALU.subtract), reads=[cktk, 'cref'], writes=['biasT'])
                CP(6)
                for gi_, (cc0, name) in enumerate([(16, "fk"), (20, "fv")]):
                    w, wk = wkv[gi_]
                    for t in range(nt):
                        r = rows_of(t)
                        ps, pk = proj_tm(w, wk, 512, t, r)
                        t1, t1k = tmr.next()
                        op('act', lambda e, r=r, t1=t1, ps=ps: e.copy(out=t1[:r, :], in_=ps[:r, :]), reads=[pk], writes=[t1k])
                        dma('act', O[name + sfx][op0_ + t * 128:op0_ + t * 128 + r, :], t1[:r, :], reads=[t1k])
                        if name == "fv":
                            v, vk = vb.next()
                            for pr in range(4):
                                op('dve' if pr % 2 else 'pool', lambda e, r=r, v=v, t1=t1, pr=pr: e.tensor_copy(
                                    out=v[:r, pr, :].rearrange("p (h d) -> p h d", d=65)[:, :, 0:64],
                                    in_=t1[:r, pr * 128:(pr + 1) * 128].rearrange("p (h d) -> p h d", d=64)),
                                   reads=[t1k], writes=[vk])
                            dma(q(), SC[f"VS0{sq}"][:, :r, pos0 // 128 + t, :].rearrange("a p d -> p a d"), v[:r, :, :],
                                reads=[vk], writes=[f"VS0{sq}"])
                CP(7)
                for c in range(4):
                    cw_, cwk = cdw.next()
                    dma(q(), cw_[:], WS["cd"][c], reads=['ws_cd'], writes=[cwk])
                    ps, pk = PS.next()
                    for w in range(31):
                        op('pe', lambda e, c=c, w=w, cw_=cw_, ps=ps: e.matmul(ps[:, :TB], lhsT=cw_[:, w, :], rhs=ab[:, c, w:w + TB],
                                                                             start=(w == 0), stop=(w == 30)),
                           reads=[cwk, abk], writes=[pk])
                    op('act', lambda e, c=c, ps=ps: e.activation(out=zT[:, c, :TB], in_=ps[:, :TB], func=AF.Identity,
                                                                 bias=pcol[:, c, R_CB:R_CB + 1]), reads=[pk, 'pcol'], writes=['zT'])
                t1, t1k = tmb.next()
                op('pool', lambda e, t1=t1: e.tensor_copy(out=t1[:, 0:120].rearrange("p (c w) -> p c w", w=30),
                                                          in_=ab[:, :, TB:TB + 30]), reads=[abk], writes=[t1k])
                op('pool', lambda e, t1=t1: e.tensor_copy(out=ab[:, :, 0:30],
                                                          in_=t1[:, 0:120].rearrange("p (c w) -> p c w", w=30)),
                   reads=[t1k], writes=[abk])
                ln_fm(4, 512, R_CG, R_CBB, AF.Silu, TB, [(zT, 'zT')])
                for c in range(4):
                    op('pool', lambda e, c=c: e.tensor_tensor(out=uA[:, c, :TB], in0=zT[:, c, :TB], in1=gA[:, c, :TB],
                                                              op=ALU.mult), reads=['zT', 'gA'], writes=['uA'])
                CP(8)
                attend(0, sq, TB, pos0, True, nt)
                CP(9)
                out_proj_ln("e_oa", "e_oo", R_EG, R_EB, TB)

            def rope(eng, dst, dstk, src, srck, r, ncol):
                H = ncol // 64
                v3 = lambda ap_, a, b: ap_.rearrange("p (h d) -> p h d", d=64)[:, :, a:b]
                c3 = lambda tl: tl[:r, 0:32 * H].rearrange("p (h d) -> p h d", d=32)
                ta, tak = rq.next()
                tb_, tbk = rq.next()
                x1, x2 = v3(src[:r, :ncol], 0, 32), v3(src[:r, :ncol], 32, 64)
                o1, o2 = v3(dst[:r, :ncol], 0, 32), v3(dst[:r, :ncol], 32, 64)
                a3, b3 = c3(ta), c3(tb_)
                rd = [srck, 'ropeC', 'ropeS']
                op(eng, lambda e: e.tensor_tensor(out=a3, in0=x1, in1=c3(ropeC), op=ALU.mult), reads=rd, writes=[tak])
                op(eng, lambda e: e.tensor_tensor(out=b3, in0=x2, in1=c3(ropeS), op=ALU.mult), reads=rd, writes=[tbk])
                op(eng, lambda e: e.tensor_tensor(out=o1, in0=a3, in1=b3, op=ALU.subtract), reads=[tak, tbk], writes=[dstk])
                op(eng, lambda e: e.tensor_tensor(out=a3, in0=x2, in1=c3(ropeC), op=ALU.mult), reads=rd + [dstk], writes=[tak])
                op(eng, lambda e: e.tensor_tensor(out=b3, in0=x1, in1=c3(ropeS), op=ALU.mult), reads=rd, writes=[tbk])
                op(eng, lambda e: e.tensor_tensor(out=o2, in0=a3, in1=b3, op=ALU.add), reads=[tak, tbk], writes=[dstk])

            def layer1(sq, TB, pos0, final):
                nt = (TB + 127) // 128
                rows_of = lambda t: min(128, TB - t * 128)
                sfx = "_p" if sq == "P" else "_s"
                op0_ = pos0 if sq == "P" else 0
                tab0 = pos0 if sq == "P" else SEQ
                cv = cv_f[sq]
                cvk = 'cv_f' + sq
                work = []
                for c in range(4):
                    work.append(lambda c=c: proj_fm("o_in", c, 0, 128, TB,
                                lambda ps, pk, c=c: op('act', lambda e: e.copy(out=cv[:, c, 15:15 + TB], in_=ps[:, :TB]), reads=[pk],
                                                       writes=[cvk])))
                for c in range(4):
                    work.append(lambda c=c: proj_fm("o_in", 4 + c, 0, 128, TB,
                                lambda ps, pk, c=c: op('act', lambda e: e.activation(out=gA[:, c, :TB], in_=ps[:, :TB], func=AF.Silu),
                                                       reads=[pk], writes=['gA'])))
                for h in range(8):
                    work.append(lambda h=h: proj_fm("o_in", 20 + h // 2, 64 * (h % 2), 64, TB,
                                lambda ps, pk, h=h: op('act', lambda e: e.activation(out=gB[:64, h, :TB], in_=ps[:64, :TB],
                                                                                     func=AF.Silu), reads=[pk], writes=['gB'])))
                def _fin():
                    wcache.clear()
                    if final:
                        for c in range(4):
                            ps, pk = PS.next()
                            op('pe', lambda e, c=c, ps=ps: e.transpose(out=ps[:15, 0:128], in_=cv[:, c, TB:TB + 15],
                                                                        identity=ident[:, :]), reads=[cvk, 'ident'], writes=[pk])
                            t1, t1k = tmr.next()
                            op('act', lambda e, t1=t1, ps=ps: e.copy(out=t1[:15, 0:128], in_=ps[:15, 0:128]), reads=[pk], writes=[t1k])
                            dma(q(), O["pool" + sfx][:, c * 128:(c + 1) * 128], t1[:15, 0:128], reads=[t1k])
                work.append(_fin)
                W_ = 15 + TB
                def _pool(g):
                    src = cv[:, g, :]
                    srck = cvk
                    pp = [tmr.next(), tmr.next()]
                    for s in range(g + 1):
                        sh = 1 << s
                        dst_, dstk_ = pp[s % 2]
                        eng = 'pool'
                        op(eng, lambda e, src=src, dst_=dst_, sh=sh: e.tensor_tensor(
                            out=dst_[:, sh:W_], in0=src[:, sh:W_], in1=src[:, 0:W_ - sh], op=ALU.add),
                           reads=[srck], writes=[dstk_])
                        src = dst_
                        srck = dstk_
                    wdw = 2 << g
                    t1, t1k = tmr.next()
                    op('pool', lambda e, src=src, t1=t1, wdw=wdw: e.tensor_scalar(out=t1[:, :TB], in0=src[:, 15:15 + TB],
                                                                                  scalar1=1.0 / wdw, scalar2=None, op0=ALU.mult),
                       reads=[srck], writes=[t1k])
                    if sq == "P" and pos0 == 0:
                        op('pool', lambda e, src=src, t1=t1, g=g: e.tensor_tensor(out=t1[:, 0:16], in0=src[:, 15:31],
                                                                                 in1=poolfix[:, g, :], op=ALU.mult),
                           reads=[srck, 'poolfix', t1k], writes=[t1k])
                    pb_, pbk = tmb.next()
                    op('pool', lambda e, t1=t1, pb_=pb_, g=g: e.tensor_tensor(out=pb_[:, :TB], in0=t1[:, :TB],
                                                                             in1=cv[:, g, 15:15 + TB], op=ALU.subtract),
                       reads=[t1k, cvk], writes=[pbk])
                    ps, pk = PS.next()
                    op('pe', lambda e, g=g, pb_=pb_, ps=ps: e.matmul(ps[:, :TB], lhsT=poolw[:, g, :], rhs=pb_[:, :TB], start=True,
                                                                      stop=True), reads=['poolw', pbk], writes=[pk])
                    t2, t2k = tmr.next()
                    op('act', lambda e, g=g, t2=t2, ps=ps: e.activation(out=t2[:, :TB], in_=ps[:, :TB], func=AF.Copy,
                                                                        scale=pcol[:, g, R_PS:R_PS + 1]), reads=[pk, 'pcol'],
                       writes=[t2k])
                    op('pool', lambda e, g=g, t2=t2: e.tensor_tensor(out=uA[:, g, :TB], in0=t2[:, :TB], in1=gA[:, g, :TB],
                                                                     op=ALU.mult), reads=[t2k, 'gA'], writes=['uA'])
                for g in range(4):
                    work.append(lambda g=g: _pool(g))
                def _carry():
                    t1, t1k = tmr.next()
                    op('pool', lambda e, t1=t1: e.tensor_copy(out=t1[:, 0:60].rearrange("p (c w) -> p c w", w=15),
                                                              in_=cv[:, :, TB:TB + 15]), reads=[cvk], writes=[t1k])
                    op('pool', lambda e, t1=t1: e.tensor_copy(out=cv[:, :, 0:15],
                                                              in_=t1[:, 0:60].rearrange("p (c w) -> p c w", w=15)),
                       reads=[t1k], writes=[cvk])
                work.append(_carry)
                CP(12)
                grps = [(8, 512, "q"), (12, 512, "k"), (16, 512, "v"), (24, 324, "i")]
                wpre = load_wide("o_in", grps[0][0], grps[0][1])
                for gi_, (cc0, ncols, what) in enumerate(grps):
                    w, wk = wpre
                    if gi_ + 1 < len(grps):
                        wpre = load_wide("o_in", grps[gi_ + 1][0], grps[gi_ + 1][1])
                    for t in range(nt):
                        r = rows_of(t)
                        if what != "v":
                            dma(q(), ropeC[:r, :], I["k_cos"][tab0 + t * 128:tab0 + t * 128 + r, :], writes=['ropeC'])
                            dma(q(), ropeS[:r, :], I["k_sin"][tab0 + t * 128:tab0 + t * 128 + r, :], writes=['ropeS'])
                        ps, pk = proj_tm(w, wk, ncols, t, r)
                        x_, xk_ = rp.next()
                        op('act', lambda e, r=r, x_=x_, ps=ps, ncols=ncols: e.copy(out=x_[:r, :ncols], in_=ps[:r, :ncols]),
                           reads=[pk], writes=[xk_])
                        orow = slice(op0_ + t * 128, op0_ + t * 128 + r)
                        if what == "v":
                            dma('act', O["dv" + sfx][orow, :], x_[:r, :], reads=[xk_])
                            v, vk = vb.next()
                            for pr in range(4):
                                op('dve' if pr % 2 else 'pool', lambda e, r=r, v=v, x_=x_, pr=pr: e.tensor_copy(
                                    out=v[:r, pr, :].rearrange("p (h d) -> p h d", d=65)[:, :, 0:64],
                                    in_=x_[:r, pr * 128:(pr + 1) * 128].rearrange("p (h d) -> p h d", d=64)),
                                   reads=[xk_], writes=[vk])
                            dma(q(), SC[f"VS1{sq}"][:, :r, pos0 // 128 + t, :].rearrange("a p d -> p a d"), v[:r, :, :],
                                reads=[vk], writes=[f"VS1{sq}"])
                            continue
                        y_, yk_ = rp.next()
                        if what == "q":
                            rope('dve', y_, yk_, x_, xk_, r, 512)
                            transposes_to(lambda j, ps, pk, t=t, r=r: (
                                op('act', lambda e: e.copy(out=QTb[sq][0:64, j, t * 128:t * 128 + r], in_=ps[0:64, :r]),
                                   reads=[pk], writes=['QT']),
                                op('act', lambda e: e.copy(out=QTb[sq][64:128, j, TB + t * 128:TB + t * 128 + r],
                                                           in_=ps[64:128, :r]), reads=[pk], writes=['QT'])), y_, yk_, r, 4, 128)
                        elif what == "k":
                            rope('dve', y_, yk_, x_, xk_, r, 512)
                            dma(q(), O["dk" + sfx][orow, :], y_[:r, :], reads=[yk_])
                            transposes_to(lambda j, ps, pk, t=t, r=r: op('dve', lambda e: e.tensor_copy(
                                out=KTn[:, j, t * 128:t * 128 + r], in_=ps[:, :r]), reads=[pk], writes=['KTn']), y_, yk_, r, 4, 128)
                        else:
                            rope('dve', y_, yk_, x_, xk_, r, 320)
                            dma(q(), O["di" + sfx][orow, :], y_[:r, 256:320], reads=[yk_])
                            transposes_to(lambda j, ps, pk, t=t, r=r: op('act', lambda e: e.copy(
                                out=IQT[:, j, t * 128:t * 128 + r], in_=ps[:64, :r]), reads=[pk], writes=['IQT']), y_, yk_, r, 4, 64)
                            ps2, pk2 = PS.next()
                            op('pe', lambda e, r=r, y_=y_, ps2=ps2: e.transpose(out=ps2[:64, :r], in_=y_[:r, 256:320],
                                                                                identity=ident[:r, :r]), reads=[yk_, 'ident'],
                               writes=[pk2])
                            op('dve', lambda e, t=t, r=r, ps2=ps2: e.tensor_copy(out=IKn[:, t * 128:t * 128 + r], in_=ps2[:64, :r]),
                               reads=[pk2], writes=['IKn'])
                            op('dve', lambda e, t=t, r=r, x_=x_: e.tensor_scalar(out=wq[:r, t, :], in0=x_[:r, 320:324], scalar1=0.5,
                                                                                 scalar2=None, op0=ALU.mult), reads=[xk_],
                               writes=['wq'])
                dma(q(), SC[f"KT1{sq}"][:, :, pos0:pos0 + TB].rearrange("a p n -> p a n"), KTn[:, :, :TB], reads=['KTn'],
                    writes=[f"KT1{sq}"])
                dma(q(), SC[f"IK{sq}"][:, pos0:pos0 + TB], IKn[:, :TB], reads=['IKn'], writes=[f"IK{sq}"])
                CP(13)
                nk = pos0 + TB
                nkb = (nk + 511) // 512
                for qt in range(nt):
                    r = rows_of(qt)
                    for ch in range((nk + KCH - 1) // KCH):
                        k0 = ch * KCH
                        n = min(KCH, nk - k0)
                        ib, ibk = kbuf.next()
                        dma(q(), ib[0:64, :n], SC[f"IK{sq}"][:, k0:k0 + n], reads=[f"IK{sq}"], writes=[ibk])
                        for kb_ in range((n + 511) // 512):
                            c0 = kb_ * 512
                            m = min(512, n - c0)
                            g0 = k0 + c0
                            diag = (sq == "P") and (g0 >= pos0)
                            pss = []
                            for hi in range(4):
                                ps, pk = PS.next()
                                op('pe', lambda e, hi=hi, r=r, qt=qt, ib=ib, c0=c0, m=m, ps=ps: e.matmul(
                                    ps[:r, :m], lhsT=IQT[:, hi, qt * 128:qt * 128 + r], rhs=ib[0:64, c0:c0 + m], start=True,
                                    stop=True), reads=['IQT', ibk], writes=[pk])
                                pss.append((ps, pk))
                            for hi in range(4):
                                ps, pk = pss[hi]
                                rl, rlk = tmr.next()
                                op('act', lambda e, r=r, m=m, rl=rl, ps=ps: e.activation(out=rl[:r, :m], in_=ps[:r, :m],
                                                                                         func=AF.Relu, scale=0.125),
                                   reads=[pk], writes=[rlk])
                                dst_ = scores[:r, g0:g0 + m]
                                if hi == 0 and diag:
                                    op('dve', lambda e, r=r, m=m, rl=rl, dst_=dst_, qt=qt: e.scalar_tensor_tensor(
                                        out=dst_, in0=rl[:r, :m], scalar=wq[:r, qt, 0:1], in1=admn[:r, qt, :m], op0=ALU.mult,
                                        op1=ALU.add), reads=[rlk, 'wq', 'admn', 'scores'], writes=['scores'])
                                elif hi == 0:
                                    op('dve', lambda e, r=r, m=m, rl=rl, dst_=dst_, qt=qt: e.tensor_scalar(
                                        out=dst_, in0=rl[:r, :m], scalar1=wq[:r, qt, 0:1], scalar2=None, op0=ALU.mult),
                                       reads=[rlk, 'wq', 'scores'], writes=['scores'])
                                else:
                                    op('dve', lambda e, r=r, m=m, rl=rl, dst_=dst_, qt=qt, hi=hi: e.scalar_tensor_tensor(
                                        out=dst_, in0=rl[:r, :m], scalar=wq[:r, qt, hi:hi + 1], in1=dst_, op0=ALU.mult,
                                        op1=ALU.add), reads=[rlk, 'wq', 'scores'], writes=['scores'])
                    hi_, lo_, w_, mid_, cnt_, ge_, mn_ = (bs[n_] for n_ in ["hi", "lo", "w", "mid", "cnt", "ge", "mn"])
                    B = lambda tl, c=0: tl[:r, c:c + 1]
                    op('dve', lambda e: e.tensor_reduce(out=B(hi_), in_=scores[:r, :nk], axis=AX.X, op=ALU.max),
                       reads=['scores'], writes=['bs_hi'])
                    if sq == "P":
                        nd = pos0
                        t1, t1k = tmr.next()
                        op('dve', lambda e, t1=t1: e.scalar_tensor_tensor(out=t1[:r, :TB], in0=admn[:r, qt, :TB], scalar=-2.0,
                                                                          in1=scores[:r, nd:nd + TB], op0=ALU.mult, op1=ALU.add),
                           reads=['scores', 'admn'], writes=[t1k])
                        op('dve', lambda e, t1=t1: e.tensor_reduce(out=B(lo_), in_=t1[:r, :TB], axis=AX.X, op=ALU.min),
                           reads=[t1k], writes=['bs_lo'])
                        if nd > 0:
                            op('dve', lambda e: e.tensor_reduce(out=B(mn_), in_=scores[:r, :nd], axis=AX.X, op=ALU.min),
                               reads=['scores'], writes=['bs_mn'])
                            op('dve', lambda e: e.tensor_tensor(out=B(lo_), in0=B(lo_), in1=B(mn_), op=ALU.min),
                               reads=['bs_lo', 'bs_mn'], writes=['bs_lo'])
                    else:
                        op('dve', lambda e: e.tensor_reduce(out=B(lo_), in_=scores[:r, :nk], axis=AX.X, op=ALU.min),
                           reads=['scores'], writes=['bs_lo'])
                    op('dve', lambda e: e.tensor_tensor(out=B(w_), in0=B(hi_), in1=B(lo_), op=ALU.subtract),
                       reads=['bs_hi', 'bs_lo'], writes=['bs_w'])
                    spl = nk if nk < 1024 else max(((nk * 9 // 20) // 512) * 512, nk - 4096, 512)
                    nA = nk - spl
                    npc = (spl + 2047) // 2048
                    sA_, nm_, wt_ = bs["sa"], bs["nm"], bs["wt"]
                    op('dve', lambda e: e.tensor_scalar(out=wt_[:r, 0:NIT], in0=pow2[:r, 0:NIT], scalar1=B(w_), scalar2=None,
                                                        op0=ALU.mult), reads=['bs_w', 'pow2'], writes=['bs_wt'])
                    for it in range(NIT):
                        wk = wt_[:r, it:it + 1]
                        op('dve', lambda e: e.tensor_tensor(out=B(mid_), in0=B(lo_), in1=wk, op=ALU.add),
                           reads=['bs_lo', 'bs_wt'], writes=['bs_mid'])
                        if nA > 0:
                            op('pool', lambda e: e.tensor_scalar(out=B(nm_), in0=B(mid_), scalar1=-1.0, scalar2=None, op0=ALU.mult),
                               reads=['bs_mid'], writes=['bs_nm'])
                            op('act', lambda e: e.activation(out=junkA[:r, :nA], in_=scores[:r, spl:nk], func=AF.Sign,
                                                             bias=B(nm_), scale=1.0, accum_out=B(sA_)),
                               reads=['scores', 'bs_nm', 'wwide0'], writes=['wwide0', 'bs_sa'])
                        for pc in range(npc):
                            c0 = pc * 2048
                            m = min(2048, spl - c0)
                            op('dve', lambda e, pc=pc, c0=c0, m=m: e.tensor_scalar(
                                out=notsel[:r, qt, c0:c0 + m], in0=scores[:r, c0:c0 + m], scalar1=B(mid_), scalar2=None,
                                op0=ALU.is_ge, op1=ALU.add, accum_out=B(cnt_, pc)), reads=['scores', 'bs_mid', 'notsel'],
                               writes=['notsel', 'bs_cnt'])
                        if work:
                            work.pop(0)()
                        if npc > 1:
                            op('dve', lambda e: e.tensor_reduce(out=B(ge_), in_=cnt_[:r, 0:npc], axis=AX.X, op=ALU.add),
                               reads=['bs_cnt'], writes=['bs_ge'])
                            cs = B(ge_)
                        else:
                            cs = B(cnt_)
                        thr = 255.5
                        if nA > 0:
                            op('dve', lambda e, cs=cs: e.scalar_tensor_tensor(out=B(ge_), in0=B(sA_), scalar=0.5, in1=cs,
                                                                              op0=ALU.mult, op1=ALU.add),
                               reads=['bs_sa', 'bs_cnt', 'bs_ge'], writes=['bs_ge'])
                            cs = B(ge_)
                            thr = 255.5 - nA / 2.0
                        op('dve', lambda e, cs=cs: e.tensor_scalar(out=B(ge_), in0=cs, scalar1=thr, scalar2=wk, op0=ALU.is_ge,
                                                                   op1=ALU.mult), reads=['bs_cnt', 'bs_ge', 'bs_wt'],
                           writes=['bs_ge'])
                        op('dve', lambda e: e.tensor_tensor(out=B(lo_), in0=B(lo_), in1=B(ge_), op=ALU.add),
                           reads=['bs_ge', 'bs_lo'], writes=['bs_lo'])
                    npc = (nk + 2047) // 2048
                    for pc in range(npc):
                        c0 = pc * 2048
                        m = min(2048, nk - c0)
                        op('dve' if pc % 2 == 0 else 'pool', lambda e, c0=c0, m=m: e.tensor_scalar(
                            out=notsel[:r, qt, c0:c0 + m], in0=scores[:r, c0:c0 + m], scalar1=B(lo_), scalar2=None,
                            op0=ALU.is_lt), reads=['scores', 'bs_lo'], writes=['notsel'])
                while work:
                    work.pop(0)()
                CP(14)
                attend(1, sq, TB, pos0, False, nt)
                CP(15)
                out_proj_ln("o_oa", "o_oo", R_OG, R_OB, TB)
                CP(16)
                for t in range(nt):
                    r = rows_of(t)
                    for half in range(2):
                        ps, pk = PS.next()
                        for j in range(4):
                            kc = half * 4 + j
                            op('pe', lambda e, kc=kc, j=j, t=t, r=r, ps=ps: e.transpose(
                                out=ps[:r, j * 128:(j + 1) * 128], in_=xT_f[:, kc, t * 128:t * 128 + r], identity=ident[:, :]),
                               reads=['xT_f', 'ident'], writes=[pk])
                        op('act' if half else 'dve',
                           (lambda e, t=t, r=r, ps=ps, half=half: e.copy(out=xtk(t, half * 512, (half + 1) * 512)[:r, :], in_=ps[:r, :]))
                           if half else
                           (lambda e, t=t, r=r, ps=ps, half=half: e.tensor_copy(out=xtk(t, half * 512, (half + 1) * 512)[:r, :],
                                                                                in_=ps[:r, :])),
                           reads=[pk], writes=['zT'])
                    dma(q(), O["y" + sfx][op0_ + t * 128:op0_ + t * 128 + r, :], xtk(t, 0, 1024)[:r, :], reads=['zT'])

            def stage_sample():
                npt = PAST // 128
                t1, t1k = tmr.next()
                dma(q(), t1[:30, :], I["c_conv"], writes=[t1k])
                for c in range(4):
                    ps, pk = PS.next()
                    op('pe', lambda e, c=c, ps=ps, t1=t1: e.transpose(out=ps[:, 0:30], in_=t1[:30, c * 128:(c + 1) * 128],
                                                                      identity=ident[:30, :30]), reads=[t1k, 'ident'], writes=[pk])
                    op('act', lambda e, c=c, ps=ps: e.copy(out=a_bf["S"][:, c, 0:30], in_=ps[:, 0:30]), reads=[pk],
                       writes=['a_bfS'])
                t2, t2k = tmr.next()
                dma(q(), t2[:15, :], I["c_pool"], writes=[t2k])
                for c in range(4):
                    ps, pk = PS.next()
                    op('pe', lambda e, c=c, ps=ps, t2=t2: e.transpose(out=ps[:, 0:15], in_=t2[:15, c * 128:(c + 1) * 128],
                                                                      identity=ident[:15, :15]), reads=[t2k, 'ident'], writes=[pk])
                    op('act', lambda e, c=c, ps=ps: e.copy(out=cv_f["S"][:, c, 0:15], in_=ps[:, 0:15]), reads=[pk],
                       writes=['cv_fS'])
                dma(q(), lf[:], I["c_ff"].rearrange("(t p) h -> p t h", p=128), writes=['lf_s'])
                for g4 in range(4):
                    ps, pk = PS.next()
                    for j in range(4):
                        t = g4 * 4 + j
                        op('pe', lambda e, t=t, j=j, ps=ps: e.transpose(out=ps[:8, j * 128:(j + 1) * 128], in_=lf[:, t, :],
                                                                        identity=ident[:, :]), reads=['lf_s', 'ident'], writes=[pk])
                    op('act', lambda e, ps=ps: e.activation(out=Lrow[:8, :], in_=ps[:8, :], func=AF.Copy, scale=-1.0), reads=[pk],
                       writes=['Lrow'])
                    op('dve', lambda e: e.tensor_tensor_scan(out=CKrow[:8, :], data0=onesrow[:8, :], data1=Lrow[:8, :],
                                                             initial=ckcar["S"][:8, 0:1], op0=ALU.mult, op1=ALU.add),
                       reads=['Lrow', 'onesrow', 'ckcarS'], writes=['CKrow'])
                    op('dve', lambda e: e.tensor_copy(out=ckcar["S"][:8, 0:1], in_=CKrow[:8, 511:512]), reads=['CKrow'],
                       writes=['ckcarS'])
                    for j in range(4):
                        t = g4 * 4 + j
                        ps2, pk2 = PS.next()
                        op('pe', lambda e, j=j, ps2=ps2: e.transpose(out=ps2[:, 0:8], in_=CKrow[:8, j * 128:(j + 1) * 128],
                                                                     identity=ident[:8, :8]), reads=['CKrow', 'ident'], writes=[pk2])
                        op('dve', lambda e, t=t, ps2=ps2: e.tensor_copy(out=CKT["S"][:, t, :], in_=ps2[:, 0:8]), reads=[pk2],
                           writes=['CKTS'])
                for (src, l) in [("c_fk", 0), ("c_dk", 1)]:
                    for t in range(npt):
                        x_, xk_ = rp.next()
                        dma(q(), x_[:, :], I[src][t * 128:(t + 1) * 128, :], writes=[xk_])
                        ps, pk = PS.next()
                        for j in range(4):
                            op('pe', lambda e, j=j, ps=ps, x_=x_: e.transpose(out=ps[:, j * 128:(j + 1) * 128],
                                                                              in_=x_[:, j * 128:(j + 1) * 128], identity=ident[:, :]),
                               reads=[xk_, 'ident'], writes=[pk])
                        kb, kbk = tmb.next()
                        op('act' if t % 2 else 'dve',
                           (lambda e, kb=kb, ps=ps: e.copy(out=kb[:, :], in_=ps[:, :])) if t % 2 else
                           (lambda e, kb=kb, ps=ps: e.tensor_copy(out=kb[:, :], in_=ps[:, :])), reads=[pk], writes=[kbk])
                        dma(q(), SC[f"KT{l}S"][:, :, t * 128:(t + 1) * 128].rearrange("a p n -> p a n"),
                            kb[:, :].rearrange("p (a n) -> p a n", n=128), reads=[kbk], writes=[f"KT{l}S"])
                for (src, l) in [("c_fv", 0), ("c_dv", 1)]:
                    for t in range(npt):
                        x_, xk_ = rp.next()
                        dma(q(), x_[:, :], I[src][t * 128:(t + 1) * 128, :], writes=[xk_])
                        v, vk = vb.next()
                        for pr in range(4):
                            op('dve' if pr % 2 else 'pool', lambda e, v=v, x_=x_, pr=pr: e.tensor_copy(
                                out=v[:, pr, :].rearrange("p (h d) -> p h d", d=65)[:, :, 0:64],
                                in_=x_[:, pr * 128:(pr + 1) * 128].rearrange("p (h d) -> p h d", d=64)), reads=[xk_], writes=[vk])
                        dma(q(), SC[f"VS{l}S"][:, :, t, :].rearrange("a p d -> p a d"), v[:, :, :], reads=[vk], writes=[f"VS{l}S"])
                for t in range(npt):
                    x_, xk_ = rp.next()
                    dma(q(), x_[:, 0:64], I["c_di"][t * 128:(t + 1) * 128, :], writes=[xk_])
                    ps, pk = PS.next()
                    op('pe', lambda e, ps=ps, x_=x_: e.transpose(out=ps[:64, 0:128], in_=x_[:, 0:64], identity=ident[:, :]),
                       reads=[xk_, 'ident'], writes=[pk])
                    kb, kbk = tmb.next()
                    op('act', lambda e, kb=kb, ps=ps: e.copy(out=kb[:64, 0:128], in_=ps[:64, 0:128]), reads=[pk], writes=[kbk])
                    dma(q(), SC["IKS"][:, t * 128:(t + 1) * 128], kb[:64, 0:128], reads=[kbk], writes=["IKS"])

            def x_load(x_ap, TB):
                for t in range((TB + 127) // 128):
                    r = min(128, TB - t * 128)
                    dma(q(), xtokb[:r, t, :], x_ap[t * 128:t * 128 + r, :], writes=['xtokb'])

            try:
              x_load(I["xp"][0:TBP, :], TBP)
              for blk in range(nblk):
                fin = (blk == nblk - 1)
                layer0("P", I["xp"][blk * TBP:(blk + 1) * TBP, :], TBP, blk * TBP, None, fin)
                CP(10)
                if blk + 1 < nblk:
                    x_load(I["xp"][(blk + 1) * TBP:(blk + 2) * TBP, :], TBP)
                elif do_sample:
                    x_load(I["xs"], TS)
                layer1("P", TBP, blk * TBP, fin)
              if do_sample:
                stage_sample()
                CP(20)
                layer0("S", I["xs"], TS, PAST, None, True)
                CP(21)
                layer1("S", TS, PAST, True)
            except _Stop:
                pass
        try:
            body()
        except _Stop:
            pass
        S.finish()
        build.ninst = S.ninst
    return nc


def _consts(SEQ):
    kk = np.arange(128)[:, None]
    qq = np.arange(TBP)[None, :]
    cmask = np.stack([((128 * a + kk) > qq).astype(np.float32) for a in range(NTP)])
    q2 = np.arange(128)[:, None]
    k2 = np.arange(TBP)[None, :]
    adm = np.stack([np.where((k2 // 64) > ((128 * a + q2) // 64), -1e30, 0.0).astype(np.float32) for a in range(NTP)])
    pos = np.concatenate([np.arange(SEQ), PAST + np.arange(TS)]).astype(np.float32)
    inv = (10000.0 ** (-np.arange(0, 64, 2, dtype=np.float32) / 64)).astype(np.float32)
    ang = pos[:, None] * inv[None, :]
    cos = np.tile(np.cos(ang).astype(np.float32), (1, 8))
    sin = np.tile(np.sin(ang).astype(np.float32), (1, 8))
    fix = np.zeros((128, 4, 16), np.float32)
    for g, w in enumerate((2, 4, 8, 16)):
        fix[:, g, :] = 1.0 / np.minimum(np.arange(16) + 1, w)
    return {"k_cmask": cmask, "k_adm": adm, "k_cos": cos, "k_sin": sin, "k_poolfix": fix}


_NC_CACHE = {}


def kernel(x_prompt, x_sample, cache_conv, cache_fox_k, cache_fox_v, cache_fox_logf, cache_pool, cache_dsa_k,
           cache_dsa_v, cache_dsa_idx_k, e_w_in, e_b_f, e_conv_w, e_conv_b, e_conv_ln_g, e_conv_ln_b, e_w_out, e_ln_g,
           e_ln_b, o_w_in, o_pool_w, o_pool_scale, o_w_out, o_ln_g, o_ln_b, _nblk=SEQ // TBP, _sample=True):
    f = lambda a: np.ascontiguousarray(np.asarray(a, dtype=np.float32))
    if (_nblk, _sample) not in _NC_CACHE:
        _NC_CACHE[(_nblk, _sample)] = build(_nblk, _sample)
    nc = _NC_CACHE[(_nblk, _sample)]
    pvec = np.zeros((40, 1024), np.float32)
    pvec[0:31, 0:512] = f(e_conv_w)[0]
    pvec[31, 0:512] = f(e_conv_b)[0]
    pvec[32, 0:512] = f(e_conv_ln_g)[0]
    pvec[33, 0:512] = f(e_conv_ln_b)[0]
    pvec[34] = f(e_ln_g)[0]
    pvec[35] = f(e_ln_b)[0]
    pvec[36, 0:512] = f(o_pool_scale)[0]
    pvec[37] = f(o_ln_g)[0]
    pvec[38] = f(o_ln_b)[0]
    SEQE = _nblk * TBP
    cst = _consts(SEQE)
    shared = {"e_w_in": f(e_w_in)[0], "e_b_f": f(e_b_f)[0].reshape(8, 1), "e_w_out": f(e_w_out)[0],
              "o_w_in": f(o_w_in)[0], "o_pool_w": f(o_pool_w)[0], "o_w_out": f(o_w_out)[0], "pvec": pvec}
    shared.update(cst)
    in_maps = []
    for c in range(8):
        m = dict(shared)
        m["xp"] = f(x_prompt)[c // 4][:SEQE]
        m["xs"] = f(x_sample)[c]
        m["c_conv"] = f(cache_conv)[0, c]
        m["c_fk"] = f(cache_fox_k)[0, c].reshape(PAST, 512)
        m["c_fv"] = f(cache_fox_v)[0, c].reshape(PAST, 512)
        m["c_ff"] = f(cache_fox_logf)[0, c]
        m["c_pool"] = f(cache_pool)[0, c]
        m["c_dk"] = f(cache_dsa_k)[0, c].reshape(PAST, 512)
        m["c_dv"] = f(cache_dsa_v)[0, c].reshape(PAST, 512)
        m["c_di"] = f(cache_dsa_idx_k)[0, c]
        in_maps.append(m)
    res = run_bass_kernel_spmd(nc, in_maps, core_ids=list(range(8)))
    R = res.results
    P = lambda n: np.stack([R[0][n], R[4][n]])
    Sm = lambda n: np.stack([R[c][n] for c in range(8)])
    out = (P("y_p"), Sm("y_s"),
           P("conv_p")[None], Sm("conv_s")[None],
           P("fk_p").reshape(1, 2, SEQE, 8, 64), Sm("fk_s").reshape(1, 8, TS, 8, 64),
           P("fv_p").reshape(1, 2, SEQE, 8, 64), Sm("fv_s").reshape(1, 8, TS, 8, 64),
           P("ff_p")[None], Sm("ff_s")[None],
           P("pool_p")[None], Sm("pool_s")[None],
           P("dk_p").reshape(1, 2, SEQE, 8, 64), Sm("dk_s").reshape(1, 8, TS, 8, 64),
           P("dv_p").reshape(1, 2, SEQE, 8, 64), Sm("dv_s").reshape(1, 8, TS, 8, 64),
           P("di_p")[None], Sm("di_s")[None])
    return tuple(np.ascontiguousarray(o, dtype=np.float32) for o in out)
```

```python
import numpy as np
import concourse.bass as bass
import concourse.mybir as mybir
from concourse.bass_utils import run_bass_kernel_spmd
from contextlib import ExitStack

F32 = mybir.dt.float32
BF16 = mybir.dt.bfloat16
AF = mybir.ActivationFunctionType
ALU = mybir.AluOpType
AX = mybir.AxisListType

SEQ = 8192
PAST = 2048
TS = 32
ALPHA = 4.0 ** 0.25
NIT = 12
TBP = 256
NTP = TBP // 128
KCH = 1024
NEGM = -30000.0
PE_SKIP = True


class _Rec:
    def __getattr__(self, name):
        def f(*a, **k):
            self.call = (name, a, k)
            return self
        return f


class Sched:
    SEM_LIMIT = 2000
    NDMA = 24

    def __init__(self, nc, es):
        self.nc = nc
        self.es = es
        self.names = ['pe', 'act', 'dve', 'pool', 'sp']
        self.prog = {e: [] for e in self.names}
        self.cnt = {e: 0 for e in self.names}
        self.waited = {e: {} for e in self.names}
        self.bufs = {}
        self.pe_force = False
        self.pe_mode = None
        self.dsem = []
        for i in range(self.NDMA):
            s = es.enter_context(nc.semaphore(f"dq{i}"))
            self.dsem.append([s, 0])
        self.dnext = 0
        self.ninst = 0

    def _deps(self, reads, writes):
        toks = []
        for k in reads:
            b = self.bufs.get(k)
            if b and b[0] is not None:
                toks.append(b[0])
            if b and k.startswith('pb'):
                toks.extend(b[1])
        for k in writes:
            b = self.bufs.get(k)
            if b:
                if b[0] is not None:
                    toks.append(b[0])
                toks.extend(b[1])
        return toks

    @staticmethod
    def _key(tok):
        return ('E', tok[1]) if tok[0] == 'E' else ('D', id(tok[1]))

    def _emit_waits(self, e, toks):
        need = {}
        for tok in toks:
            k = self._key(tok)
            if PE_SKIP and e == 'pe' and tok[0] == 'E' and tok[1] == 'pe' and not self.pe_force:
                continue
            if self.waited[e].get(k, 0) >= tok[2]:
                continue
            if k not in need or need[k][2] < tok[2]:
                need[k] = tok
        for k, tok in need.items():
            self.waited[e][k] = tok[2]
            self.prog[e].append(('w', tok))

    def _compact(self, toks):
        best = {}
        for tok in toks:
            k = self._key(tok)
            if k not in best or best[k][2] < tok[2]:
                best[k] = tok
        return list(best.values())

    def _update(self, tok, reads, writes):
        for k in reads:
            b = self.bufs.setdefault(k, [None, []])
            b[1].append(tok)
            if len(b[1]) > 12:
                b[1] = self._compact(b[1])
        for k in writes:
            self.bufs[k] = [tok, []]

    def op(self, e, fn, reads=(), writes=()):
        self._emit_waits(e, self._deps(reads, writes))
        rec = _Rec()
        fn(rec)
        name, a, k = rec.call
        if e == 'pe':
            st_ = k.get('lhsT', k.get('in_'))
            r32 = lambda n: 32 if n <= 32 else (64 if n <= 64 else 128)
            fr = 1
            for d in st_.shape[1:]:
                fr *= d
            mode = (name, r32(st_.shape[0]), r32(fr), str(st_.dtype))
            if mode != self.pe_mode and self.cnt['pe'] > 0:
                self.pe_force = True
                self._emit_waits('pe', [('E', 'pe', self.cnt['pe'])])
                self.pe_force = False
            self.pe_mode = mode
        self.cnt[e] += 1
        idx = self.cnt[e]
        self.prog[e].append(('op', name, a, k, idx))
        self._update(('E', e, idx), reads, writes)
        self.ninst += 1

    def dma(self, q, out, in_, reads=(), writes=()):
        slot = self.dsem[self.dnext]
        self.dnext = (self.dnext + 1) % self.NDMA
        s = slot[0]
        toks = self._deps(reads, writes)
        if slot[1] > 0:
            toks.append(('D', s, slot[1]))
        self._emit_waits(q, toks)
        slot[1] += 16
        self.prog[q].append(('dma', out, in_, s))
        self._update(('D', s, slot[1]), reads, writes)
        self.ninst += 1

    def _all_toks(self):
        toks = [('D', s, v) for (s, v) in self.dsem if v > 0]
        for e in ['pe', 'act', 'dve', 'pool']:
            if self.cnt[e] > 0:
                toks.append(('E', e, self.cnt[e]))
        return toks

    def barrier(self):
        toks = self._all_toks()
        for e in self.names:
            self.pe_force = True
            self._emit_waits(e, toks)
            self.pe_force = False

    def finish(self):
        self._emit_waits('sp', self._all_toks())
        nc = self.nc
        need = {e: set() for e in self.names}
        for e in self.names:
            for ent in self.prog[e]:
                if ent[0] == 'w' and ent[1][0] == 'E':
                    need[ent[1][1]].add(ent[1][2])
        sig = {}
        sems = {}
        self.nsem = 0
        for p in self.names:
            for r, idx in enumerate(sorted(need[p])):
                ep = r // self.SEM_LIMIT
                if (p, ep) not in sems:
                    sems[(p, ep)] = self.es.enter_context(nc.semaphore(f"s{p}{ep}"))
                    self.nsem += 1
                sig[(p, idx)] = (sems[(p, ep)], r % self.SEM_LIMIT + 1)

        def replay(e, eng):
            for ent in self.prog[e]:
                if ent[0] == 'w':
                    tok = ent[1]
                    if tok[0] == 'E':
                        sm, v = sig[(tok[1], tok[2])]
                        eng.wait_ge(sm, v)
                    else:
                        eng.wait_ge(tok[1], tok[2])
                elif ent[0] == 'op':
                    _, name, a, k, idx = ent
                    ins = getattr(eng, name)(*a, **k)
                    if (e, idx) in sig:
                        ins.then_inc(sig[(e, idx)][0], 1)
                else:
                    _, out, in_, s = ent
                    eng.dma_start(out=out, in_=in_).then_inc(s, 16)

        with nc.Block() as block:
            @block.tensor
            def _(eng):
                replay('pe', eng)

            @block.scalar
            def _(eng):
                replay('act', eng)

            @block.vector
            def _(eng):
                replay('dve', eng)

            @block.gpsimd
            def _(eng):
                replay('pool', eng)

            @block.sync
            def _(eng):
                replay('sp', eng)


class _Stop(Exception):
    pass


STOP = [0]


def CP(n):
    if STOP[0] == n:
        raise _Stop()


class Rot:
    def __init__(self, tiles, name):
        self.tiles = tiles
        self.name = name
        self.i = 0

    def next(self):
        j = self.i % len(self.tiles)
        self.i += 1
        return self.tiles[j], f"{self.name}{j}"


E_IN = 3592
O_IN = 3396
W_SPECS = None


def build(nblk=SEQ // TBP, do_sample=True):
    nc = bass.Bass("TRN2", target_bir_lowering=False)
    SEQ = nblk * TBP
    din = lambda n, s: nc.dram_tensor(n, list(s), F32, kind="ExternalInput").ap()
    dout = lambda n, s: nc.dram_tensor(n, list(s), F32, kind="ExternalOutput").ap()
    dscr = lambda n, s, dt: nc.dram_tensor(n, list(s), dt).ap()

    I = {}
    for n, s in [("xp", (SEQ, 1024)), ("xs", (TS, 1024)), ("c_conv", (30, 512)), ("c_fk", (PAST, 512)),
                 ("c_fv", (PAST, 512)), ("c_ff", (PAST, 8)), ("c_pool", (15, 512)), ("c_dk", (PAST, 512)),
                 ("c_dv", (PAST, 512)), ("c_di", (PAST, 64)),
                 ("e_w_in", (1024, E_IN)), ("e_b_f", (8, 1)), ("e_w_out", (1024, 1024)),
                 ("o_w_in", (1024, O_IN)), ("o_pool_w", (4, 128, 128)), ("o_w_out", (1024, 1024)),
                 ("pvec", (40, 1024)),
                 ("k_cmask", (NTP, 128, TBP)), ("k_adm", (NTP, 128, TBP)), ("k_cos", (SEQ + TS, 256)),
                 ("k_sin", (SEQ + TS, 256)), ("k_poolfix", (128, 4, 16))]:
        I[n] = din(n, s)
    O = {}
    for n, s in [("y_p", (SEQ, 1024)), ("y_s", (TS, 1024)), ("conv_p", (30, 512)), ("conv_s", (30, 512)),
                 ("fk_p", (SEQ, 512)), ("fk_s", (TS, 512)), ("fv_p", (SEQ, 512)), ("fv_s", (TS, 512)),
                 ("ff_p", (SEQ, 8)), ("ff_s", (TS, 8)), ("pool_p", (15, 512)), ("pool_s", (15, 512)),
                 ("dk_p", (SEQ, 512)), ("dk_s", (TS, 512)), ("dv_p", (SEQ, 512)), ("dv_s", (TS, 512)),
                 ("di_p", (SEQ, 64)), ("di_s", (TS, 64))]:
        O[n] = dout(n, s)

    NKS = PAST + TS
    SC = {}
    for sq, nk in [("P", SEQ), ("S", NKS)]:
        nkt = (nk + 127) // 128
        for l in (0, 1):
            SC[f"KT{l}{sq}"] = dscr(f"KT{l}{sq}", (4, 128, nk), BF16)
            SC[f"VS{l}{sq}"] = dscr(f"VS{l}{sq}", (4, 128, nkt, 130), BF16)
        SC[f"IK{sq}"] = dscr(f"IK{sq}", (64, nk), BF16)
    WS = {"e_in": dscr("ws_e_in", (29, 128, 8, 128), BF16), "o_in": dscr("ws_o_in", (27, 128, 8, 128), BF16),
          "e_oa": dscr("ws_e_oa", (8, 128, 4, 128), BF16), "e_oo": dscr("ws_e_oo", (8, 64, 8, 128), BF16),
          "o_oa": dscr("ws_o_oa", (8, 128, 4, 128), BF16), "o_oo": dscr("ws_o_oo", (8, 64, 8, 128), BF16),
          "cd": dscr("ws_cd", (4, 128, 31, 128), BF16)}

    with ExitStack() as es:
        S = Sched(nc, es)
        T = lambda name, shape, dt=F32: es.enter_context(nc.sbuf_tensor(name, list(shape), dt))
        banks = [es.enter_context(nc.psum_tensor(f"pb{i}", [128, 512], F32)) for i in range(8)]
        PS = Rot(banks[:6], "pb")
        PO = [(banks[6], "pb6"), (banks[7], "pb7")]
        op = S.op
        dma = S.dma
        dq = ['sp', 'pool']
        dqi = [0]

        def q():
            dqi[0] += 1
            return 'sp'

        def body():
            scores = T("scores", [128, 8192])
            notsel = T("notsel", [128, NTP, 8192], BF16)
            ident = T("ident", [128, 128])
            negI = T("negI", [128, 128], BF16)
            negI2 = T("negI2", [128, 2, 128], BF16)
            ones_f = T("ones_f", [128, 128])
            onesD = {512: T("ones512", [128, 128], BF16), 1024: T("ones1024", [128, 128], BF16)}
            onesrow = T("onesrow", [8, 512])
            op('pool', lambda e: e.memset(ident[:], 1.0), writes=['ident'])
            op('pool', lambda e: e.affine_select(out=ident[:], in_=ident[:], pattern=[[-1, 128]], compare_op=ALU.is_equal,
                                                 fill=0.0, base=0, channel_multiplier=1), reads=['ident'], writes=['ident'])
            op('dve', lambda e: e.tensor_scalar(out=negI[:], in0=ident[:], scalar1=NEGM, scalar2=None, op0=ALU.mult),
               reads=['ident'], writes=['negI'])
            for i2 in range(2):
                op('dve', lambda e: e.tensor_scalar(out=negI2[:, i2, :], in0=ident[:], scalar1=NEGM, scalar2=None,
                                                    op0=ALU.mult), reads=['ident'], writes=['negI'])
            op('pool', lambda e: e.memset(ones_f[:], 1.0), writes=['ones_f'])
            op('pool', lambda e: e.memset(onesD[512][:], 1.0 / 512), writes=['onesD'])
            op('pool', lambda e: e.memset(onesD[1024][:], 1.0 / 1024), writes=['onesD'])
            op('pool', lambda e: e.memset(onesrow[:], 1.0), writes=['onesrow'])

            cmask_f = scores[:, 4096:4096 + NTP * TBP].rearrange("p (a q) -> p a q", q=TBP)
            cmask = T("cmask", [128, NTP, TBP], BF16)
            cmask2 = T("cmask2", [128, NTP, 2, TBP], BF16)
            dma('sp', cmask_f, I["k_cmask"].rearrange("a p q -> p a q"), writes=['cmask_f'])
            op('dve', lambda e: e.tensor_copy(out=cmask[:], in_=cmask_f), reads=['cmask_f'], writes=['cmask'])
            for i2 in range(2):
                op('dve', lambda e: e.tensor_copy(out=cmask2[:, :, i2, :], in_=cmask_f), reads=['cmask_f'], writes=['cmask'])
            admn = T("admn", [128, NTP, TBP])
            dma('sp', admn[:], I["k_adm"].rearrange("a p q -> p a q"), writes=['admn'])
            poolfix = T("poolfix", [128, 4, 16])
            dma('sp', poolfix[:], I["k_poolfix"], writes=['poolfix'])
            bf_col = T("bf_col", [8, 1])
            dma('sp', bf_col[:], I["e_b_f"], writes=['bf_col'])
            nbf_col = T("nbf_col", [8, 1])
            op('dve', lambda e: e.tensor_scalar(out=nbf_col[:], in0=bf_col[:], scalar1=-1.0, scalar2=None, op0=ALU.mult),
               reads=['bf_col'], writes=['nbf_col'])
            pvt = scores[:, 2048:3072]
            pcol = T("pcol", [128, 8, 40])
            dma('sp', pvt[0:40, :], I["pvec"], writes=['pvt'])
            for c in range(8):
                ps, pk = PS.next()
                op('pe', lambda e, ps=ps, c=c: e.transpose(out=ps[:, 0:40], in_=pvt[0:40, c * 128:(c + 1) * 128],
                                                            identity=ident[0:40, 0:40]), reads=['pvt', 'ident'], writes=[pk])
                op('act', lambda e, ps=ps, c=c: e.copy(out=pcol[:, c, :], in_=ps[:, 0:40]), reads=[pk], writes=['pcol'])
            R_CB, R_CG, R_CBB, R_EG, R_EB, R_PS, R_OG, R_OB = 31, 32, 33, 34, 35, 36, 37, 38
            poolw_f = scores[:, 3072:3584].rearrange("p (g d) -> p g d", d=128)
            poolw = T("poolw", [128, 4, 128], BF16)
            dma('sp', poolw_f, I["o_pool_w"].rearrange("g c d -> c g d"), writes=['poolw_f'])
            op('dve', lambda e: e.tensor_copy(out=poolw[:], in_=poolw_f), reads=['poolw_f'], writes=['poolw'])

            CP(1)
            v8 = lambda ap_: ap_.rearrange("p (k c) -> p k c", c=128)
            stg_f = Rot([v8(scores[:, i * 1024:(i + 1) * 1024]) for i in range(2)], "stgf")
            stg_b = Rot([v8(notsel[:, 0, i * 1024:(i + 1) * 1024]) for i in range(2)], "stgb")
            ceng = ['dve', 'act', 'pool']
            cei = [0]

            def cast(out, in_, reads, writes):
                e = ceng[cei[0] % 3]
                cei[0] += 1
                if e == 'act':
                    op('act', lambda g: g.copy(out=out, in_=in_), reads=reads, writes=writes)
                else:
                    op(e, lambda g: g.tensor_copy(out=out, in_=in_), reads=reads, writes=writes)

            for wn, key, NC in [("e_w_in", "e_in", E_IN), ("o_w_in", "o_in", O_IN)]:
                for cc in range((NC + 127) // 128):
                    ncol = min(128, NC - cc * 128)
                    sf, sfk = stg_f.next()
                    sb, sbk = stg_b.next()
                    if ncol < 128:
                        op('pool', lambda e: e.memset(sf, 0.0), writes=[sfk])
                    dma('sp', sf[:, :, :ncol], I[wn][:, cc * 128:cc * 128 + ncol].rearrange("(k p) c -> p k c", p=128),
                        writes=[sfk])
                    cast(sb, sf, [sfk], [sbk])
                    dma('sp', WS[key][cc], sb, reads=[sbk], writes=['ws_' + key])
            for wn, ka, ko in [("e_w_out", "e_oa", "e_oo"), ("o_w_out", "o_oa", "o_oo")]:
                for cc in range(8):
                    sf, sfk = stg_f.next()
                    sb, sbk = stg_b.next()
                    dma('sp', sf[:, 0:4, :], I[wn][0:512, cc * 128:(cc + 1) * 128].rearrange("(k p) c -> p k c", p=128),
                        writes=[sfk])
                    cast(sb[:, 0:4, :], sf[:, 0:4, :], [sfk], [sbk])
                    dma('sp', WS[ka][cc], sb[:, 0:4, :], reads=[sbk], writes=['ws_' + ka])
                    sf, sfk = stg_f.next()
                    sb, sbk = stg_b.next()
                    dma('sp', sf[0:64, :, :], I[wn][512:1024, cc * 128:(cc + 1) * 128].rearrange("(h p) c -> p h c", p=64),
                        writes=[sfk])
                    cast(sb[0:64, :, :], sf[0:64, :, :], [sfk], [sbk])
                    dma('sp', WS[ko][cc], sb[0:64, :, :], reads=[sbk], writes=['ws_' + ko])
            cdst = notsel[:, 1, 0:31 * 128].rearrange("p (w c) -> p w c", c=128)
            for c in range(4):
                for w in range(31):
                    eng = 'dve' if w % 2 == 0 else 'pool'
                    op(eng, lambda e, c=c, w=w: e.tensor_scalar(out=cdst[:, w, :], in0=ident[:], scalar1=pcol[:, c, w:w + 1],
                                                                scalar2=None, op0=ALU.mult),
                       reads=['ident', 'pcol'], writes=['cdst'])
                dma('sp', WS["cd"][c], cdst, reads=['cdst'], writes=['ws_cd'])

            CP(2)
            S.barrier()
            xT_f = T("xT_f", [128, 8, TBP])
            xT_b = T("xT_b", [128, 8, TBP], BF16)
            zT = T("zT", [128, 8, TBP])
            zflat = zT[:].rearrange("p c t -> p (c t)")
            xtk = lambda t, a, b: zflat[:, t * 1024 + a:t * 1024 + b]
            xtokb = T("xtokb", [128, NTP, 1024])
            wnar = Rot([T(f"wnar{i}", [128, 8, 128], BF16) for i in range(4)], "wnar")
            wwide = Rot([T(f"wwide{i}", [128, 8, 512], BF16) for i in range(2)], "wwide")
            cdw = Rot([T(f"cdw{i}", [128, 31, 128], BF16) for i in range(1)], "cdw")
            a_bf = {sq: T(f"a_bf{sq}", [128, 4, 30 + (TBP if sq == "P" else TS)], BF16) for sq in "PS"}
            cv_f = {sq: T(f"cv_f{sq}", [128, 4, 15 + (TBP if sq == "P" else TS)]) for sq in "PS"}
            sg = T("sg", [128, TBP])
            gA = T("gA", [128, 4, TBP], BF16)
            gB = T("gB", [64, 8, TBP], BF16)
            QTb = {"P": T("QTbP", [128, 4, 2 * TBP], BF16), "S": T("QTbS", [128, 4, 2 * TS], BF16)}
            KTn = T("KTn", [128, 4, TBP], BF16)
            uA = T("uA", [128, 4, TBP], BF16)
            uO = T("uO", [64, 8, TBP], BF16)
            Lrow = T("Lrow", [8, 512])
            CKrow = T("CKrow", [8, 512])
            ckcar = {sq: T(f"ckcar{sq}", [8, 1]) for sq in "PS"}
            CKT = {sq: T(f"CKT{sq}", [128, 64 if sq == "P" else 17, 8]) for sq in "PS"}
            biasT = T("biasT", [128, 64, 8])
            cref = T("cref", [128, 8])
            tmr = Rot([T(f"tmr{i}", [128, 512]) for i in range(3)], "tmr")
            tmb = Rot([T(f"tmb{i}", [128, 512], BF16) for i in range(3)], "tmb")
            vb = Rot([T(f"vb{i}", [128, 4, 130], BF16) for i in range(2)], "vb")
            kbuf = Rot([T(f"kbuf{i}", [128, KCH], BF16) for i in range(2)], "kbuf")
            vbuf = Rot([T(f"vbuf{i}", [128, KCH // 128, 130], BF16) for i in range(2)], "vbuf")
            osb = Rot([T(f"osb{i}", [65, TBP]) for i in range(2)], "osb")
            stat = {n: T("st_" + n, [128, TBP]) for n in ["mean", "m2", "rstd"]}
            ropeC = T("ropeC", [128, 256])
            ropeS = T("ropeS", [128, 256])
            rp = Rot([T(f"rp{i}", [128, 512]) for i in range(2)], "rp")
            rq = Rot([T(f"rq{i}", [128, 256]) for i in range(2)], "rq")
            IQT = T("IQT", [64, 4, TBP], BF16)
            IKn = T("IKn", [64, TBP], BF16)
            wq = T("wq", [128, NTP, 4])
            bs = {n: T("bs_" + n, [128, 8]) for n in ["hi", "lo", "w", "mid", "cnt", "ge", "mn", "sa", "nm"]}
            bs["wt"] = T("bs_wt", [128, 16])
            pow2 = T("pow2", [128, 16])
            junkA = wwide.tiles[0][:].rearrange("p k c -> p (k c)")
            lf = T("lf_s", [128, 16, 8])

            for k2 in range(16):
                op('pool', lambda e, k2=k2: e.memset(pow2[:, k2:k2 + 1], 0.5 ** (k2 + 1)), writes=['pow2'])
            for sq in "PS":
                op('pool', lambda e, sq=sq: e.memset(QTb[sq][:], 0.0), writes=['QT'])
                op('pool', lambda e, sq=sq: e.memset(a_bf[sq][:], 0.0), writes=['a_bf' + sq])
                op('pool', lambda e, sq=sq: e.memset(cv_f[sq][:], 0.0), writes=['cv_f' + sq])
                op('pool', lambda e, sq=sq: e.memset(ckcar[sq][:], 0.0), writes=['ckcar' + sq])
                op('pool', lambda e, sq=sq: e.memset(CKT[sq][:], 0.0), writes=['CKT' + sq])
            for (vt, vk) in [(vb.tiles[0], 'vb0'), (vb.tiles[1], 'vb1')]:
                op('pool', lambda e, vt=vt: e.memset(vt[:], 1.0), writes=[vk])

            CP(30)
            wcache = {}

            def load_w(key, cc):
                if wcache.get('n') == (key, cc):
                    return wcache['t']
                w, wk = wnar.next()
                dma(q(), w[:], WS[key][cc], reads=['ws_' + key], writes=[wk])
                wcache['n'] = (key, cc)
                wcache['t'] = (w, wk)
                return w, wk

            def proj_fm(key, cc, lo, ncol, TB, evac):
                w, wk = load_w(key, cc)
                ps, pk = PS.next()
                for kc in range(8):
                    op('pe', lambda e, kc=kc: e.matmul(ps[:ncol, :TB], lhsT=w[:, kc, lo:lo + ncol], rhs=xT_b[:, kc, :TB],
                                                        start=(kc == 0), stop=(kc == 7)), reads=[wk, 'xT_b'], writes=[pk])
                evac(ps, pk)

            def load_wide(key, cc0, ncols):
                w, wk = wwide.next()
                for j in range((ncols + 127) // 128):
                    n = min(128, ncols - j * 128)
                    dma(q(), w[:, :, j * 128:j * 128 + n], WS[key][cc0 + j, :, :, :n], reads=['ws_' + key], writes=[wk])
                return w, wk

            def proj_tm(w, wk, ncols, t, rows):
                ps, pk = PS.next()
                for kc in range(8):
                    op('pe', lambda e, kc=kc: e.matmul(ps[:rows, :ncols], lhsT=xT_b[:, kc, t * 128:t * 128 + rows],
                                                        rhs=w[:, kc, :ncols], start=(kc == 0), stop=(kc == 7)),
                       reads=[wk, 'xT_b'], writes=[pk])
                return ps, pk

            def ln_fm(C, D, rg, rb, func, TB, outs):
                mps, mk = PS.next()
                sps, sk = PS.next()
                for c in range(C):
                    zb, zbk = tmb.next()
                    op('dve', lambda e, c=c, zb=zb: e.tensor_copy(out=zb[:, :TB], in_=zT[:, c, :TB]), reads=['zT'], writes=[zbk])
                    op('pe', lambda e, c=c, zb=zb: e.matmul(mps[:, :TB], lhsT=onesD[D][:], rhs=zb[:, :TB], start=(c == 0),
                                                             stop=(c == C - 1)), reads=[zbk, 'onesD'], writes=[mk])
                    zs, zsk = tmb.next()
                    op('act', lambda e, c=c, zs=zs: e.activation(out=zs[:, :TB], in_=zT[:, c, :TB], func=AF.Square),
                       reads=['zT'], writes=[zsk])
                    op('pe', lambda e, c=c, zs=zs: e.matmul(sps[:, :TB], lhsT=onesD[D][:], rhs=zs[:, :TB], start=(c == 0),
                                                             stop=(c == C - 1)), reads=[zsk, 'onesD'], writes=[sk])
                mean, m2, rstd = stat["mean"], stat["m2"], stat["rstd"]
                op('act', lambda e: e.copy(out=mean[:, :TB], in_=mps[:, :TB]), reads=[mk], writes=['st_mean'])
                op('dve', lambda e: e.tensor_tensor(out=m2[:, :TB], in0=mean[:, :TB], in1=mean[:, :TB], op=ALU.mult),
                   reads=['st_mean'], writes=['st_m2'])
                op('dve', lambda e: e.tensor_tensor(out=m2[:, :TB], in0=sps[:, :TB], in1=m2[:, :TB], op=ALU.subtract),
                   reads=[sk, 'st_m2'], writes=['st_m2'])
                op('dve', lambda e: e.tensor_scalar(out=m2[:, :TB], in0=m2[:, :TB], scalar1=1e-5, scalar2=None, op0=ALU.add),
                   reads=['st_m2'], writes=['st_m2'])
                op('act', lambda e: e.activation(out=rstd[:, :TB], in_=m2[:, :TB], func=AF.Sqrt), reads=['st_m2'],
                   writes=['st_rstd'])
                op('dve', lambda e: e.reciprocal(out=rstd[:, :TB], in_=rstd[:, :TB]), reads=['st_rstd'], writes=['st_rstd'])
                bc = lambda tl: tl[:, :TB].unsqueeze(1).to_broadcast([128, C, TB])
                op('pool', lambda e: e.tensor_tensor(out=zT[:, :C, :TB], in0=zT[:, :C, :TB], in1=bc(mean), op=ALU.subtract),
                   reads=['zT', 'st_mean'], writes=['zT'])
                op('dve', lambda e: e.tensor_tensor(out=zT[:, :C, :TB], in0=zT[:, :C, :TB], in1=bc(rstd), op=ALU.mult),
                   reads=['zT', 'st_rstd'], writes=['zT'])
                ot, ok = outs[0]
                for c in range(C):
                    op('act', lambda e, c=c: e.activation(out=ot[:, c, :TB], in_=zT[:, c, :TB], func=func,
                                                          bias=pcol[:, c, rb:rb + 1], scale=pcol[:, c, rg:rg + 1]),
                       reads=['zT', 'pcol'], writes=[ok])
                for (o2, o2k) in outs[1:]:
                    op('dve', lambda e: e.tensor_copy(out=o2[:, :C, :TB], in_=ot[:, :C, :TB]), reads=[ok], writes=[o2k])

            def out_proj_ln(ka, ko, rg, rb, TB):
                for cc in range(8):
                    wa, wak = wnar.next()
                    dma(q(), wa[:, 0:4, :], WS[ka][cc], reads=['ws_' + ka], writes=[wak])
                    wo, wok = wnar.next()
                    dma(q(), wo[0:64, :, :], WS[ko][cc], reads=['ws_' + ko], writes=[wok])
                    CP(60 + cc)
                    ps, pk = PS.next()
                    for kc in range(4):
                        op('pe', lambda e, kc=kc: e.matmul(ps[:, :TB], lhsT=wa[:, kc, :], rhs=uA[:, kc, :TB], start=(kc == 0),
                                                            stop=False), reads=[wak, 'uA'], writes=[pk])
                    CP(70 + cc)
                    for h in range(8):
                        op('pe', lambda e, h=h: e.matmul(ps[:, :TB], lhsT=wo[0:64, h, :], rhs=uO[0:64, h, :TB], start=False,
                                                          stop=(h == 7)), reads=[wok, 'uO'], writes=[pk])
                    CP(80 + cc)
                    op('dve', lambda e, cc=cc: e.scalar_tensor_tensor(out=zT[:, cc, :TB], in0=xT_f[:, cc, :TB], scalar=ALPHA,
                                                                      in1=ps[:, :TB], op0=ALU.mult, op1=ALU.add),
                       reads=[pk, 'xT_f'], writes=['zT'])
                    CP(44 + cc)
                wcache.clear()
                CP(40)
                ln_fm(8, 1024, rg, rb, AF.Identity, TB, [(xT_f, 'xT_f'), (xT_b, 'xT_b')])

            def attend(l, sq, TB, pos0, fox, nqt):
                KTS, VSS = SC[f"KT{l}{sq}"], SC[f"VS{l}{sq}"]
                nk = pos0 + TB
                nktot = (nk + 127) // 128
                tpc = KCH // 128
                LA = 3
                for pair in range(4):
                    chunk = {}
                    st = {}

                    def get_chunk(ch):
                        if ch not in chunk:
                            k0 = ch * KCH
                            n = min(KCH, nk - k0)
                            nkt = (n + 127) // 128
                            kb, kbk = kbuf.next()
                            vv, vvk = vbuf.next()
                            dma(q(), kb[:, :n], KTS[pair, :, k0:k0 + n], reads=[f"KT{l}{sq}"], writes=[kbk])
                            nfull = n // 128
                            if nfull > 0:
                                dma(q(), vv[:, :nfull, :], VSS[pair, :, k0 // 128:k0 // 128 + nfull, :], reads=[f"VS{l}{sq}"],
                                    writes=[vvk])
                            if nfull < nkt:
                                rr = n - nfull * 128
                                dma(q(), vv[:rr, nfull, :], VSS[pair, :rr, k0 // 128 + nfull, :], reads=[f"VS{l}{sq}"],
                                    writes=[vvk])
                            chunk[ch] = (kb, kbk, vv, vvk)
                        return chunk[ch]

                    def stA(kt):
                        ch, j = kt // tpc, kt % tpc
                        ks = min(128, nk - kt * 128)
                        kb, kbk, vv, vvk = get_chunk(ch)
                        ps, pk = PS.next()
                        diag = (kt * 128 >= pos0)
                        masked = (not fox) or diag
                        op('pe', lambda e: e.matmul(ps[:ks, :2 * TB], lhsT=kb[:, j * 128:j * 128 + ks], rhs=QTb[sq][:, pair, :2 * TB],
                                                    start=True, stop=not masked), reads=[kbk, 'QT'], writes=[pk])
                        if fox and diag:
                            kl = (kt * 128 - pos0) // 128
                            if TB == TBP:
                                op('pe', lambda e: e.matmul(ps[:ks, :2 * TB], lhsT=negI[:ks, :ks],
                                                            rhs=cmask2[:ks, kl, :, :].rearrange("p h q -> p (h q)"),
                                                            start=False, stop=True), reads=['negI', 'cmask'], writes=[pk])
                            else:
                                for hh in range(2):
                                    op('pe', lambda e: e.matmul(
                                        ps[:ks, hh * TB:hh * TB + TB], lhsT=negI[:ks, :ks], rhs=cmask[:ks, kl, :TB], start=False,
                                        stop=(hh == 1)), reads=['negI', 'cmask'], writes=[pk])
                        if not fox:
                            for qt in range(nqt):
                                qs = min(128, TB - qt * 128)
                                for hh in range(2):
                                    op('pe', lambda e: e.matmul(
                                        ps[:ks, hh * TB + qt * 128:hh * TB + qt * 128 + qs],
                                        lhsT=notsel[:qs, qt, kt * 128:kt * 128 + ks], rhs=negI[:qs, :qs], start=False,
                                        stop=(qt == nqt - 1 and hh == 1)), reads=['negI', 'notsel'], writes=[pk])
                        st[kt] = (ks, j, vv, vvk, ps, pk)

                    def stBC(kt):
                        ks, j, vv, vvk, ps, pk = st.pop(kt)
                        pt, ptk = tmb.next()
                        if fox:
                            for hh in range(2):
                                h = 2 * pair + hh
                                op('act', lambda e: e.activation(
                                    out=pt[:ks, hh * TB:hh * TB + TB], in_=ps[:ks, hh * TB:hh * TB + TB], func=AF.Exp,
                                    bias=biasT[:ks, kt, h:h + 1], scale=0.125), reads=[pk, 'biasT'], writes=[ptk])
                        else:
                            op('act', lambda e: e.activation(out=pt[:ks, :2 * TB], in_=ps[:ks, :2 * TB], func=AF.Exp,
                                                             scale=0.125), reads=[pk], writes=[ptk])
                        for hh in range(2):
                            po, pok = PO[hh]
                            op('pe', lambda e: e.matmul(
                                po[:65, :TB], lhsT=vv[:ks, j, 65 * hh:65 * hh + 65], rhs=pt[:ks, hh * TB:hh * TB + TB],
                                start=(kt == 0), stop=(kt == nktot - 1)), reads=[vvk, ptk], writes=[pok])

                    for i in range(nktot + LA):
                        if i < nktot:
                            stA(i)
                        if i >= LA:
                            stBC(i - LA)
                    for hh in range(2):
                        h = 2 * pair + hh
                        po, pok = PO[hh]
                        ob, obk = osb.next()
                        op('act', lambda e: e.copy(out=ob[:65, :TB], in_=po[:65, :TB]), reads=[pok], writes=[obk])
                        ps, pk = PS.next()
                        op('pe', lambda e: e.matmul(ps[:64, :TB], lhsT=ones_f[64:65, 0:64], rhs=ob[64:65, :TB],
                                                    start=True, stop=True), reads=[obk, 'ones_f'], writes=[pk])
                        t1, t1k = tmr.next()
                        op('dve', lambda e: e.reciprocal(out=t1[:64, :TB], in_=ps[:64, :TB]), reads=[pk], writes=[t1k])
                        op('dve', lambda e: e.tensor_tensor(out=t1[:64, :TB], in0=t1[:64, :TB], in1=ob[:64, :TB],
                                                            op=ALU.mult), reads=[t1k, obk], writes=[t1k])
                        op('pool', lambda e: e.tensor_tensor(out=uO[:64, h, :TB], in0=t1[:64, :TB], in1=gB[:64, h, :TB],
                                                             op=ALU.mult), reads=[t1k, 'gB'], writes=['uO'])

            def transposes_to(dst_fn, src, srck, rows, nchunks, cw, reads_extra=()):
                for j in range(nchunks):
                    ps, pk = PS.next()
                    op('pe', lambda e, j=j, ps=ps: e.transpose(out=ps[:cw, :rows], in_=src[:rows, j * cw:(j + 1) * cw],
                                                                identity=ident[:rows, :rows]),
                       reads=[srck, 'ident'] + list(reads_extra), writes=[pk])
                    dst_fn(j, ps, pk)

            def layer0(sq, x_ap, TB, pos0, outn, final):
                nt = (TB + 127) // 128
                rows_of = lambda t: min(128, TB - t * 128)
                sfx = "_p" if sq == "P" else "_s"
                op0_ = pos0 if sq == "P" else 0
                CP(31)
                for kc in range(8):
                    ps, pk = PS.next()
                    for t in range(nt):
                        r = rows_of(t)
                        op('pe', lambda e, kc=kc, t=t, r=r, ps=ps: e.transpose(
                            out=ps[:, t * 128:t * 128 + r], in_=xtokb[:r, t, kc * 128:(kc + 1) * 128], identity=ident[:r, :r]),
                           reads=['xtokb', 'ident'], writes=[pk])
                    op('act', lambda e, kc=kc, ps=ps: e.copy(out=xT_f[:, kc, :TB], in_=ps[:, :TB]), reads=[pk], writes=['xT_f'])
                    op('dve', lambda e, kc=kc, ps=ps: e.tensor_copy(out=xT_b[:, kc, :TB], in_=ps[:, :TB]), reads=[pk],
                       writes=['xT_b'])
                CP(3)
                ab = a_bf[sq]
                abk = 'a_bf' + sq
                wkv = [load_wide("e_in", 16, 512), load_wide("e_in", 20, 512)]
                for c in range(4):
                    proj_fm("e_in", 4 + c, 0, 128, TB,
                            lambda ps, pk: op('act', lambda e: e.activation(out=sg[:, :TB], in_=ps[:, :TB], func=AF.Sigmoid),
                                              reads=[pk], writes=['sg']))
                    proj_fm("e_in", c, 0, 128, TB,
                            lambda ps, pk, c=c: op('dve', lambda e: e.tensor_tensor(out=zT[:, 4 + c, :TB], in0=ps[:, :TB],
                                                                                    in1=sg[:, :TB], op=ALU.mult),
                                                   reads=[pk, 'sg'], writes=['zT']))
                    op('pool', lambda e, c=c: e.tensor_copy(out=ab[:, c, 30:30 + TB], in_=zT[:, 4 + c, :TB]), reads=['zT'],
                       writes=[abk])
                if final:
                    assert TB >= 30
                    for c in range(4):
                        ps, pk = PS.next()
                        op('pe', lambda e, c=c, ps=ps: e.transpose(out=ps[:30, 0:128], in_=zT[:, 4 + c, TB - 30:TB],
                                                                    identity=ident[:, :]), reads=['zT', 'ident'], writes=[pk])
                        t1, t1k = tmr.next()
                        op('act', lambda e, t1=t1, ps=ps: e.copy(out=t1[:30, 0:128], in_=ps[:30, 0:128]), reads=[pk], writes=[t1k])
                        dma(q(), O["conv" + sfx][:, c * 128:(c + 1) * 128], t1[:30, 0:128], reads=[t1k])
                CP(4)
                for c in range(4):
                    proj_fm("e_in", 8 + c, 0, 128, TB,
                            lambda ps, pk, c=c: op('act', lambda e: e.activation(out=gA[:, c, :TB], in_=ps[:, :TB], func=AF.Silu),
                                                   reads=[pk], writes=['gA']))
                for h in range(8):
                    proj_fm("e_in", 24 + h // 2, 64 * (h % 2), 64, TB,
                            lambda ps, pk, h=h: op('act', lambda e: e.activation(out=gB[:64, h, :TB], in_=ps[:64, :TB],
                                                                                 func=AF.Silu), reads=[pk], writes=['gB']))
                for c in range(4):
                    proj_fm("e_in", 12 + c, 0, 128, TB,
                            lambda ps, pk, c=c: (op('dve', lambda e: e.tensor_copy(out=QTb[sq][0:64, c, 0:TB], in_=ps[0:64, :TB]),
                                                    reads=[pk], writes=['QT']),
                                                 op('dve', lambda e: e.tensor_copy(out=QTb[sq][64:128, c, TB:2 * TB],
                                                                                   in_=ps[64:128, :TB]), reads=[pk], writes=['QT'])))
                for c in range(4):
                    proj_fm("e_in", 16 + c, 0, 128, TB,
                            lambda ps, pk, c=c: op('act', lambda e: e.copy(out=KTn[:, c, :TB], in_=ps[:, :TB]), reads=[pk],
                                                   writes=['KTn']))
                dma(q(), SC[f"KT0{sq}"][:, :, pos0:pos0 + TB].rearrange("a p n -> p a n"), KTn[:, :, :TB], reads=['KTn'],
                    writes=[f"KT0{sq}"])
                CP(5)
                def ev_f(ps, pk):
                    op('act', lambda e: e.activation(out=Lrow[:8, :TB], in_=ps[:8, :TB], func=AF.Exp, bias=nbf_col[:8, 0:1],
                                                     scale=-1.0), reads=[pk, 'nbf_col'], writes=['Lrow'])
                    op('act', lambda e: e.activation(out=Lrow[:8, :TB], in_=Lrow[:8, :TB], func=AF.Ln, bias=1.0),
                       reads=['Lrow'], writes=['Lrow'])
                proj_fm("e_in", 28, 0, 8, TB, ev_f)
                wcache.clear()
                op('dve', lambda e: e.tensor_tensor_scan(out=CKrow[:8, :TB], data0=onesrow[:8, :TB], data1=Lrow[:8, :TB],
                                                         initial=ckcar[sq][:8, 0:1], op0=ALU.mult, op1=ALU.add),
                   reads=['Lrow', 'onesrow', 'ckcar' + sq], writes=['CKrow'])
                op('dve', lambda e: e.tensor_copy(out=ckcar[sq][:8, 0:1], in_=CKrow[:8, TB - 1:TB]), reads=['CKrow'],
                   writes=['ckcar' + sq])
                ckt = CKT[sq]
                cktk = 'CKT' + sq
                for t in range(nt):
                    r = rows_of(t)
                    kt = pos0 // 128 + t
                    ps, pk = PS.next()
                    op('pe', lambda e, t=t, r=r, ps=ps: e.transpose(out=ps[:r, 0:8], in_=CKrow[:8, t * 128:t * 128 + r],
                                                                    identity=ident[:8, :8]), reads=['CKrow', 'ident'], writes=[pk])
                    op('pe', lambda e, t=t, r=r, ps=ps: e.transpose(out=ps[:r, 8:16], in_=Lrow[:8, t * 128:t * 128 + r],
                                                                    identity=ident[:8, :8]), reads=['Lrow', 'ident'], writes=[pk])
                    op('dve', lambda e, r=r, kt=kt, ps=ps: e.tensor_copy(out=ckt[:r, kt, :], in_=ps[:r, 0:8]), reads=[pk],
                       writes=[cktk])
                    t1, t1k = tmr.next()
                    op('act', lambda e, r=r, t1=t1, ps=ps: e.activation(out=t1[:r, 0:8], in_=ps[:r, 8:16], func=AF.Copy,
                                                                        scale=-1.0), reads=[pk], writes=[t1k])
                    dma('act', O["ff" + sfx][op0_ + t * 128:op0_ + t * 128 + r, :], t1[:r, 0:8], reads=[t1k])
                refp = pos0 + (TB // 2 if TB >= 256 else 0)
                ktm = refp // 128
                rowm = refp % 128
                assert rowm in (0, 32, 64)
                ps, pk = PS.next()
                op('pe', lambda e, ps=ps: e.matmul(ps[:, 0:8], lhsT=ones_f[rowm:rowm + 1, :], rhs=ckt[rowm:rowm + 1, ktm, :],
                                                   start=True, stop=True), reads=[cktk, 'ones_f'], writes=[pk])
                op('act', lambda e, ps=ps: e.copy(out=cref[:, :], in_=ps[:, 0:8]), reads=[pk], writes=['cref'])
                nktot = (pos0 + TB + 127) // 128
                for h in range(8):
                    op('dve' if h % 2 else 'pool', lambda e, h=h: e.tensor_scalar(
                        out=biasT[:, :nktot, h], in0=ckt[:, :nktot, h], scalar1=cref[:, h:h + 1], scalar2=None,
                        op0=ALU.subtract), reads=[cktk, 'cref'], writes=['biasT'])
                CP(6)
                for gi_, (cc0, name) in enumerate([(16, "fk"), (20, "fv")]):
                    w, wk = wkv[gi_]
                    for t in range(nt):
                        r = rows_of(t)
                        ps, pk = proj_tm(w, wk, 512, t, r)
                        t1, t1k = tmr.next()
                        op('act', lambda e, r=r, t1=t1, ps=ps: e.copy(out=t1[:r, :], in_=ps[:r, :]), reads=[pk], writes=[t1k])
                        dma('act', O[name + sfx][op0_ + t * 128:op0_ + t * 128 + r, :], t1[:r, :], reads=[t1k])
                        if name == "fv":
                            v, vk = vb.next()
                            for pr in range(4):
                                op('dve' if pr % 2 else 'pool', lambda e, r=r, v=v, t1=t1, pr=pr: e.tensor_copy(
                                    out=v[:r, pr, :].rearrange("p (h d) -> p h d", d=65)[:, :, 0:64],
                                    in_=t1[:r, pr * 128:(pr + 1) * 128].rearrange("p (h d) -> p h d", d=64)),
                                   reads=[t1k], writes=[vk])
                            dma(q(), SC[f"VS0{sq}"][:, :r, pos0 // 128 + t, :].rearrange("a p d -> p a d"), v[:r, :, :],
                                reads=[vk], writes=[f"VS0{sq}"])
                CP(7)
                for c in range(4):
                    cw_, cwk = cdw.next()
                    dma(q(), cw_[:], WS["cd"][c], reads=['ws_cd'], writes=[cwk])
                    ps, pk = PS.next()
                    for w in range(31):
                        op('pe', lambda e, c=c, w=w, cw_=cw_, ps=ps: e.matmul(ps[:, :TB], lhsT=cw_[:, w, :], rhs=ab[:, c, w:w + TB],
                                                                             start=(w == 0), stop=(w == 30)),
                           reads=[cwk, abk], writes=[pk])
                    op('act', lambda e, c=c, ps=ps: e.activation(out=zT[:, c, :TB], in_=ps[:, :TB], func=AF.Identity,
                                                                 bias=pcol[:, c, R_CB:R_CB + 1]), reads=[pk, 'pcol'], writes=['zT'])
                t1, t1k = tmb.next()
                op('pool', lambda e, t1=t1: e.tensor_copy(out=t1[:, 0:120].rearrange("p (c w) -> p c w", w=30),
                                                          in_=ab[:, :, TB:TB + 30]), reads=[abk], writes=[t1k])
                op('pool', lambda e, t1=t1: e.tensor_copy(out=ab[:, :, 0:30],
                                                          in_=t1[:, 0:120].rearrange("p (c w) -> p c w", w=30)),
                   reads=[t1k], writes=[abk])
                ln_fm(4, 512, R_CG, R_CBB, AF.Silu, TB, [(zT, 'zT')])
                for c in range(4):
                    op('pool', lambda e, c=c: e.tensor_tensor(out=uA[:, c, :TB], in0=zT[:, c, :TB], in1=gA[:, c, :TB],
                                                              op=ALU.mult), reads=['zT', 'gA'], writes=['uA'])
                CP(8)
                attend(0, sq, TB, pos0, True, nt)
                CP(9)
                out_proj_ln("e_oa", "e_oo", R_EG, R_EB, TB)

            def rope(eng, dst, dstk, src, srck, r, ncol):
                H = ncol // 64
                v3 = lambda ap_, a, b: ap_.rearrange("p (h d) -> p h d", d=64)[:, :, a:b]
                c3 = lambda tl: tl[:r, 0:32 * H].rearrange("p (h d) -> p h d", d=32)
                ta, tak = rq.next()
                tb_, tbk = rq.next()
                x1, x2 = v3(src[:r, :ncol], 0, 32), v3(src[:r, :ncol], 32, 64)
                o1, o2 = v3(dst[:r, :ncol], 0, 32), v3(dst[:r, :ncol], 32, 64)
                a3, b3 = c3(ta), c3(tb_)
                rd = [srck, 'ropeC', 'ropeS']
                op(eng, lambda e: e.tensor_tensor(out=a3, in0=x1, in1=c3(ropeC), op=ALU.mult), reads=rd, writes=[tak])
                op(eng, lambda e: e.tensor_tensor(out=b3, in0=x2, in1=c3(ropeS), op=ALU.mult), reads=rd, writes=[tbk])
                op(eng, lambda e: e.tensor_tensor(out=o1, in0=a3, in1=b3, op=ALU.subtract), reads=[tak, tbk], writes=[dstk])
                op(eng, lambda e: e.tensor_tensor(out=a3, in0=x2, in1=c3(ropeC), op=ALU.mult), reads=rd + [dstk], writes=[tak])
                op(eng, lambda e: e.tensor_tensor(out=b3, in0=x1, in1=c3(ropeS), op=ALU.mult), reads=rd, writes=[tbk])
                op(eng, lambda e: e.tensor_tensor(out=o2, in0=a3, in1=b3, op=ALU.add), reads=[tak, tbk], writes=[dstk])

            def layer1(sq, TB, pos0, final):
                nt = (TB + 127) // 128
                rows_of = lambda t: min(128, TB - t * 128)
                sfx = "_p" if sq == "P" else "_s"
                op0_ = pos0 if sq == "P" else 0
                tab0 = pos0 if sq == "P" else SEQ
                cv = cv_f[sq]
                cvk = 'cv_f' + sq
                work = []
                for c in range(4):
                    work.append(lambda c=c: proj_fm("o_in", c, 0, 128, TB,
                                lambda ps, pk, c=c: op('act', lambda e: e.copy(out=cv[:, c, 15:15 + TB], in_=ps[:, :TB]), reads=[pk],
                                                       writes=[cvk])))
                for c in range(4):
                    work.append(lambda c=c: proj_fm("o_in", 4 + c, 0, 128, TB,
                                lambda ps, pk, c=c: op('act', lambda e: e.activation(out=gA[:, c, :TB], in_=ps[:, :TB], func=AF.Silu),
                                                       reads=[pk], writes=['gA'])))
                for h in range(8):
                    work.append(lambda h=h: proj_fm("o_in", 20 + h // 2, 64 * (h % 2), 64, TB,
                                lambda ps, pk, h=h: op('act', lambda e: e.activation(out=gB[:64, h, :TB], in_=ps[:64, :TB],
                                                                                     func=AF.Silu), reads=[pk], writes=['gB'])))
                def _fin():
                    wcache.clear()
                    if final:
                        for c in range(4):
                            ps, pk = PS.next()
                            op('pe', lambda e, c=c, ps=ps: e.transpose(out=ps[:15, 0:128], in_=cv[:, c, TB:TB + 15],
                                                                        identity=ident[:, :]), reads=[cvk, 'ident'], writes=[pk])
                            t1, t1k = tmr.next()
                            op('act', lambda e, t1=t1, ps=ps: e.copy(out=t1[:15, 0:128], in_=ps[:15, 0:128]), reads=[pk], writes=[t1k])
                            dma(q(), O["pool" + sfx][:, c * 128:(c + 1) * 128], t1[:15, 0:128], reads=[t1k])
                work.append(_fin)
                W_ = 15 + TB
                def _pool(g):
                    src = cv[:, g, :]
                    srck = cvk
                    pp = [tmr.next(), tmr.next()]
                    for s in range(g + 1):
                        sh = 1 << s
                        dst_, dstk_ = pp[s % 2]
                        eng = 'pool'
                        op(eng, lambda e, src=src, dst_=dst_, sh=sh: e.tensor_tensor(
                            out=dst_[:, sh:W_], in0=src[:, sh:W_], in1=src[:, 0:W_ - sh], op=ALU.add),
                           reads=[srck], writes=[dstk_])
                        src = dst_
                        srck = dstk_
                    wdw = 2 << g
                    t1, t1k = tmr.next()
                    op('pool', lambda e, src=src, t1=t1, wdw=wdw: e.tensor_scalar(out=t1[:, :TB], in0=src[:, 15:15 + TB],
                                                                                  scalar1=1.0 / wdw, scalar2=None, op0=ALU.mult),
                       reads=[srck], writes=[t1k])
                    if sq == "P" and pos0 == 0:
                        op('pool', lambda e, src=src, t1=t1, g=g: e.tensor_tensor(out=t1[:, 0:16], in0=src[:, 15:31],
                                                                                 in1=poolfix[:, g, :], op=ALU.mult),
                           reads=[srck, 'poolfix', t1k], writes=[t1k])
                    pb_, pbk = tmb.next()
                    op('pool', lambda e, t1=t1, pb_=pb_, g=g: e.tensor_tensor(out=pb_[:, :TB], in0=t1[:, :TB],
                                                                             in1=cv[:, g, 15:15 + TB], op=ALU.subtract),
                       reads=[t1k, cvk], writes=[pbk])
                    ps, pk = PS.next()
                    op('pe', lambda e, g=g, pb_=pb_, ps=ps: e.matmul(ps[:, :TB], lhsT=poolw[:, g, :], rhs=pb_[:, :TB], start=True,
                                                                      stop=True), reads=['poolw', pbk], writes=[pk])
                    t2, t2k = tmr.next()
                    op('act', lambda e, g=g, t2=t2, ps=ps: e.activation(out=t2[:, :TB], in_=ps[:, :TB], func=AF.Copy,
                                                                        scale=pcol[:, g, R_PS:R_PS + 1]), reads=[pk, 'pcol'],
                       writes=[t2k])
                    op('pool', lambda e, g=g, t2=t2: e.tensor_tensor(out=uA[:, g, :TB], in0=t2[:, :TB], in1=gA[:, g, :TB],
                                                                     op=ALU.mult), reads=[t2k, 'gA'], writes=['uA'])
                for g in range(4):
                    work.append(lambda g=g: _pool(g))
                def _carry():
                    t1, t1k = tmr.next()
                    op('pool', lambda e, t1=t1: e.tensor_copy(out=t1[:, 0:60].rearrange("p (c w) -> p c w", w=15),
                                                              in_=cv[:, :, TB:TB + 15]), reads=[cvk], writes=[t1k])
                    op('pool', lambda e, t1=t1: e.tensor_copy(out=cv[:, :, 0:15],
                                                              in_=t1[:, 0:60].rearrange("p (c w) -> p c w", w=15)),
                       reads=[t1k], writes=[cvk])
                work.append(_carry)
                CP(12)
                grps = [(8, 512, "q"), (12, 512, "k"), (16, 512, "v"), (24, 324, "i")]
                wpre = load_wide("o_in", grps[0][0], grps[0][1])
                for gi_, (cc0, ncols, what) in enumerate(grps):
                    w, wk = wpre
                    if gi_ + 1 < len(grps):
                        wpre = load_wide("o_in", grps[gi_ + 1][0], grps[gi_ + 1][1])
                    for t in range(nt):
                        r = rows_of(t)
                        if what != "v":
                            dma(q(), ropeC[:r, :], I["k_cos"][tab0 + t * 128:tab0 + t * 128 + r, :], writes=['ropeC'])
                            dma(q(), ropeS[:r, :], I["k_sin"][tab0 + t * 128:tab0 + t * 128 + r, :], writes=['ropeS'])
                        ps, pk = proj_tm(w, wk, ncols, t, r)
                        x_, xk_ = rp.next()
                        op('act', lambda e, r=r, x_=x_, ps=ps, ncols=ncols: e.copy(out=x_[:r, :ncols], in_=ps[:r, :ncols]),
                           reads=[pk], writes=[xk_])
                        orow = slice(op0_ + t * 128, op0_ + t * 128 + r)
                        if what == "v":
                            dma('act', O["dv" + sfx][orow, :], x_[:r, :], reads=[xk_])
                            v, vk = vb.next()
                            for pr in range(4):
                                op('dve' if pr % 2 else 'pool', lambda e, r=r, v=v, x_=x_, pr=pr: e.tensor_copy(
                                    out=v[:r, pr, :].rearrange("p (h d) -> p h d", d=65)[:, :, 0:64],
                                    in_=x_[:r, pr * 128:(pr + 1) * 128].rearrange("p (h d) -> p h d", d=64)),
                                   reads=[xk_], writes=[vk])
                            dma(q(), SC[f"VS1{sq}"][:, :r, pos0 // 128 + t, :].rearrange("a p d -> p a d"), v[:r, :, :],
                                reads=[vk], writes=[f"VS1{sq}"])
                            continue
                        y_, yk_ = rp.next()
                        if what == "q":
                            rope('dve', y_, yk_, x_, xk_, r, 512)
                            transposes_to(lambda j, ps, pk, t=t, r=r: (
                                op('act', lambda e: e.copy(out=QTb[sq][0:64, j, t * 128:t * 128 + r], in_=ps[0:64, :r]),
                                   reads=[pk], writes=['QT']),
                                op('act', lambda e: e.copy(out=QTb[sq][64:128, j, TB + t * 128:TB + t * 128 + r],
                                                           in_=ps[64:128, :r]), reads=[pk], writes=['QT'])), y_, yk_, r, 4, 128)
                        elif what == "k":
                            rope('dve', y_, yk_, x_, xk_, r, 512)
                            dma(q(), O["dk" + sfx][orow, :], y_[:r, :], reads=[yk_])
                            transposes_to(lambda j, ps, pk, t=t, r=r: op('dve', lambda e: e.tensor_copy(
                                out=KTn[:, j, t * 128:t * 128 + r], in_=ps[:, :r]), reads=[pk], writes=['KTn']), y_, yk_, r, 4, 128)
                        else:
                            rope('dve', y_, yk_, x_, xk_, r, 320)
                            dma(q(), O["di" + sfx][orow, :], y_[:r, 256:320], reads=[yk_])
                            transposes_to(lambda j, ps, pk, t=t, r=r: op('act', lambda e: e.copy(
                                out=IQT[:, j, t * 128:t * 128 + r], in_=ps[:64, :r]), reads=[pk], writes=['IQT']), y_, yk_, r, 4, 64)
                            ps2, pk2 = PS.next()
                            op('pe', lambda e, r=r, y_=y_, ps2=ps2: e.transpose(out=ps2[:64, :r], in_=y_[:r, 256:320],
                                                                                identity=ident[:r, :r]), reads=[yk_, 'ident'],
                               writes=[pk2])
                            op('dve', lambda e, t=t, r=r, ps2=ps2: e.tensor_copy(out=IKn[:, t * 128:t * 128 + r], in_=ps2[:64, :r]),
                               reads=[pk2], writes=['IKn'])
                            op('dve', lambda e, t=t, r=r, x_=x_: e.tensor_scalar(out=wq[:r, t, :], in0=x_[:r, 320:324], scalar1=0.5,
                                                                                 scalar2=None, op0=ALU.mult), reads=[xk_],
                               writes=['wq'])
                dma(q(), SC[f"KT1{sq}"][:, :, pos0:pos0 + TB].rearrange("a p n -> p a n"), KTn[:, :, :TB], reads=['KTn'],
                    writes=[f"KT1{sq}"])
                dma(q(), SC[f"IK{sq}"][:, pos0:pos0 + TB], IKn[:, :TB], reads=['IKn'], writes=[f"IK{sq}"])
                CP(13)
                nk = pos0 + TB
                nkb = (nk + 511) // 512
                for qt in range(nt):
                    r = rows_of(qt)
                    for ch in range((nk + KCH - 1) // KCH):
                        k0 = ch * KCH
                        n = min(KCH, nk - k0)
                        ib, ibk = kbuf.next()
                        dma(q(), ib[0:64, :n], SC[f"IK{sq}"][:, k0:k0 + n], reads=[f"IK{sq}"], writes=[ibk])
                        for kb_ in range((n + 511) // 512):
                            c0 = kb_ * 512
                            m = min(512, n - c0)
                            g0 = k0 + c0
                            diag = (sq == "P") and (g0 >= pos0)
                            pss = []
                            for hi in range(4):
                                ps, pk = PS.next()
                                op('pe', lambda e, hi=hi, r=r, qt=qt, ib=ib, c0=c0, m=m, ps=ps: e.matmul(
                                    ps[:r, :m], lhsT=IQT[:, hi, qt * 128:qt * 128 + r], rhs=ib[0:64, c0:c0 + m], start=True,
                                    stop=True), reads=['IQT', ibk], writes=[pk])
                                pss.append((ps, pk))
                            for hi in range(4):
                                ps, pk = pss[hi]
                                rl, rlk = tmr.next()
                                op('act', lambda e, r=r, m=m, rl=rl, ps=ps: e.activation(out=rl[:r, :m], in_=ps[:r, :m],
                                                                                         func=AF.Relu, scale=0.125),
                                   reads=[pk], writes=[rlk])
                                dst_ = scores[:r, g0:g0 + m]
                                if hi == 0 and diag:
                                    op('dve', lambda e, r=r, m=m, rl=rl, dst_=dst_, qt=qt: e.scalar_tensor_tensor(
                                        out=dst_, in0=rl[:r, :m], scalar=wq[:r, qt, 0:1], in1=admn[:r, qt, :m], op0=ALU.mult,
                                        op1=ALU.add), reads=[rlk, 'wq', 'admn', 'scores'], writes=['scores'])
                                elif hi == 0:
                                    op('dve', lambda e, r=r, m=m, rl=rl, dst_=dst_, qt=qt: e.tensor_scalar(
                                        out=dst_, in0=rl[:r, :m], scalar1=wq[:r, qt, 0:1], scalar2=None, op0=ALU.mult),
                                       reads=[rlk, 'wq', 'scores'], writes=['scores'])
                                else:
                                    op('dve', lambda e, r=r, m=m, rl=rl, dst_=dst_, qt=qt, hi=hi: e.scalar_tensor_tensor(
                                        out=dst_, in0=rl[:r, :m], scalar=wq[:r, qt, hi:hi + 1], in1=dst_, op0=ALU.mult,
                                        op1=ALU.add), reads=[rlk, 'wq', 'scores'], writes=['scores'])
                    hi_, lo_, w_, mid_, cnt_, ge_, mn_ = (bs[n_] for n_ in ["hi", "lo", "w", "mid", "cnt", "ge", "mn"])
                    B = lambda tl, c=0: tl[:r, c:c + 1]
                    op('dve', lambda e: e.tensor_reduce(out=B(hi_), in_=scores[:r, :nk], axis=AX.X, op=ALU.max),
                       reads=['scores'], writes=['bs_hi'])
                    if sq == "P":
                        nd = pos0
                        t1, t1k = tmr.next()
                        op('dve', lambda e, t1=t1: e.scalar_tensor_tensor(out=t1[:r, :TB], in0=admn[:r, qt, :TB], scalar=-2.0,
                                                                          in1=scores[:r, nd:nd + TB], op0=ALU.mult, op1=ALU.add),
                           reads=['scores', 'admn'], writes=[t1k])
                        op('dve', lambda e, t1=t1: e.tensor_reduce(out=B(lo_), in_=t1[:r, :TB], axis=AX.X, op=ALU.min),
                           reads=[t1k], writes=['bs_lo'])
                        if nd > 0:
                            op('dve', lambda e: e.tensor_reduce(out=B(mn_), in_=scores[:r, :nd], axis=AX.X, op=ALU.min),
                               reads=['scores'], writes=['bs_mn'])
                            op('dve', lambda e: e.tensor_tensor(out=B(lo_), in0=B(lo_), in1=B(mn_), op=ALU.min),
                               reads=['bs_lo', 'bs_mn'], writes=['bs_lo'])
                    else:
                        op('dve', lambda e: e.tensor_reduce(out=B(lo_), in_=scores[:r, :nk], axis=AX.X, op=ALU.min),
                           reads=['scores'], writes=['bs_lo'])
                    op('dve', lambda e: e.tensor_tensor(out=B(w_), in0=B(hi_), in1=B(lo_), op=ALU.subtract),
                       reads=['bs_hi', 'bs_lo'], writes=['bs_w'])
                    spl = nk if nk < 1024 else max(((nk * 9 // 20) // 512) * 512, nk - 4096, 512)
                    nA = nk - spl
                    npc = (spl + 2047) // 2048
                    sA_, nm_, wt_ = bs["sa"], bs["nm"], bs["wt"]
                    op('dve', lambda e: e.tensor_scalar(out=wt_[:r, 0:NIT], in0=pow2[:r, 0:NIT], scalar1=B(w_), scalar2=None,
                                                        op0=ALU.mult), reads=['bs_w', 'pow2'], writes=['bs_wt'])
                    for it in range(NIT):
                        wk = wt_[:r, it:it + 1]
                        op('dve', lambda e: e.tensor_tensor(out=B(mid_), in0=B(lo_), in1=wk, op=ALU.add),
                           reads=['bs_lo', 'bs_wt'], writes=['bs_mid'])
                        if nA > 0:
                            op('act', lambda e: e.activation(out=junkA[:r, :nA], in_=scores[:r, spl:nk], func=AF.Sign,
                                                             bias=B(mid_), scale=-1.0, accum_out=B(sA_)),
                               reads=['scores', 'bs_mid', 'wwide0'], writes=['wwide0', 'bs_sa'])
                        for pc in range(npc):
                            c0 = pc * 2048
                            m = min(2048, spl - c0)
                            op('dve', lambda e, pc=pc, c0=c0, m=m: e.tensor_scalar(
                                out=notsel[:r, qt, c0:c0 + m], in0=scores[:r, c0:c0 + m], scalar1=B(mid_), scalar2=None,
                                op0=ALU.is_ge, op1=ALU.add, accum_out=B(cnt_, pc)), reads=['scores', 'bs_mid', 'notsel'],
                               writes=['notsel', 'bs_cnt'])
                        if npc > 1:
                            op('dve', lambda e: e.tensor_reduce(out=B(nm_), in_=cnt_[:r, 0:npc], axis=AX.X, op=ALU.add),
                               reads=['bs_cnt'], writes=['bs_nm'])
                            cs = B(nm_)
                        else:
                            cs = B(cnt_)
                        if nA > 0:
                            op('dve', lambda e, cs=cs: e.tensor_scalar(out=B(nm_), in0=cs, scalar1=2.0, scalar2=-(511.0 - nA),
                                                                       op0=ALU.mult, op1=ALU.add), reads=['bs_cnt', 'bs_nm'],
                               writes=['bs_nm'])
                        if work:
                            work.pop(0)()
                        if nA > 0:
                            op('dve', lambda e: e.tensor_tensor(out=B(ge_), in0=B(sA_), in1=B(nm_), op=ALU.is_le),
                               reads=['bs_sa', 'bs_nm'], writes=['bs_ge'])
                        else:
                            op('dve', lambda e, cs=cs: e.tensor_scalar(out=B(ge_), in0=cs, scalar1=255.5, scalar2=None,
                                                                       op0=ALU.is_ge), reads=['bs_cnt', 'bs_nm'], writes=['bs_ge'])
                        op('dve', lambda e: e.scalar_tensor_tensor(out=B(lo_), in0=B(ge_), scalar=wk, in1=B(lo_), op0=ALU.mult,
                                                                   op1=ALU.add), reads=['bs_ge', 'bs_wt', 'bs_lo'], writes=['bs_lo'])
                    npc = (nk + 2047) // 2048
                    for pc in range(npc):
                        c0 = pc * 2048
                        m = min(2048, nk - c0)
                        op('dve' if pc % 2 == 0 else 'pool', lambda e, c0=c0, m=m: e.tensor_scalar(
                            out=notsel[:r, qt, c0:c0 + m], in0=scores[:r, c0:c0 + m], scalar1=B(lo_), scalar2=None,
                            op0=ALU.is_lt), reads=['scores', 'bs_lo'], writes=['notsel'])
                while work:
                    work.pop(0)()
                CP(14)
                attend(1, sq, TB, pos0, False, nt)
                CP(15)
                out_proj_ln("o_oa", "o_oo", R_OG, R_OB, TB)
                CP(16)
                for t in range(nt):
                    r = rows_of(t)
                    for half in range(2):
                        ps, pk = PS.next()
                        for j in range(4):
                            kc = half * 4 + j
                            op('pe', lambda e, kc=kc, j=j, t=t, r=r, ps=ps: e.transpose(
                                out=ps[:r, j * 128:(j + 1) * 128], in_=xT_f[:, kc, t * 128:t * 128 + r], identity=ident[:, :]),
                               reads=['xT_f', 'ident'], writes=[pk])
                        op('act' if half else 'dve',
                           (lambda e, t=t, r=r, ps=ps, half=half: e.copy(out=xtk(t, half * 512, (half + 1) * 512)[:r, :], in_=ps[:r, :]))
                           if half else
                           (lambda e, t=t, r=r, ps=ps, half=half: e.tensor_copy(out=xtk(t, half * 512, (half + 1) * 512)[:r, :],
                                                                                in_=ps[:r, :])),
                           reads=[pk], writes=['zT'])
                    dma(q(), O["y" + sfx][op0_ + t * 128:op0_ + t * 128 + r, :], xtk(t, 0, 1024)[:r, :], reads=['zT'])

            def stage_sample():
                npt = PAST // 128
                t1, t1k = tmr.next()
                dma(q(), t1[:30, :], I["c_conv"], writes=[t1k])
                for c in range(4):
                    ps, pk = PS.next()
                    op('pe', lambda e, c=c, ps=ps, t1=t1: e.transpose(out=ps[:, 0:30], in_=t1[:30, c * 128:(c + 1) * 128],
                                                                      identity=ident[:30, :30]), reads=[t1k, 'ident'], writes=[pk])
                    op('act', lambda e, c=c, ps=ps: e.copy(out=a_bf["S"][:, c, 0:30], in_=ps[:, 0:30]), reads=[pk],
                       writes=['a_bfS'])
                t2, t2k = tmr.next()
                dma(q(), t2[:15, :], I["c_pool"], writes=[t2k])
                for c in range(4):
                    ps, pk = PS.next()
                    op('pe', lambda e, c=c, ps=ps, t2=t2: e.transpose(out=ps[:, 0:15], in_=t2[:15, c * 128:(c + 1) * 128],
                                                                      identity=ident[:15, :15]), reads=[t2k, 'ident'], writes=[pk])
                    op('act', lambda e, c=c, ps=ps: e.copy(out=cv_f["S"][:, c, 0:15], in_=ps[:, 0:15]), reads=[pk],
                       writes=['cv_fS'])
                dma(q(), lf[:], I["c_ff"].rearrange("(t p) h -> p t h", p=128), writes=['lf_s'])
                for g4 in range(4):
                    ps, pk = PS.next()
                    for j in range(4):
                        t = g4 * 4 + j
                        op('pe', lambda e, t=t, j=j, ps=ps: e.transpose(out=ps[:8, j * 128:(j + 1) * 128], in_=lf[:, t, :],
                                                                        identity=ident[:, :]), reads=['lf_s', 'ident'], writes=[pk])
                    op('act', lambda e, ps=ps: e.activation(out=Lrow[:8, :], in_=ps[:8, :], func=AF.Copy, scale=-1.0), reads=[pk],
                       writes=['Lrow'])
                    op('dve', lambda e: e.tensor_tensor_scan(out=CKrow[:8, :], data0=onesrow[:8, :], data1=Lrow[:8, :],
                                                             initial=ckcar["S"][:8, 0:1], op0=ALU.mult, op1=ALU.add),
                       reads=['Lrow', 'onesrow', 'ckcarS'], writes=['CKrow'])
                    op('dve', lambda e: e.tensor_copy(out=ckcar["S"][:8, 0:1], in_=CKrow[:8, 511:512]), reads=['CKrow'],
                       writes=['ckcarS'])
                    for j in range(4):
                        t = g4 * 4 + j
                        ps2, pk2 = PS.next()
                        op('pe', lambda e, j=j, ps2=ps2: e.transpose(out=ps2[:, 0:8], in_=CKrow[:8, j * 128:(j + 1) * 128],
                                                                     identity=ident[:8, :8]), reads=['CKrow', 'ident'], writes=[pk2])
                        op('dve', lambda e, t=t, ps2=ps2: e.tensor_copy(out=CKT["S"][:, t, :], in_=ps2[:, 0:8]), reads=[pk2],
                           writes=['CKTS'])
                for (src, l) in [("c_fk", 0), ("c_dk", 1)]:
                    for t in range(npt):
                        x_, xk_ = rp.next()
                        dma(q(), x_[:, :], I[src][t * 128:(t + 1) * 128, :], writes=[xk_])
                        ps, pk = PS.next()
                        for j in range(4):
                            op('pe', lambda e, j=j, ps=ps, x_=x_: e.transpose(out=ps[:, j * 128:(j + 1) * 128],
                                                                              in_=x_[:, j * 128:(j + 1) * 128], identity=ident[:, :]),
                               reads=[xk_, 'ident'], writes=[pk])
                        kb, kbk = tmb.next()
                        op('act' if t % 2 else 'dve',
                           (lambda e, kb=kb, ps=ps: e.copy(out=kb[:, :], in_=ps[:, :])) if t % 2 else
                           (lambda e, kb=kb, ps=ps: e.tensor_copy(out=kb[:, :], in_=ps[:, :])), reads=[pk], writes=[kbk])
                        dma(q(), SC[f"KT{l}S"][:, :, t * 128:(t + 1) * 128].rearrange("a p n -> p a n"),
                            kb[:, :].rearrange("p (a n) -> p a n", n=128), reads=[kbk], writes=[f"KT{l}S"])
                for (src, l) in [("c_fv", 0), ("c_dv", 1)]:
                    for t in range(npt):
                        x_, xk_ = rp.next()
                        dma(q(), x_[:, :], I[src][t * 128:(t + 1) * 128, :], writes=[xk_])
                        v, vk = vb.next()
                        for pr in range(4):
                            op('dve' if pr % 2 else 'pool', lambda e, v=v, x_=x_, pr=pr: e.tensor_copy(
                                out=v[:, pr, :].rearrange("p (h d) -> p h d", d=65)[:, :, 0:64],
                                in_=x_[:, pr * 128:(pr + 1) * 128].rearrange("p (h d) -> p h d", d=64)), reads=[xk_], writes=[vk])
                        dma(q(), SC[f"VS{l}S"][:, :, t, :].rearrange("a p d -> p a d"), v[:, :, :], reads=[vk], writes=[f"VS{l}S"])
                for t in range(npt):
                    x_, xk_ = rp.next()
                    dma(q(), x_[:, 0:64], I["c_di"][t * 128:(t + 1) * 128, :], writes=[xk_])
                    ps, pk = PS.next()
                    op('pe', lambda e, ps=ps, x_=x_: e.transpose(out=ps[:64, 0:128], in_=x_[:, 0:64], identity=ident[:, :]),
                       reads=[xk_, 'ident'], writes=[pk])
                    kb, kbk = tmb.next()
                    op('act', lambda e, kb=kb, ps=ps: e.copy(out=kb[:64, 0:128], in_=ps[:64, 0:128]), reads=[pk], writes=[kbk])
                    dma(q(), SC["IKS"][:, t * 128:(t + 1) * 128], kb[:64, 0:128], reads=[kbk], writes=["IKS"])

            def x_load(x_ap, TB):
                for t in range((TB + 127) // 128):
                    r = min(128, TB - t * 128)
                    dma(q(), xtokb[:r, t, :], x_ap[t * 128:t * 128 + r, :], writes=['xtokb'])

            try:
              x_load(I["xp"][0:TBP, :], TBP)
              for blk in range(nblk):
                fin = (blk == nblk - 1)
                layer0("P", I["xp"][blk * TBP:(blk + 1) * TBP, :], TBP, blk * TBP, None, fin)
                CP(10)
                if blk + 1 < nblk:
                    x_load(I["xp"][(blk + 1) * TBP:(blk + 2) * TBP, :], TBP)
                elif do_sample:
                    x_load(I["xs"], TS)
                layer1("P", TBP, blk * TBP, fin)
              if do_sample:
                stage_sample()
                CP(20)
                layer0("S", I["xs"], TS, PAST, None, True)
                CP(21)
                layer1("S", TS, PAST, True)
            except _Stop:
                pass
        try:
            body()
        except _Stop:
            pass
        S.finish()
        build.ninst = S.ninst
    return nc


def _consts(SEQ):
    kk = np.arange(128)[:, None]
    qq = np.arange(TBP)[None, :]
    cmask = np.stack([((128 * a + kk) > qq).astype(np.float32) for a in range(NTP)])
    q2 = np.arange(128)[:, None]
    k2 = np.arange(TBP)[None, :]
    adm = np.stack([np.where((k2 // 64) > ((128 * a + q2) // 64), -1e30, 0.0).astype(np.float32) for a in range(NTP)])
    pos = np.concatenate([np.arange(SEQ), PAST + np.arange(TS)]).astype(np.float32)
    inv = (10000.0 ** (-np.arange(0, 64, 2, dtype=np.float32) / 64)).astype(np.float32)
    ang = pos[:, None] * inv[None, :]
    cos = np.tile(np.cos(ang).astype(np.float32), (1, 8))
    sin = np.tile(np.sin(ang).astype(np.float32), (1, 8))
    fix = np.zeros((128, 4, 16), np.float32)
    for g, w in enumerate((2, 4, 8, 16)):
        fix[:, g, :] = 1.0 / np.minimum(np.arange(16) + 1, w)
    return {"k_cmask": cmask, "k_adm": adm, "k_cos": cos, "k_sin": sin, "k_poolfix": fix}


_NC_CACHE = {}


def kernel(x_prompt, x_sample, cache_conv, cache_fox_k, cache_fox_v, cache_fox_logf, cache_pool, cache_dsa_k,
           cache_dsa_v, cache_dsa_idx_k, e_w_in, e_b_f, e_conv_w, e_conv_b, e_conv_ln_g, e_conv_ln_b, e_w_out, e_ln_g,
           e_ln_b, o_w_in, o_pool_w, o_pool_scale, o_w_out, o_ln_g, o_ln_b, _nblk=SEQ // TBP, _sample=True):
    f = lambda a: np.ascontiguousarray(np.asarray(a, dtype=np.float32))
    if (_nblk, _sample) not in _NC_CACHE:
        _NC_CACHE[(_nblk, _sample)] = build(_nblk, _sample)
    nc = _NC_CACHE[(_nblk, _sample)]
    pvec = np.zeros((40, 1024), np.float32)
    pvec[0:31, 0:512] = f(e_conv_w)[0]
    pvec[31, 0:512] = f(e_conv_b)[0]
    pvec[32, 0:512] = f(e_conv_ln_g)[0]
    pvec[33, 0:512] = f(e_conv_ln_b)[0]
    pvec[34] = f(e_ln_g)[0]
    pvec[35] = f(e_ln_b)[0]
    pvec[36, 0:512] = f(o_pool_scale)[0]
    pvec[37] = f(o_ln_g)[0]
    pvec[38] = f(o_ln_b)[0]
    SEQE = _nblk * TBP
    cst = _consts(SEQE)
    shared = {"e_w_in": f(e_w_in)[0], "e_b_f": f(e_b_f)[0].reshape(8, 1), "e_w_out": f(e_w_out)[0],
              "o_w_in": f(o_w_in)[0], "o_pool_w": f(o_pool_w)[0], "o_w_out": f(o_w_out)[0], "pvec": pvec}
    shared.update(cst)
    in_maps = []
    for c in range(8):
        m = dict(shared)
        m["xp"] = f(x_prompt)[c // 4][:SEQE]
        m["xs"] = f(x_sample)[c]
        m["c_conv"] = f(cache_conv)[0, c]
        m["c_fk"] = f(cache_fox_k)[0, c].reshape(PAST, 512)
        m["c_fv"] = f(cache_fox_v)[0, c].reshape(PAST, 512)
        m["c_ff"] = f(cache_fox_logf)[0, c]
        m["c_pool"] = f(cache_pool)[0, c]
        m["c_dk"] = f(cache_dsa_k)[0, c].reshape(PAST, 512)
        m["c_dv"] = f(cache_dsa_v)[0, c].reshape(PAST, 512)
        m["c_di"] = f(cache_dsa_idx_k)[0, c]
        in_maps.append(m)
    res = run_bass_kernel_spmd(nc, in_maps, core_ids=list(range(8)))
    R = res.results
    P = lambda n: np.stack([R[0][n], R[4][n]])
    Sm = lambda n: np.stack([R[c][n] for c in range(8)])
    out = (P("y_p"), Sm("y_s"),
           P("conv_p")[None], Sm("conv_s")[None],
           P("fk_p").reshape(1, 2, SEQE, 8, 64), Sm("fk_s").reshape(1, 8, TS, 8, 64),
           P("fv_p").reshape(1, 2, SEQE, 8, 64), Sm("fv_s").reshape(1, 8, TS, 8, 64),
           P("ff_p")[None], Sm("ff_s")[None],
           P("pool_p")[None], Sm("pool_s")[None],
           P("dk_p").reshape(1, 2, SEQE, 8, 64), Sm("dk_s").reshape(1, 8, TS, 8, 64),
           P("dv_p").reshape(1, 2, SEQE, 8, 64), Sm("dv_s").reshape(1, 8, TS, 8, 64),
           P("di_p")[None], Sm("di_s")[None])
    return tuple(np.ascontiguousarray(o, dtype=np.float32) for o in out)
```
